# Optimizing a Trainium2 kernel written in Bass

```python
import math
import jax, jax.numpy as jnp
from jax import lax
import numpy as np

D_MODEL = 1024
BATCH = 2
SEQ = 8192
DEPTH = 2

D_INNER = D_MODEL
HEAD_DIM = 64
N_HEADS = D_INNER // HEAD_DIM
N_MIXERS = 2
GRID_W = 64
NA_ROWS = 8
NA_COLS = 16
NA_QBLOCK = 16
NA_KBLOCK = NA_QBLOCK + NA_COLS
DIL_PAIRS = ((128, 1), (512, 4), (2048, 16))
N_DIL_GROUPS = len(DIL_PAIRS)
RMS_EPS = 1e-6
NEG_INF = -1e30

kernel_name = "hybrid_natten_dilated_encoder"


def rmsnorm(x, g):
    xf = x.astype(jnp.float32)
    y = xf * lax.rsqrt(jnp.mean(xf * xf, axis=-1, keepdims=True) + RMS_EPS)
    return (y * g.astype(jnp.float32)).astype(x.dtype)


def alibi_slopes(n_heads):
    return jnp.asarray(2.0 ** (-8.0 * (np.arange(n_heads) + 1) / n_heads), dtype=jnp.float32)


def neighbourhood_attention(q, k, v, rpb):
    B, S, H, hd = q.shape
    rows = S // GRID_W
    kh = min(NA_ROWS, rows)
    kw = NA_COLS
    nb = GRID_W // NA_QBLOCK
    scale = 1.0 / math.sqrt(hd)
    qg = q.reshape(B, rows, GRID_W, H, hd)
    kg = k.reshape(B, rows, GRID_W, H, hd)
    vg = v.reshape(B, rows, GRID_W, H, hd)

    blk = np.arange(nb)
    kb_start = np.clip(blk * NA_QBLOCK - kw // 2, 0, GRID_W - NA_KBLOCK)
    col_idx = kb_start[:, None] + np.arange(NA_KBLOCK)[None, :]
    q_col = blk[:, None] * NA_QBLOCK + np.arange(NA_QBLOCK)[None, :]
    q_cstart = np.clip(q_col - kw // 2, 0, GRID_W - kw)
    kcol = col_idx[:, None, :]
    col_valid = (kcol >= q_cstart[:, :, None]) & (kcol < q_cstart[:, :, None] + kw)
    col_off = np.clip(kcol - q_col[:, :, None] + NA_COLS - 1, 0, 2 * NA_COLS - 2)
    col_bias = rpb.astype(jnp.float32)[:, :, col_off]
    col_valid = jnp.asarray(col_valid)[:, :, None, :]

    def row_fn(r):
        rs = jnp.clip(r - kh // 2, 0, rows - kh)
        qr = lax.dynamic_index_in_dim(qg, r, axis=1, keepdims=False).reshape(B, nb, NA_QBLOCK, H, hd)
        kr = lax.dynamic_slice_in_dim(kg, rs, kh, axis=1)[:, :, col_idx]
        vr = lax.dynamic_slice_in_dim(vg, rs, kh, axis=1)[:, :, col_idx]
        s = jnp.einsum('bnqhd,bknjhd->bhnqkj', qr, kr).astype(jnp.float32) * scale
        row_off = rs + jnp.arange(kh) - r + NA_ROWS - 1
        bias = jnp.take(col_bias, row_off, axis=1).transpose(0, 2, 3, 1, 4)
        s = jnp.where(col_valid[None, None], s + bias[None], NEG_INF)
        p = jax.nn.softmax(s.reshape(B, H, nb, NA_QBLOCK, kh * NA_KBLOCK), axis=-1)
        p = p.reshape(B, H, nb, NA_QBLOCK, kh, NA_KBLOCK).astype(v.dtype)
        o = jnp.einsum('bhnqkj,bknjhd->bnqhd', p, vr)
        return o.reshape(B, GRID_W, H, hd)

    out = lax.map(row_fn, jnp.arange(rows))
    return out.transpose(1, 0, 2, 3, 4).reshape(B, S, H, hd)


def dilated_attention(q, k, v, dil, radius, slopes):
    B, S, H, hd = q.shape
    L = S // dil
    C = radius
    nc = -(-L // C)
    lp = nc * C
    scale = 1.0 / math.sqrt(hd)
    qs = jnp.pad(q.reshape(B, L, dil, H, hd), ((0, 0), (0, lp - L), (0, 0), (0, 0), (0, 0)))
    qc = qs.reshape(B, nc, C, dil, H, hd)

    def band(a):
        a = jnp.pad(a.reshape(B, L, dil, H, hd), ((0, 0), (C, lp - L + C), (0, 0), (0, 0), (0, 0)))
        a = a.reshape(B, nc + 2, C, dil, H, hd)
        return jnp.concatenate([a[:, :-2], a[:, 1:-1], a[:, 2:]], axis=2)

    kc, vc = band(k), band(v)
    s = jnp.einsum('bncrhd,bnjrhd->bhrncj', qc, kc).astype(jnp.float32) * scale
    q_i = np.arange(nc)[:, None] * C + np.arange(C)[None, :]
    k_i = (np.arange(nc)[:, None] - 1) * C + np.arange(3 * C)[None, :]
    delta = k_i[:, None, :] - q_i[:, :, None]
    valid = jnp.asarray((np.abs(delta) <= radius) & (k_i[:, None, :] >= 0) & (k_i[:, None, :] < L))
    dist = jnp.asarray(np.abs(delta) * dil, dtype=jnp.float32)
    bias = -slopes[:, None, None, None] * dist[None]
    s = jnp.where(valid[None, None, None], s + bias[None, :, None], NEG_INF)
    lse = jax.nn.logsumexp(s, axis=-1)
    p = jnp.exp(s - lse[..., None]).astype(v.dtype)
    o = jnp.einsum('bhrncj,bnjrhd->bncrhd', p, vc)
    o = o.reshape(B, lp, dil, H, hd)[:, :L].reshape(B, S, H, hd)
    lse = lse.transpose(0, 3, 4, 2, 1).reshape(B, lp, dil, H)[:, :L].reshape(B, S, H)
    return o, lse


def neighbourhood_mixer(h, w_in, rpb):
    B, S, _ = h.shape
    proj = h @ w_in
    q, k, v, gate = jnp.split(proj, 4, axis=-1)
    heads = lambda a: a.reshape(B, S, N_HEADS, HEAD_DIM)
    o = neighbourhood_attention(heads(q), heads(k), heads(v), rpb)
    return o.reshape(B, S, D_INNER), gate


def dilated_mixer(h, w_in):
    B, S, _ = h.shape
    proj = h @ w_in
    qkv = proj[..., :3 * N_DIL_GROUPS * D_INNER].reshape(B, S, N_DIL_GROUPS, 3, N_HEADS, HEAD_DIM)
    gate = proj[..., 3 * N_DIL_GROUPS * D_INNER:]
    slopes = alibi_slopes(N_HEADS)
    outs, lses = [], []
    for g, (window, dil) in enumerate(DIL_PAIRS):
        o_g, lse_g = dilated_attention(qkv[:, :, g, 0], qkv[:, :, g, 1], qkv[:, :, g, 2],
                                       dil, window // (2 * dil), slopes)
        outs.append(o_g)
        lses.append(lse_g)
    wts = jax.nn.softmax(jnp.stack(lses, axis=0), axis=0)
    o = jnp.einsum('gbsh,gbshd->bshd', wts, jnp.stack(outs, axis=0).astype(jnp.float32))
    return o.astype(h.dtype).reshape(B, S, D_INNER), gate


def setup_inputs(seed: int = 0) -> dict:
    key = jax.random.key(seed)
    ks = jax.random.split(key, 10)
    f32 = jnp.float32
    n_in_a = 4 * D_INNER
    n_in_b = (3 * N_DIL_GROUPS + 1) * D_INNER
    return {
        "x": jax.random.normal(ks[0], (BATCH, SEQ, D_MODEL), f32),
        "norm_0": 1.0 + 0.02 * jax.random.normal(ks[1], (D_MODEL,), f32),
        "w_in_0": jax.random.normal(ks[2], (D_MODEL, n_in_a), f32) * D_MODEL ** -0.5,
        "rpb_0": 0.02 * jax.random.normal(ks[3], (N_HEADS, 2 * NA_ROWS - 1, 2 * NA_COLS - 1), f32),
        "w_out_0": jax.random.normal(ks[4], (D_INNER, D_MODEL), f32) * D_INNER ** -0.5,
        "norm_1": 1.0 + 0.02 * jax.random.normal(ks[5], (D_MODEL,), f32),
        "w_in_1": jax.random.normal(ks[6], (D_MODEL, n_in_b), f32) * D_MODEL ** -0.5,
        "w_out_1": jax.random.normal(ks[7], (D_INNER, D_MODEL), f32) * D_INNER ** -0.5,
        "norm_f": 1.0 + 0.02 * jax.random.normal(ks[8], (D_MODEL,), f32),
    }


def reference(x, norm_0, w_in_0, rpb_0, w_out_0, norm_1, w_in_1, w_out_1, norm_f):
    layers = ((norm_0, w_in_0, rpb_0, w_out_0), (norm_1, w_in_1, None, w_out_1))
    for i in range(DEPTH):
        g, w_in, rpb, w_out = layers[i]
        h = rmsnorm(x, g)
        if i % N_MIXERS == 0:
            o, gate = neighbourhood_mixer(h, w_in, rpb)
        else:
            o, gate = dilated_mixer(h, w_in)
        x = x + (o * jax.nn.silu(gate)) @ w_out
    return rmsnorm(x, norm_f)
```

```python
import numpy as np
from contextlib import ExitStack
import concourse.bass as bass
import concourse.mybir as mybir
from concourse.bass_utils import run_bass_kernel_spmd

F32 = mybir.dt.float32
BF16 = mybir.dt.bfloat16
AF = mybir.ActivationFunctionType
ALU = mybir.AluOpType


class Op:
    __slots__ = ("eng", "fn", "deps", "is_dma", "slot", "ord", "marked", "name")

    def __init__(self, eng, fn, is_dma=False, slot=None, name=""):
        self.eng = eng
        self.fn = fn
        self.deps = []
        self.is_dma = is_dma
        self.slot = slot
        self.ord = None
        self.marked = False
        self.name = name


class Sched:
    ENGS = ("pe", "act", "dve", "pool", "sp")

    def __init__(self, nc, es):
        self.nc = nc
        self.es = es
        self.q = {e: [] for e in self.ENGS}
        self.last_w = {}
        self.readers = {}
        self.finals = []
        self.slot_count = {}
        self.n_sb = 0

    def sbuf(self, name, shape, dtype):
        return self.es.enter_context(self.nc.sbuf_tensor(name, shape, dtype))

    def psum(self, name, shape, dtype):
        return self.es.enter_context(self.nc.psum_tensor(name, shape, dtype))

    def _add(self, op, reads, writes, deps):
        dl = []
        for r in reads:
            dl.extend(self.last_w.get(r, ()))
            self.readers.setdefault(r, []).append(op)
        for w_ in writes:
            prev = self.last_w.get(w_, [])
            dl.extend(prev)
            for rd in self.readers.get(w_, ()):
                if rd is not op:
                    dl.append(rd)
            if op.is_dma and prev and all(q.is_dma for q in prev):
                self.last_w[w_] = prev + [op]
            else:
                self.last_w[w_] = [op]
            self.readers[w_] = []
        dl.extend(deps)
        seen = set()
        for d in dl:
            if d is op or id(d) in seen:
                continue
            seen.add(id(d))
            if d.eng == "pe" and op.eng == "pe" and not d.is_dma and not op.is_dma:
                continue
            op.deps.append(d)
        self.q[op.eng].append(op)
        return op

    def op(self, eng, fn, reads=(), writes=(), deps=(), name=""):
        return self._add(Op(eng, fn, name=name), reads, writes, deps)

    def dma(self, eng, fn, reads=(), writes=(), deps=(), slot=None, name=""):
        if slot is None:
            slot = ("auto", len(self.slot_count))
        op = Op(eng, fn, is_dma=True, slot=slot, name=name)
        k = self.slot_count.get(slot, 0) + 1
        self.slot_count[slot] = k
        op.ord = k
        return self._add(op, reads, writes, deps)

    def final_wait(self, eng, ops):
        self.finals.append((eng, list(ops)))

    def emit(self):
        nc = self.nc
        for e in self.ENGS:
            for op in self.q[e]:
                for d in op.deps:
                    d.marked = True
        for _, ops in self.finals:
            for d in ops:
                d.marked = True
        for e in self.ENGS:
            for op in self.q[e]:
                if op.is_dma:
                    op.marked = True
        for e in self.ENGS:
            k = 0
            for op in self.q[e]:
                if not op.is_dma and op.marked:
                    k += 1
                    op.ord = k
        sems = {}

        def sem_of(op):
            key = ("dma", op.slot) if op.is_dma else ("eng", op.eng)
            if key not in sems:
                sems[key] = self.es.enter_context(nc.semaphore("s%d" % len(sems)))
            return sems[key]

        def val_of(op):
            return op.ord * 16 if op.is_dma else op.ord

        for e in self.ENGS:
            for op in self.q[e]:
                if op.marked:
                    sem_of(op)
        finals = {}
        for eng, ops in self.finals:
            finals.setdefault(eng, []).extend(ops)
        self.n_waits = 0

        def run(engname, engine):
            known = {}

            def need(d):
                s = sem_of(d)
                v = val_of(d)
                if known.get(id(s), 0) >= v:
                    return
                known[id(s)] = v
                engine.wait_ge(s, v)
                self.n_waits += 1

            for op in self.q[engname]:
                best = {}
                for d in op.deps:
                    s = sem_of(d)
                    if id(s) not in best or val_of(best[id(s)]) < val_of(d):
                        best[id(s)] = d
                for d in best.values():
                    need(d)
                ins = op.fn(engine)
                if op.marked:
                    ins.then_inc(sem_of(op), 16 if op.is_dma else 1)
            for d in finals.get(engname, ()):
                need(d)

        with nc.Block() as block:
            @block.sync
            def _(eng):
                run("sp", eng)

            @block.tensor
            def _(eng):
                run("pe", eng)

            @block.scalar
            def _(eng):
                run("act", eng)

            @block.vector
            def _(eng):
                run("dve", eng)

            @block.gpsimd
            def _(eng):
                run("pool", eng)


D_MODEL = 1024
SEQ = 8192
NCORE = 8
OWN = 2048
NCH = 8
NPAIR = 8
HALO0 = 256
W0 = OWN + 2 * HALO0
HALO1 = 1024
W1 = OWN + 2 * HALO1
DILS = (1, 4, 16)
NEG = -30000.0
EPS = 1e-6
TAB_OFFS = {"G": (0, (-2, -1, 0, 1, 2)), 0: (5, (-2, -1, 0, 1, 2, 3)), 1: (11, (-2, -1, 0, 1, 2)),
            14: (16, (-2, -1, 0, 1, 2)), 15: (21, (-3, -2, -1, 0, 1, 2))}
NTABBLK = 27
U32 = mybir.dt.uint32
NHIDX = 64
MASK_ENG = "pool"
FIN_ENG = "pool"
CSTW = 128 + 5 * 128 + 4


def tab_for(lt):
    return TAB_OFFS[lt] if lt in TAB_OFFS else TAB_OFFS["G"]


class Ctx:
    pass


def common_setup(nc, es, hcols=W0):
    C = Ctx()
    S = Sched(nc, es)
    C.S = S
    C.nc = nc
    C.G = S.psum("G", [128, 2, 512], F32)
    C.Sp = S.psum("Sp", [128, 4, 512], F32)
    C.O = S.psum("O", [128, 2, 512], F32)
    C.gi = 0
    C.xT = S.sbuf("xT", [128, NCH, OWN], F32)
    C.hT = S.sbuf("hT", [128, NCH, hcols], BF16)
    C.QT = S.sbuf("QT", [128, 2, OWN], BF16)
    C.KV = S.sbuf("KV", [128, 2 * W1], BF16)
    C.KT = C.KV[:, 0:W1]
    C.VT = C.KV[:, W1:2 * W1]
    C.V = S.sbuf("V", [128, 32, 128], BF16)
    C.GT = S.sbuf("GT", [128, OWN], BF16)
    C.uT = S.sbuf("uT", [128, OWN], BF16)
    C.Wb = S.sbuf("Wb", [128, 8, 1024], BF16)
    C.Wo = S.sbuf("Wo", [128, 2, 1024], BF16)
    C.E = S.sbuf("E", [128, 4, 512], BF16)
    C.P = S.sbuf("P", [128, 4, 512], BF16)
    C.sq = S.sbuf("sq", [128, 2, 512], BF16)
    C.rstd = S.sbuf("rstd", [128, 512], F32)
    C.tmp = S.sbuf("tmp", [128, 4, 128], F32)
    C.onesf = S.sbuf("onesf", [128, 128], BF16)
    C.vm = S.sbuf("vm", [128, 5, 128], BF16)
    C.vcol = S.sbuf("vcol", [128, 4], F32)
    C.gam = S.sbuf("gam", [128, 3, NCH], F32)
    C.wslot = 0
    C.pend = []
    C.nrm = 0
    C.epsb = S.sbuf("epsb", [128, 1], F32)
    C.ident = S.sbuf("ident", [128, 128], BF16)
    S.op("dve", lambda e: e.memset(C.epsb[:], EPS), writes=["epsb"])
    S.op("dve", lambda e: e.memset(C.onesf[:], 1.0), writes=["onesf"])
    S.op("pool", lambda e: e.memset(C.QT[:], 0.0), writes=["QT"])
    return C


def next_g(C):
    gi = C.gi
    C.gi = (gi + 1) % 2
    return gi


def load_wblock(C, src_ap):
    S = C.S
    s = C.wslot
    C.wslot = (s + 1) % 8
    S.dma("pool", lambda e: e.dma_start(out=C.Wb[:, s, :], in_=src_ap), writes=[("W", s)], slot=("W", s))
    return s


def proj_block(C, ws, h_of, n, evac):
    S = C.S
    gi = next_g(C)
    gk = ("G", gi)
    for c in range(NCH):
        rhs, rk = h_of(c)
        S.op("pe", lambda e, c=c, rhs=rhs: e.matmul(C.G[:, gi, 0:n], lhsT=C.Wb[:, ws, c * 128:(c + 1) * 128],
                                                    rhs=rhs, start=(c == 0), stop=(c == NCH - 1)),
             reads=[("W", ws), rk], writes=[gk])
    evac(C.G[:, gi, 0:n], gk)


def emit_norm(C, x_of, ntok, gidx, out_of):
    t0 = 0
    while t0 < ntok:
        n = min(512, ntok - t0)
        _norm_block(C, x_of, t0, n, gidx, out_of)
        t0 += n


def rstd_buf(C):
    k = C.nrm % 2
    C.nrm += 1
    if k == 0:
        return C.rstd[:, :], [("rstd", 0)]
    return C.tmp[:, :, :].rearrange("p a c -> p (a c)"), [("rstd", 1)] + [("tmp", i) for i in range(4)]


def _norm_block(C, x_of, t0, n, gidx, out_of):
    S = C.S
    gi = next_g(C)
    gk = ("G", gi)
    rs, rkeys = rstd_buf(C)
    for c in range(NCH):
        xa, xk = x_of(c, t0, n)
        sb = c % 2
        S.op("act", lambda e, xa=xa, sb=sb: e.activation(out=C.sq[:, sb, 0:n], in_=xa, func=AF.Square),
             reads=[xk], writes=[("sq", sb)])
        S.op("pe", lambda e, c=c, sb=sb: e.matmul(C.G[:, gi, 0:n], lhsT=C.onesf[:, :], rhs=C.sq[:, sb, 0:n],
                                                  start=(c == 0), stop=(c == NCH - 1)),
             reads=["onesf", ("sq", sb)], writes=[gk])
    S.op("act", lambda e: e.activation(out=rs[:, 0:n], in_=C.G[:, gi, 0:n], func=AF.Ln,
                                       scale=1.0 / D_MODEL, bias=C.epsb[:, 0:1]),
         reads=[gk, "epsb"], writes=rkeys)
    S.op("act", lambda e: e.activation(out=rs[:, 0:n], in_=rs[:, 0:n], func=AF.Exp, scale=-0.5), reads=rkeys, writes=rkeys)
    for c in range(NCH):
        xa, xk = x_of(c, t0, n)
        oa, ok = out_of(c, t0, n)
        if False:
            S.op("pool", lambda e, xa=xa, oa=oa, c=c: e.scalar_tensor_tensor(
                out=oa, in0=xa, scalar=C.gam[:, gidx, c:c + 1], in1=rs[:, 0:n], op0=ALU.mult, op1=ALU.mult),
                 reads=[xk, "gam"] + rkeys, writes=[ok])
        else:
            S.op("dve", lambda e, xa=xa, oa=oa, c=c: e.scalar_tensor_tensor(
                out=oa, in0=xa, scalar=C.gam[:, gidx, c:c + 1], in1=rs[:, 0:n], op0=ALU.mult, op1=ALU.mult),
                 reads=[xk, "gam"] + rkeys, writes=[ok])


def emit_outproj(C, pair, wo_src, tbs, ms=None):
    S = C.S
    for tb in tbs:
        for m in (range(NCH) if ms is None else ms):
            gi = next_g(C)
            gk = ("G", gi)
            S.op("pe", lambda e, m=m, gi=gi, tb=tb: e.matmul(C.G[:, gi, :], lhsT=C.Wo[:, pair % 2, m * 128:(m + 1) * 128],
                                                          rhs=C.uT[:, tb * 512:(tb + 1) * 512], start=True, stop=True),
                 reads=[("Wo", pair % 2), ("uT", tb)], writes=[gk])
            S.op("dve", lambda e, m=m, gi=gi, tb=tb: e.tensor_tensor(out=C.xT[:, m, tb * 512:(tb + 1) * 512],
                                                                 in0=C.G[:, gi, :], in1=C.xT[:, m, tb * 512:(tb + 1) * 512],
                                                                 op=ALU.add),
                 reads=[gk, ("x", m, tb)], writes=[("x", m, tb)])


def pend_outproj(C, pair, tb):
    for m in range(NCH):
        C.pend.append((pair, tb, m))


def pop_outproj(C, n=1):
    for _ in range(n):
        if C.pend:
            pair, tb, m = C.pend.pop(0)
            emit_outproj(C, pair, None, [tb], [m])


def flush_outproj(C):
    pop_outproj(C, len(C.pend))


def evac_copy_act(C, dst, dkey):
    def f(ps, gk):
        C.S.op("act", lambda e: e.activation(out=dst, in_=ps, func=AF.Copy), reads=[gk], writes=[dkey])
    return f


def evac_copy_dve(C, dst, dkey):
    def f(ps, gk):
        C.S.op("dve", lambda e: e.tensor_copy(out=dst, in_=ps), reads=[gk], writes=[dkey])
    return f


def evac_q(C, c0, n):
    def f(ps, gk):
        C.S.op("act", lambda e: e.activation(out=C.QT[0:64, 0, c0:c0 + n], in_=ps[0:64, :], func=AF.Copy), reads=[gk], writes=["QT"])
        C.S.op("act", lambda e: e.activation(out=C.QT[64:128, 1, c0:c0 + n], in_=ps[64:128, :], func=AF.Copy), reads=[gk], writes=["QT"])
    return f


def evac_scaled_dve(C, dst, dkey, flag_ap):
    def f(ps, gk):
        C.S.op("dve", lambda e: e.tensor_scalar(out=dst, in0=ps, scalar1=flag_ap, scalar2=None, op0=ALU.mult), reads=[gk, "vcol"], writes=[dkey])
    return f


def evac_silu(C, dst, dkey):
    def f(ps, gk):
        C.S.op("act", lambda e: e.activation(out=dst, in_=ps, func=AF.Silu), reads=[gk], writes=[dkey])
    return f


def emit_vtrans(C, tiles):
    for i in range(0, len(tiles), 4):
        _vtrans_group(C, tiles[i:i + 4])


def _vtrans_group(C, grp):
    S = C.S
    gi = next_g(C)
    gk = ("G", gi)
    gb = C.G[:, gi, :].bitcast(BF16)
    n = len(grp)
    ti0 = grp[0][0]
    assert [t[0] for t in grp] == list(range(ti0, ti0 + n))
    for j, (ti, src, edge) in enumerate(grp):
        S.op("pe", lambda e, j=j, src=src: e.transpose(gb[:, j * 128:(j + 1) * 128], src, C.ident[:, :]),
             reads=["VT", "ident"], writes=[gk])
    S.op("act", lambda e: e.activation(out=C.V[:, ti0:ti0 + n, :].rearrange("p a c -> p (a c)"), in_=gb[:, 0:n * 128], func=AF.Copy),
         reads=[gk], writes=["V"])


XWF = OWN + 2 * (HALO1 + HALO0)


def l0_decl(C, nc, xwin):
    S = C.S
    io = Ctx()
    io.xw = nc.dram_tensor("xw", [D_MODEL, xwin], F32, kind="ExternalInput").ap()
    io.gam = nc.dram_tensor("gamd", [128, 3 * NCH], F32, kind="ExternalInput").ap()
    io.w_in = nc.dram_tensor("w_in0", [NPAIR * 4, 128, 1024], F32, kind="ExternalInput").ap()
    io.w_out = nc.dram_tensor("w_out0", [NPAIR, 128, 1024], F32, kind="ExternalInput").ap()
    io.tab = nc.dram_tensor("tab0", [NPAIR, 128, 2 * NTABBLK * 128], F32, kind="ExternalInput").ap()
    io.cst = nc.dram_tensor("cst", [128, CSTW], F32, kind="ExternalInput").ap()
    io.xv = io.xw.rearrange("(c p) t -> p c t", p=128)
    io.xh = C.KV[:, :].bitcast(F32).rearrange("p (c t) -> p c t", c=NCH)
    cst = io.cst
    S.dma("sp", lambda e: e.dma_start(out=C.gam[:].rearrange("p a c -> p (a c)"), in_=io.gam), writes=["gam"], slot="c0")
    S.dma("pool", lambda e: e.dma_start(out=C.ident[:], in_=cst[:, 0:128]), writes=["ident"], slot="c1")
    S.dma("pool", lambda e: e.dma_start(out=C.vm[:].rearrange("p a c -> p (a c)"), in_=cst[:, 128:768]), writes=["vm"], slot="c2")
    S.dma("sp", lambda e: e.dma_start(out=C.vcol[:], in_=cst[:, 768:772]), writes=["vcol"], slot="c3")
    return io


def l0_pass(C, io, Tab, rs, NT, special, npair=NPAIR):
    S = C.S
    xv, xh = io.xv, io.xh
    NQ = NT * 128
    WIN = NQ + 2 * HALO0
    NTB = NQ // 512
    for c in range(NCH):
        S.dma("sp", lambda e, c=c: e.dma_start(out=C.xT[:, c, 0:NQ], in_=xv[:, c, rs:rs + NQ]),
              writes=[("x", c, tb) for tb in range(NTB)], slot=("xl", c))
    S.dma("sp", lambda e: e.dma_start(out=xh[:, :, 0:HALO0], in_=xv[:, :, rs - HALO0:rs]), writes=["xh", "KT", "VT"], slot="xh0")
    S.dma("sp", lambda e: e.dma_start(out=xh[:, :, HALO0:2 * HALO0], in_=xv[:, :, rs + NQ:rs + NQ + HALO0]),
          writes=["xh", "KT", "VT"], slot="xh1")

    wslots = {}

    def prefetch(p):
        if p >= npair:
            return
        wslots[p] = [load_wblock(C, io.w_in[p * 4 + k]) for k in range(4)]
        S.dma("pool", lambda e: e.dma_start(out=C.Wo[:, p % 2, :], in_=io.w_out[p]), writes=[("Wo", p % 2)], slot=("Wo", p % 2))

    def prefetch_tab(p):
        S.dma("pool", lambda e: e.dma_start(out=Tab[:, 0, :], in_=io.tab[p]), writes=[("Tab", k_) for k_ in TAB_OFFS], slot=("Tab", 0))

    prefetch(0)

    emit_norm(C, lambda c, t0, n: (C.xT[:, c, t0:t0 + n], ("x", c, t0 // 512)), NQ, 0,
              lambda c, t0, n: (C.hT[:, c, HALO0 + t0:HALO0 + t0 + n], "hT"))
    for half in range(2):
        emit_norm(C, lambda c, t0, n, half=half: (xh[:, c, half * HALO0 + t0:half * HALO0 + t0 + n], "xh"), HALO0, 0,
                  lambda c, t0, n, half=half: (C.hT[:, c, half * (HALO0 + NQ) + t0:half * (HALO0 + NQ) + t0 + n], "hT"))

    hkey = "hT"
    for p in range(npair):
        prefetch(p + 1)
        prefetch_tab(p)
        _l0_pair(C, p, wslots[p], Tab, NT, WIN, NTB, special)


def _l0_pair(C, p, ws, Tab, NT, WIN, NTB, special):
    S = C.S
    hkey = "hT"
    wq, wk, wv, wg = ws
    for tb in range(WIN // 512):
        proj_block(C, wk, lambda c, tb=tb: (C.hT[:, c, tb * 512:(tb + 1) * 512], hkey), 512,
                   evac_copy_act(C, C.KT[:, tb * 512:(tb + 1) * 512], "KT"))
    for tb in range(WIN // 512):
        proj_block(C, wv, lambda c, tb=tb: (C.hT[:, c, tb * 512:(tb + 1) * 512], hkey), 512,
                   evac_copy_dve(C, C.VT[:, tb * 512:(tb + 1) * 512], "VT"))
    emit_vtrans(C, [(wt, C.VT[:, wt * 128:(wt + 1) * 128], None) for wt in range(WIN // 128)])
    for tb in range(NTB):
        proj_block(C, wq, lambda c, tb=tb: (C.hT[:, c, HALO0 + tb * 512:HALO0 + (tb + 1) * 512], hkey), 512,
                   evac_q(C, tb * 512, 512))
    for tb in range(NTB):
        proj_block(C, wg, lambda c, tb=tb: (C.hT[:, c, HALO0 + tb * 512:HALO0 + (tb + 1) * 512], hkey), 512,
                   evac_silu(C, C.GT[:, tb * 512:(tb + 1) * 512], "GT"))
    for key in ((0, 1, "G", 14, 15) if special else ("G",)):
        toff_, offs_ = TAB_OFFS[key]
        a_, b_ = toff_ * 256, (toff_ + len(offs_)) * 256
        S.op("act", lambda e, a_=a_, b_=b_: e.activation(out=Tab[:, 0, a_:b_], in_=Tab[:, 0, a_:b_], func=AF.Exp),
             reads=[("Tab", key)], writes=[("Tab", key)])
    tabf = (lambda lt: tab_for(lt)) if special else (lambda lt: TAB_OFFS["G"])
    units = []
    for lt in range(NT):
        toff, offs = tabf(lt)
        for i in range(0, len(offs), 2):
            units.append((lt, toff, offs[i:i + 2], i, i == 0, i + 2 >= len(offs)))
    tkey = (lambda lt: lt if lt in TAB_OFFS else "G") if special else (lambda lt: "G")

    def qk(u):
        lt, toff, offs, kbase, first, last = units[u]
        sb, eb = u % 4, u % 4
        n = len(offs)
        for k, o in enumerate(offs):
            wt = lt + 2 + o
            S.op("pe", lambda e, wt=wt, k=k: e.matmul(
                C.Sp[:, sb, k * 256:(k + 1) * 256].rearrange("p (a c) -> p a c", a=2), lhsT=C.KT[:, wt * 128:(wt + 1) * 128],
                rhs=C.QT[:, :, lt * 128:(lt + 1) * 128], start=True, stop=True),
                reads=["KT", "QT"], writes=[("S", sb)])
        S.op("act", lambda e: e.activation(out=C.E[:, eb, 0:n * 256], in_=C.Sp[:, sb, 0:n * 256], func=AF.Exp, scale=0.125),
             reads=[("S", sb)], writes=[("E", eb)])
        t0 = (toff + kbase) * 256
        S.op(MASK_ENG if u % 3 != 0 else "dve", lambda e: e.tensor_tensor(out=C.P[:, eb, 0:n * 256], in0=C.E[:, eb, 0:n * 256],
                                                 in1=Tab[:, 0, t0:t0 + n * 256], op=ALU.mult),
             reads=[("E", eb), ("Tab", tkey(lt))], writes=[("P", eb)])

    def pv(u):
        lt, toff, offs, kbase, first, last = units[u]
        eb = u % 4
        ob = lt % 2
        for k, o in enumerate(offs):
            wt = lt + 2 + o
            S.op("pe", lambda e, wt=wt, k=k: e.matmul(
                C.O[:, ob, 0:256], lhsT=C.V[:, wt, :], rhs=C.P[:, eb, k * 256:(k + 1) * 256],
                start=(first and k == 0), stop=False), reads=["V", ("P", eb)], writes=[("O", ob)])
            for hh in range(2):
                S.op("pe", lambda e, k=k, hh=hh, fin=(last and k == len(offs) - 1 and hh == 1): e.matmul(
                    C.O[:, ob, 256:384], lhsT=C.vm[:, 3 + hh, :], rhs=C.P[:, eb, k * 256 + hh * 128:k * 256 + (hh + 1) * 128],
                    start=False, stop=fin), reads=["vm", ("P", eb)], writes=[("O", ob)])
        if last:
            td, tn = 2 * (lt % 2), 2 * (lt % 2) + 1
            S.op("act", lambda e: e.activation(out=C.tmp[:, td, :], in_=C.O[:, ob, 256:384], func=AF.Ln), reads=[("O", ob)], writes=[("tmp", td)])
            S.op("act", lambda e: e.activation(out=C.tmp[:, td, :], in_=C.tmp[:, td, :], func=AF.Exp, scale=-1.0), reads=[("tmp", td)], writes=[("tmp", td)])
            for hh in range(2):
                hs = slice(64 * hh, 64 * hh + 64)
                S.op("dve", lambda e, hs=hs, hh=hh: e.tensor_tensor(out=C.tmp[hs, tn, :], in0=C.O[hs, ob, hh * 128:(hh + 1) * 128],
                                                                 in1=C.tmp[hs, td, :], op=ALU.mult),
                     reads=[("O", ob), ("tmp", td)], writes=[("tmp", tn)])
            S.op(FIN_ENG, lambda e: e.tensor_tensor(out=C.uT[:, lt * 128:(lt + 1) * 128], in0=C.tmp[:, tn, :],
                                                    in1=C.GT[:, lt * 128:(lt + 1) * 128], op=ALU.mult),
                 reads=[("tmp", tn), "GT"], writes=[("uT", lt // 4)])
            if lt % 4 == 3:
                pend_outproj(C, p, lt // 4)

    for u0 in range(min(3, len(units))):
        qk(u0)
    for u in range(len(units)):
        if u + 3 < len(units):
            qk(u + 3)
        pv(u)
        pop_outproj(C, 1)
    flush_outproj(C)


def build_l0(nc, es, npair=NPAIR):
    C = common_setup(nc, es)
    S = C.S
    io = l0_decl(C, nc, W0)
    x1o = nc.dram_tensor("x1T", [D_MODEL, OWN], F32, kind="ExternalOutput").ap()
    h1o = nc.dram_tensor("h1T", [D_MODEL, OWN], BF16, kind="ExternalOutput").ap()
    Tab = S.sbuf("Tab", [128, 1, 2 * NTABBLK * 128], BF16)
    hout = S.sbuf("hout", [128, 2, 512], BF16)
    l0_pass(C, io, Tab, HALO0, 16, True, npair=npair)
    outs = []
    for c in range(NCH):
        outs.append(S.dma("sp", lambda e, c=c: e.dma_start(out=x1o.rearrange("(c p) t -> p c t", p=128)[:, c, :], in_=C.xT[:, c, :]),
                          reads=[("x", c, tb) for tb in range(4)], slot=("xo", c)))
    hov = h1o.rearrange("(c p) t -> p c t", p=128)
    emit_norm_out(C, 1, hout, hov, outs, 4)
    S.final_wait("sp", outs)
    S.emit()
    return C


def emit_norm_out(C, gidx, stage, dst_view, outs, ntb=4, dcol=0):
    k = [0]
    for tb in range(ntb):
        emit_norm_block_out(C, tb, gidx, stage, dst_view, outs, k, dcol)


def emit_norm_block_out(C, tb, gidx, stage, dst_view, outs, k, dcol=0):
    S = C.S
    n = 512
    t0 = tb * 512
    gi = next_g(C)
    gk = ("G", gi)
    for c in range(NCH):
        sb = c % 2
        S.op("act", lambda e, c=c, sb=sb: e.activation(out=C.sq[:, sb, :], in_=C.xT[:, c, t0:t0 + n], func=AF.Square),
             reads=[("x", c, tb)], writes=[("sq", sb)])
        S.op("pe", lambda e, c=c, sb=sb: e.matmul(C.G[:, gi, :], lhsT=C.onesf[:, :], rhs=C.sq[:, sb, :],
                                                  start=(c == 0), stop=(c == NCH - 1)),
             reads=["onesf", ("sq", sb)], writes=[gk])
    rs, rkeys = rstd_buf(C)
    S.op("act", lambda e: e.activation(out=rs[:, :], in_=C.G[:, gi, :], func=AF.Ln, scale=1.0 / D_MODEL, bias=C.epsb[:, 0:1]),
         reads=[gk, "epsb"], writes=rkeys)
    S.op("act", lambda e: e.activation(out=rs[:, :], in_=rs[:, :], func=AF.Exp, scale=-0.5), reads=rkeys, writes=rkeys)
    for c in range(NCH):
        b = k[0] % 2
        k[0] += 1
        S.op("dve", lambda e, c=c, b=b: e.scalar_tensor_tensor(out=stage[:, b, :], in0=C.xT[:, c, t0:t0 + n],
                                                             scalar=C.gam[:, gidx, c:c + 1], in1=rs[:, :],
                                                             op0=ALU.mult, op1=ALU.mult),
             reads=[("x", c, tb), "gam"] + rkeys, writes=[("stage", b)])
        outs.append(S.dma("sp", lambda e, c=c, b=b: e.dma_start(out=dst_view[:, c, dcol + t0:dcol + t0 + n], in_=stage[:, b, :]),
                          reads=[("stage", b)], writes=["hwin"], slot=("so", b)))


def build_fused(nc, es, npair=NPAIR):
    C = common_setup(nc, es)
    S = C.S
    C.big = S.sbuf("big", [128, 2 * OWN], F32)
    io = l0_decl(C, nc, XWF)
    w_in = nc.dram_tensor("w_in1", [NPAIR * 10, 128, 1024], F32, kind="ExternalInput").ap()
    w_out = nc.dram_tensor("w_out1", [NPAIR, 128, 1024], F32, kind="ExternalInput").ap()
    dtab = nc.dram_tensor("dtab", [128, 256], F32, kind="ExternalInput").ap()
    yo = nc.dram_tensor("yT", [D_MODEL, OWN], F32, kind="ExternalOutput").ap()
    hwin = nc.dram_tensor("hwin", [D_MODEL, W1], BF16).ap()
    hv = hwin.rearrange("(c p) t -> p c t", p=128)

    Tab = C.big[:, :].bitcast(BF16)[:, 0:2 * NTABBLK * 128].rearrange("p (a c) -> p a c", a=1)
    hb = S.sbuf("hb", [128, 2, NCH, 256], BF16)
    acc = C.big[:, :].rearrange("p (a t) -> p a t", a=2)
    M1 = S.sbuf("M1", [128, 3, 512], BF16)
    D = S.sbuf("D", [128, 256], BF16)
    stage = acc[:, 0, 0:1024].rearrange("p (a c) -> p a c", a=2)
    S.dma("pool", lambda e: e.dma_start(out=D[:], in_=dtab), writes=["D"], slot="c4")

    hst = hb[:, :, 0:2, :].rearrange("p a b c -> p a (b c)")
    hw_outs = []
    for rs, dcol in ((HALO0, 0), (HALO0 + HALO1 + OWN + 0, HALO1 + OWN)):
        l0_pass(C, io, Tab, rs, 8, False, npair=npair)
        emit_norm_out(C, 1, hst, hv, hw_outs, 2, dcol)
    for b_ in range(2):
        S.readers.setdefault(("hb", b_), []).extend(S.readers.get(("stage", b_), []))
        S.last_w.setdefault(("hb", b_), []).extend(S.last_w.get(("stage", b_), []))
    l0_pass(C, io, Tab, HALO0 + HALO1, 16, True, npair=npair)
    emit_norm(C, lambda c, t0, n: (C.xT[:, c, t0:t0 + n], ("x", c, t0 // 512)), OWN, 1,
              lambda c, t0, n: (C.hT[:, c, t0:t0 + n], "hT"))

    blocks = [w_in[p * 10 + k_] for p in range(npair) for k_ in range(10)]
    issued = [0]
    slots = {}

    def wget(i):
        while issued[0] < min(len(blocks), i + 5):
            slots[issued[0]] = load_wblock(C, blocks[issued[0]])
            issued[0] += 1
        return slots[i]

    hbc = [0]
    l1_body(C, npair, wget, w_out, hv, hb, hbc, acc, M1, D)
    flush_outproj(C)
    outs = []
    emit_norm_out(C, 2, stage, yo.rearrange("(c p) t -> p c t", p=128), outs)
    S.final_wait("sp", outs)
    S.emit()
    return C


def l1_body(C, npair, wget, w_out, hv, hb, hbc, acc, M1, D):
    S = C.S
    for p in range(npair):
        _l1_pair(C, p, wget, w_out, hv, hb, hbc, acc, M1, D)


def _l1_pair(C, p, wget, w_out, hv, hb, hbc, acc, M1, D):
    S = C.S
    S.dma("pool", lambda e: e.dma_start(out=C.Wo[:, p % 2, :], in_=w_out[p]), writes=[("Wo", p % 2)], slot=("Wo", p % 2))
    for g, d in enumerate(DILS):
        for hh in range(2):
            slope = 2.0 ** (-8.0 * (2 * p + hh + 1) / 16.0)
            S.op("act", lambda e, g=g, hh=hh, sc=-slope * d: e.activation(
                out=M1[:, g, :].rearrange("p (kl hh q) -> p kl hh q", kl=2, hh=2)[:, :, hh, :],
                in_=D[:, :].rearrange("p (kl q) -> p kl q", kl=2), func=AF.Exp, scale=sc),
                 reads=["D"], writes=["M1"])
    wg = wget(p * 10)
    for tb in range(4):
        proj_block(C, wg, lambda c, tb=tb: (C.hT[:, c, tb * 512:(tb + 1) * 512], "hT"), 512,
                   evac_silu(C, C.GT[:, tb * 512:(tb + 1) * 512], "GT"))
    for g, d in enumerate(DILS):
        _l1_group(C, p, g, d, wget, hv, hb, hbc, acc, M1)
    flush_outproj(C)
    for tb in range(4):
        ts_ = slice(tb * 512, (tb + 1) * 512)
        keys = [("acc", k) for k in range(tb * 4, tb * 4 + 4)]
        S.op("act", lambda e, ts_=ts_: e.activation(out=acc[:, 1, ts_], in_=acc[:, 1, ts_], func=AF.Ln), reads=keys, writes=keys)
        S.op("act", lambda e, ts_=ts_: e.activation(out=acc[:, 1, ts_], in_=acc[:, 1, ts_], func=AF.Exp, scale=-1.0), reads=keys, writes=keys)
        S.op("dve", lambda e, ts_=ts_: e.tensor_tensor(out=acc[:, 0, ts_], in0=acc[:, 0, ts_], in1=acc[:, 1, ts_], op=ALU.mult),
             reads=keys, writes=keys)
        S.op("dve", lambda e, ts_=ts_: e.tensor_tensor(out=C.uT[:, ts_], in0=acc[:, 0, ts_], in1=C.GT[:, ts_], op=ALU.mult),
             reads=keys + ["GT"], writes=[("uT", tb)])
        pend_outproj(C, p, tb)


def _l1_group(C, p, g, d, wget, hv, hb, hbc, acc, M1):
    S = C.S
    halo = 64 * d
    base = HALO1 - halo
    wq, wk, wv = wget(p * 10 + 1 + g * 3), wget(p * 10 + 2 + g * 3), wget(p * 10 + 3 + g * 3)
    hn = min(halo, 256)
    hblks = [(True, w0, hn) for w0 in range(base, HALO1, hn)] + \
            [(True, w0, hn) for w0 in range(HALO1 + OWN, HALO1 + OWN + halo, hn)]
    oblks = [(False, HALO1 + tb * 512, 512) for tb in range(4)]
    blks = []
    per = -(-len(hblks) // 4)
    for i in range(4):
        blks.extend(hblks[i * per:(i + 1) * per])
        blks.append(oblks[i])
    for (is_h, w0, n) in blks:
        col = w0 - base
        if is_h:
            hs_ = hbc[0] % 2
            hbc[0] += 1
            S.dma("sp", lambda e, hs_=hs_, w0=w0, n=n: e.dma_start(out=hb[:, hs_, :, 0:n], in_=hv[:, :, w0:w0 + n]),
                  reads=["hwin"], writes=[("hb", hs_)], slot=("hb", hs_))
            h_of = lambda c, hs_=hs_, n=n: (hb[:, hs_, c, 0:n], ("hb", hs_))
        else:
            t0 = w0 - HALO1
            h_of = lambda c, t0=t0, n=n: (C.hT[:, c, t0:t0 + n], "hT")
        proj_block(C, wk, h_of, n, evac_copy_act(C, C.KT[:, col:col + n], "KT"))
        if is_h:
            side = 0 if w0 < HALO1 else 1
            proj_block(C, wv, h_of, n, evac_scaled_dve(C, C.VT[:, col:col + n], "VT", C.vcol[:, 2 + side:3 + side]))
        else:
            proj_block(C, wv, h_of, n, evac_copy_dve(C, C.VT[:, col:col + n], "VT"))
    nq = OWN // (128 * d)
    nkt = nq + 1
    tiles = []
    for r in range(d):
        for kt in range(nkt):
            c0 = 128 * kt * d + r
            edge = None
            tiles.append((r * nkt + kt, C.VT[:, c0:c0 + 127 * d + 1:d], edge))
    emit_vtrans(C, tiles)
    for tb in range(4):
        proj_block(C, wq, lambda c, tb=tb: (C.hT[:, c, tb * 512:(tb + 1) * 512], "hT"), 512,
                   evac_q(C, tb * 512, 512))
    units = [(r, qt) for r in range(d) for qt in range(nq)]

    def qk(u):
        r, qt = units[u]
        sb, eb = u % 4, u % 4
        q0 = 128 * qt * d + r
        for kl in range(2):
            c0 = 128 * (qt + kl) * d + r
            S.op("pe", lambda e, kl=kl, c0=c0: e.matmul(
                C.Sp[:, sb, kl * 256:(kl + 1) * 256].rearrange("p (a c) -> p a c", a=2),
                lhsT=C.KT[:, c0:c0 + 127 * d + 1:d], rhs=C.QT[:, :, q0:q0 + 127 * d + 1:d], start=True, stop=True),
                reads=["KT", "QT"], writes=[("S", sb)])
        S.op("act", lambda e: e.activation(out=C.E[:, eb, 0:512], in_=C.Sp[:, sb, 0:512], func=AF.Exp, scale=0.125),
             reads=[("S", sb)], writes=[("E", eb)])
        S.op(MASK_ENG, lambda e: e.tensor_tensor(out=C.P[:, eb, 0:512], in0=C.E[:, eb, 0:512], in1=M1[:, g, :], op=ALU.mult),
             reads=[("E", eb), "M1"], writes=[("P", eb)])

    def pv(u):
        r, qt = units[u]
        sb = u % 4
        ob = u % 2
        q0 = 128 * qt * d + r
        for kl in range(2):
            kt = qt + kl
            ti = r * nkt + kt
            vsel = 1 if kt == 0 else (2 if kt == nkt - 1 else 0)
            S.op("pe", lambda e, kl=kl, ti=ti: e.matmul(
                C.O[:, ob, 0:256], lhsT=C.V[:, ti, :], rhs=C.P[:, sb, kl * 256:(kl + 1) * 256],
                start=(kl == 0), stop=False), reads=["V", ("P", sb)], writes=[("O", ob)])
            S.op("pe", lambda e, kl=kl, vsel=vsel: e.matmul(
                C.O[:, ob, 256:512], lhsT=C.vm[:, vsel, :], rhs=C.P[:, sb, kl * 256:(kl + 1) * 256],
                start=False, stop=(kl == 1)), reads=["vm", ("P", sb)], writes=[("O", ob)])
        blk_lo, blk_hi = q0 // 128, (q0 + 127 * d) // 128
        keys = [("acc", k) for k in range(blk_lo, blk_hi + 1)]
        for hh in range(2):
            hs = slice(64 * hh, 64 * hh + 64)
            dst = acc[hs, :, q0:q0 + 127 * d + 1:d]
            src = C.O[hs, ob, :].rearrange("p (a b c) -> p a b c", a=2, b=2)[:, :, hh, :]
            if g == 0:
                S.op("dve", lambda e, dst=dst, src=src: e.tensor_copy(out=dst, in_=src), reads=[("O", ob)], writes=keys)
            else:
                S.op("dve", lambda e, dst=dst, src=src: e.tensor_tensor(out=dst, in0=src, in1=dst, op=ALU.add),
                     reads=[("O", ob)] + keys, writes=keys)

    for u0 in range(min(3, len(units))):
        qk(u0)
    for u in range(len(units)):
        if u + 3 < len(units):
            qk(u + 3)
        pv(u)
        pop_outproj(C, 1)


def wblock(w, col0):
    return np.ascontiguousarray(w[:, col0:col0 + 128].reshape(NCH, 128, 128).transpose(1, 0, 2).reshape(128, 1024))


def gam_layout(*gs):
    return np.ascontiguousarray(np.concatenate([g.reshape(NCH, 128).T for g in gs], axis=1)).astype(np.float32)


def make_cst(j):
    cst = np.zeros((128, CSTW), np.float32)
    cst[:, 0:128] = np.eye(128, dtype=np.float32)
    vl = np.ones(128, np.float32)
    vr = np.ones(128, np.float32)
    if j == 0:
        vl[:64] = 0.0
    if j == 3:
        vr[64:] = 0.0
    cst[:, 128:256] = 1.0
    cst[:, 256:384] = vl[:, None]
    cst[:, 384:512] = vr[:, None]
    cst[:, 512:576] = 1.0
    cst[:, 704:768] = 1.0
    cst[:, 768] = vl
    cst[:, 769] = vr
    cst[:, 770] = 0.0 if j == 0 else 1.0
    cst[:, 771] = 0.0 if j == 3 else 1.0
    return cst


def make_tab(rpb, j):
    kp = np.arange(128)
    kr2, kc = kp // 64, kp % 64
    q = np.arange(128)
    qr2, qc = q // 64, q % 64
    cstart = np.clip(qc - 8, 0, 64 - 16)
    colv = (kc[:, None] >= cstart[None, :]) & (kc[:, None] < cstart[None, :] + 16)
    coff = np.clip(kc[:, None] - qc[None, :] + 15, 0, 30)
    out = np.full((16, NTABBLK, 128, 128), NEG, np.float32)
    for key, (toff, offs) in TAB_OFFS.items():
        lt = 5 if key == "G" else key
        m = 16 * j + lt
        r = 2 * m + qr2
        rs = np.clip(r - 4, 0, 128 - 8)
        for k, o in enumerate(offs):
            krow = 2 * (m + o) + kr2
            rowv = (krow[:, None] >= rs[None, :]) & (krow[:, None] < rs[None, :] + 8) & (krow[:, None] >= 0) & (krow[:, None] < 128)
            valid = rowv & colv
            roff = np.clip(krow[:, None] - r[None, :] + 7, 0, 14)
            vals = rpb[:, roff, coff]
            out[:, toff + k] = np.where(valid[None], vals, np.float32(NEG))
    out = out.reshape(NPAIR, 2, NTABBLK, 128, 128).transpose(0, 3, 2, 1, 4).reshape(NPAIR, 128, 2 * NTABBLK * 128)
    return np.ascontiguousarray(out)


def window_T(xb, t0, halo):
    out = np.zeros((xb.shape[1], OWN + 2 * halo), xb.dtype)
    lo, hi = t0 - halo, t0 + OWN + halo
    a, b = max(lo, 0), min(hi, SEQ)
    out[:, a - lo:b - lo] = xb[a:b].T
    return out


_CACHE = {}


def get_prog(name, builder):
    if name not in _CACHE:
        nc = bass.Bass("TRN2", target_bir_lowering=False)
        es = ExitStack()
        builder(nc, es)
        _CACHE[name] = (nc, es)
    return _CACHE[name][0]


def run_l0(x, norm_0, w_in_0, rpb_0, w_out_0, norm_1, norm_f, trace=False):
    nc = get_prog("l0", build_l0)
    w_in_b = np.stack([wblock(w_in_0, k * 1024 + p * 128) for p in range(NPAIR) for k in range(4)])
    w_out_b = np.ascontiguousarray(w_out_0.reshape(NPAIR, 128, 1024))
    gam = gam_layout(norm_0, norm_1, norm_f)
    tabs = [make_tab(rpb_0, j) for j in range(4)]
    csts = [make_cst(j) for j in range(4)]
    in_maps = []
    for c in range(NCORE):
        b, j = divmod(c, 4)
        in_maps.append({"xw": window_T(x[b], j * OWN, HALO0), "gamd": gam, "w_in0": w_in_b, "w_out0": w_out_b,
                        "tab0": tabs[j], "cst": csts[j]})
    res = run_bass_kernel_spmd(nc, in_maps, core_ids=list(range(NCORE)), trace=trace)
    return res


def make_dtab():
    kp = np.arange(128)[:, None]
    j = np.arange(128)[None, :]
    out = np.empty((128, 256), np.float32)
    for kl, sh in enumerate((-64, 64)):
        delta = np.abs(kp + sh - j)
        out[:, kl * 128:(kl + 1) * 128] = np.where(delta <= 64, delta, 1.0e6)
    return out


def run_l1(x1T_list, h1T_list, w_in_1, w_out_1, norm_0, norm_1, norm_f, trace=False):
    nc = get_prog("l1", build_l1)
    cols = []
    for p in range(NPAIR):
        cols.append(9216 + p * 128)
        for g in range(3):
            for k in range(3):
                cols.append(g * 3072 + k * 1024 + p * 128)
    w_in_b = np.stack([wblock(w_in_1, c0) for c0 in cols])
    w_out_b = np.ascontiguousarray(w_out_1.reshape(NPAIR, 128, 1024))
    gam = gam_layout(norm_0, norm_1, norm_f)
    dt = make_dtab()
    csts = [make_cst(j) for j in range(4)]
    in_maps = []
    for c in range(NCORE):
        b, j = divmod(c, 4)
        hw = np.zeros((D_MODEL, W1), h1T_list[c].dtype)
        hw[:, HALO1:HALO1 + OWN] = h1T_list[c]
        if j > 0:
            hw[:, 0:HALO1] = h1T_list[c - 1][:, OWN - HALO1:]
        if j < 3:
            hw[:, HALO1 + OWN:] = h1T_list[c + 1][:, 0:HALO1]
        in_maps.append({"x1w": x1T_list[c], "h1w": hw, "gamd": gam, "w_in1": w_in_b, "w_out1": w_out_b,
                        "cst": csts[j], "dtab": dt})
    return run_bass_kernel_spmd(nc, in_maps, core_ids=list(range(NCORE)), trace=trace)


def kernel(x, norm_0, w_in_0, rpb_0, w_out_0, norm_1, w_in_1, w_out_1, norm_f):
    f = lambda a: np.ascontiguousarray(np.asarray(a, dtype=np.float32))
    x, norm_0, w_in_0, rpb_0, w_out_0, norm_1, w_in_1, w_out_1, norm_f = map(
        f, (x, norm_0, w_in_0, rpb_0, w_out_0, norm_1, w_in_1, w_out_1, norm_f))
    nc, in_maps = run_fused(x, norm_0, w_in_0, rpb_0, w_out_0, norm_1, w_in_1, w_out_1, norm_f)
    res = run_bass_kernel_spmd(nc, in_maps, core_ids=list(range(NCORE)))
    out = np.empty((2, SEQ, D_MODEL), np.float32)
    for c in range(NCORE):
        b, j = divmod(c, 4)
        out[b, j * OWN:(j + 1) * OWN, :] = np.asarray(res.results[c]["yT"]).T
    return out


def make_hidx(j):
    l = j - 1 if j > 0 else j
    r = j + 1 if j < 3 else j
    p = np.arange(128, dtype=np.int64)
    out = np.zeros((128, NHIDX), np.uint32)
    for side, nbr in enumerate((l, r)):
        for c in range(NCH):
            for blk in range(4):
                tb = 4 + blk if side == 0 else blk
                out[:, side * 32 + c * 4 + blk] = (nbr * D_MODEL + c * 128 + p) * 8 + tb
    return out


def l1_cols():
    cols = []
    for p in range(NPAIR):
        cols.append(9216 + p * 128)
        for g in range(3):
            for k in range(3):
                cols.append(g * 3072 + k * 1024 + p * 128)
    return cols


def run_fused(x, norm_0, w_in_0, rpb_0, w_out_0, norm_1, w_in_1, w_out_1, norm_f, trace=False):
    nc = get_prog("fused", build_fused)
    w_in0_b = np.stack([wblock(w_in_0, k * 1024 + p * 128) for p in range(NPAIR) for k in range(4)])
    w_out0_b = np.ascontiguousarray(w_out_0.reshape(NPAIR, 128, 1024))
    w_in1_b = np.stack([wblock(w_in_1, c0) for c0 in l1_cols()])
    w_out1_b = np.ascontiguousarray(w_out_1.reshape(NPAIR, 128, 1024))
    gam = gam_layout(norm_0, norm_1, norm_f)
    dt = make_dtab()
    tabs = [make_tab(rpb_0, j) for j in range(4)]
    csts = [make_cst(j) for j in range(4)]
    hidx = [make_hidx(j) for j in range(4)]
    in_maps = []
    for c in range(NCORE):
        b, j = divmod(c, 4)
        in_maps.append({"xw": window_T(x[b], j * OWN, HALO0 + HALO1), "gamd": gam, "w_in0": w_in0_b, "w_out0": w_out0_b,
                        "tab0": tabs[j], "cst": csts[j], "w_in1": w_in1_b, "w_out1": w_out1_b, "dtab": dt})
    return nc, in_maps
```

```python
import numpy as np
from contextlib import ExitStack
import concourse.bass as bass
import concourse.mybir as mybir
from concourse.bass_utils import run_bass_kernel_spmd

F32 = mybir.dt.float32
BF16 = mybir.dt.bfloat16
AF = mybir.ActivationFunctionType
ALU = mybir.AluOpType


class Op:
    __slots__ = ("eng", "fn", "deps", "is_dma", "slot", "ord", "marked", "name")

    def __init__(self, eng, fn, is_dma=False, slot=None, name=""):
        self.eng = eng
        self.fn = fn
        self.deps = []
        self.is_dma = is_dma
        self.slot = slot
        self.ord = None
        self.marked = False
        self.name = name


class Sched:
    ENGS = ("pe", "act", "dve", "pool", "sp")

    def __init__(self, nc, es):
        self.nc = nc
        self.es = es
        self.q = {e: [] for e in self.ENGS}
        self.last_w = {}
        self.readers = {}
        self.finals = []
        self.slot_count = {}
        self.n_sb = 0

    def sbuf(self, name, shape, dtype):
        return self.es.enter_context(self.nc.sbuf_tensor(name, shape, dtype))

    def psum(self, name, shape, dtype):
        return self.es.enter_context(self.nc.psum_tensor(name, shape, dtype))

    def _add(self, op, reads, writes, deps):
        dl = []
        for r in reads:
            dl.extend(self.last_w.get(r, ()))
            self.readers.setdefault(r, []).append(op)
        for w_ in writes:
            prev = self.last_w.get(w_, [])
            dl.extend(prev)
            for rd in self.readers.get(w_, ()):
                if rd is not op:
                    dl.append(rd)
            if op.is_dma and prev and all(q.is_dma for q in prev):
                self.last_w[w_] = prev + [op]
            else:
                self.last_w[w_] = [op]
            self.readers[w_] = []
        dl.extend(deps)
        seen = set()
        for d in dl:
            if d is op or id(d) in seen:
                continue
            seen.add(id(d))
            if d.eng == "pe" and op.eng == "pe" and not d.is_dma and not op.is_dma:
                continue
            op.deps.append(d)
        self.q[op.eng].append(op)
        return op

    def op(self, eng, fn, reads=(), writes=(), deps=(), name=""):
        return self._add(Op(eng, fn, name=name), reads, writes, deps)

    def dma(self, eng, fn, reads=(), writes=(), deps=(), slot=None, name=""):
        if slot is None:
            slot = ("auto", len(self.slot_count))
        op = Op(eng, fn, is_dma=True, slot=slot, name=name)
        k = self.slot_count.get(slot, 0) + 1
        self.slot_count[slot] = k
        op.ord = k
        return self._add(op, reads, writes, deps)

    def final_wait(self, eng, ops):
        self.finals.append((eng, list(ops)))

    def emit(self):
        nc = self.nc
        for e in self.ENGS:
            for op in self.q[e]:
                for d in op.deps:
                    d.marked = True
        for _, ops in self.finals:
            for d in ops:
                d.marked = True
        for e in self.ENGS:
            for op in self.q[e]:
                if op.is_dma:
                    op.marked = True
        for e in self.ENGS:
            k = 0
            for op in self.q[e]:
                if not op.is_dma and op.marked:
                    k += 1
                    op.ord = k
        sems = {}

        def sem_of(op):
            key = ("dma", op.slot) if op.is_dma else ("eng", op.eng)
            if key not in sems:
                sems[key] = self.es.enter_context(nc.semaphore("s%d" % len(sems)))
            return sems[key]

        def val_of(op):
            return op.ord * 16 if op.is_dma else op.ord

        for e in self.ENGS:
            for op in self.q[e]:
                if op.marked:
                    sem_of(op)
        finals = {}
        for eng, ops in self.finals:
            finals.setdefault(eng, []).extend(ops)
        self.n_waits = 0

        def run(engname, engine):
            known = {}

            def need(d):
                s = sem_of(d)
                v = val_of(d)
                if known.get(id(s), 0) >= v:
                    return
                known[id(s)] = v
                engine.wait_ge(s, v)
                self.n_waits += 1

            for op in self.q[engname]:
                best = {}
                for d in op.deps:
                    s = sem_of(d)
                    if id(s) not in best or val_of(best[id(s)]) < val_of(d):
                        best[id(s)] = d
                for d in best.values():
                    need(d)
                ins = op.fn(engine)
                if op.marked:
                    ins.then_inc(sem_of(op), 16 if op.is_dma else 1)
            for d in finals.get(engname, ()):
                need(d)

        with nc.Block() as block:
            @block.sync
            def _(eng):
                run("sp", eng)

            @block.tensor
            def _(eng):
                run("pe", eng)

            @block.scalar
            def _(eng):
                run("act", eng)

            @block.vector
            def _(eng):
                run("dve", eng)

            @block.gpsimd
            def _(eng):
                run("pool", eng)


D_MODEL = 1024
SEQ = 8192
NCORE = 8
OWN = 2048
NCH = 8
NPAIR = 8
HALO0 = 256
W0 = OWN + 2 * HALO0
HALO1 = 1024
W1 = OWN + 2 * HALO1
DILS = (1, 4, 16)
NEG = -30000.0
EPS = 1e-6
TAB_OFFS = {"G": (0, (-2, -1, 0, 1, 2)), 0: (5, (-2, -1, 0, 1, 2, 3)), 1: (11, (-2, -1, 0, 1, 2)),
            14: (16, (-2, -1, 0, 1, 2)), 15: (21, (-3, -2, -1, 0, 1, 2))}
NTABBLK = 27
U32 = mybir.dt.uint32
NHIDX = 64
MASK_ENG = "pool"
FIN_ENG = "pool"
CSTW = 128 + 5 * 128 + 4


def tab_for(lt):
    return TAB_OFFS[lt] if lt in TAB_OFFS else TAB_OFFS["G"]


class Ctx:
    pass


def common_setup(nc, es, hcols=W0):
    C = Ctx()
    S = Sched(nc, es)
    C.S = S
    C.nc = nc
    C.G = S.psum("G", [128, 2, 512], F32)
    C.Sp = S.psum("Sp", [128, 4, 512], F32)
    C.O = S.psum("O", [128, 2, 512], F32)
    C.gi = 0
    C.xT = S.sbuf("xT", [128, NCH, OWN], F32)
    C.hT = S.sbuf("hT", [128, NCH, hcols], BF16)
    C.QT = S.sbuf("QT", [128, 2, OWN], BF16)
    C.KV = S.sbuf("KV", [128, 2 * W1], BF16)
    C.KT = C.KV[:, 0:W1]
    C.VT = C.KV[:, W1:2 * W1]
    C.V = S.sbuf("V", [128, 32, 128], BF16)
    C.GT = S.sbuf("GT", [128, OWN], BF16)
    C.uT = S.sbuf("uT", [128, OWN], BF16)
    C.Wb = S.sbuf("Wb", [128, 8, 1024], BF16)
    C.Wo = S.sbuf("Wo", [128, 2, 1024], BF16)
    C.E = S.sbuf("E", [128, 4, 512], BF16)
    C.P = S.sbuf("P", [128, 4, 512], BF16)
    C.sq = S.sbuf("sq", [128, 2, 512], BF16)
    C.rstd = S.sbuf("rstd", [128, 512], F32)
    C.tmp = S.sbuf("tmp", [128, 4, 128], F32)
    C.onesf = S.sbuf("onesf", [128, 128], BF16)
    C.vm = S.sbuf("vm", [128, 5, 128], BF16)
    C.vcol = S.sbuf("vcol", [128, 4], F32)
    C.gam = S.sbuf("gam", [128, 3, NCH], F32)
    C.wslot = 0
    C.pend = []
    C.nrm = 0
    C.epsb = S.sbuf("epsb", [128, 1], F32)
    C.ident = S.sbuf("ident", [128, 128], BF16)
    S.op("dve", lambda e: e.memset(C.epsb[:], EPS), writes=["epsb"])
    S.op("dve", lambda e: e.memset(C.onesf[:], 1.0), writes=["onesf"])
    S.op("pool", lambda e: e.memset(C.QT[:], 0.0), writes=["QT"])
    return C


def next_g(C):
    gi = C.gi
    C.gi = (gi + 1) % 2
    return gi


def load_wblock(C, src_ap):
    S = C.S
    s = C.wslot
    C.wslot = (s + 1) % 8
    S.dma("pool", lambda e: e.dma_start(out=C.Wb[:, s, :], in_=src_ap), writes=[("W", s)], slot=("W", s))
    return s


def proj_block(C, ws, h_of, n, evac):
    S = C.S
    gi = next_g(C)
    gk = ("G", gi)
    for c in range(NCH):
        rhs, rk = h_of(c)
        S.op("pe", lambda e, c=c, rhs=rhs: e.matmul(C.G[:, gi, 0:n], lhsT=C.Wb[:, ws, c * 128:(c + 1) * 128],
                                                    rhs=rhs, start=(c == 0), stop=(c == NCH - 1)),
             reads=[("W", ws), rk], writes=[gk])
    evac(C.G[:, gi, 0:n], gk)


def emit_norm(C, x_of, ntok, gidx, out_of):
    t0 = 0
    while t0 < ntok:
        n = min(512, ntok - t0)
        _norm_block(C, x_of, t0, n, gidx, out_of)
        t0 += n


def rstd_buf(C):
    k = C.nrm % 2
    C.nrm += 1
    if k == 0:
        return C.rstd[:, :], [("rstd", 0)]
    return C.tmp[:, :, :].rearrange("p a c -> p (a c)"), [("rstd", 1)] + [("tmp", i) for i in range(4)]


def _norm_block(C, x_of, t0, n, gidx, out_of):
    S = C.S
    gi = next_g(C)
    gk = ("G", gi)
    rs, rkeys = rstd_buf(C)
    for c in range(NCH):
        xa, xk = x_of(c, t0, n)
        sb = c % 2
        S.op("act", lambda e, xa=xa, sb=sb: e.activation(out=C.sq[:, sb, 0:n], in_=xa, func=AF.Square),
             reads=[xk], writes=[("sq", sb)])
        S.op("pe", lambda e, c=c, sb=sb: e.matmul(C.G[:, gi, 0:n], lhsT=C.onesf[:, :], rhs=C.sq[:, sb, 0:n],
                                                  start=(c == 0), stop=(c == NCH - 1)),
             reads=["onesf", ("sq", sb)], writes=[gk])
    S.op("act", lambda e: e.activation(out=rs[:, 0:n], in_=C.G[:, gi, 0:n], func=AF.Ln,
                                       scale=1.0 / D_MODEL, bias=C.epsb[:, 0:1]),
         reads=[gk, "epsb"], writes=rkeys)
    S.op("act", lambda e: e.activation(out=rs[:, 0:n], in_=rs[:, 0:n], func=AF.Exp, scale=-0.5), reads=rkeys, writes=rkeys)
    for c in range(NCH):
        xa, xk = x_of(c, t0, n)
        oa, ok = out_of(c, t0, n)
        if False:
            S.op("pool", lambda e, xa=xa, oa=oa, c=c: e.scalar_tensor_tensor(
                out=oa, in0=xa, scalar=C.gam[:, gidx, c:c + 1], in1=rs[:, 0:n], op0=ALU.mult, op1=ALU.mult),
                 reads=[xk, "gam"] + rkeys, writes=[ok])
        else:
            S.op("dve", lambda e, xa=xa, oa=oa, c=c: e.scalar_tensor_tensor(
                out=oa, in0=xa, scalar=C.gam[:, gidx, c:c + 1], in1=rs[:, 0:n], op0=ALU.mult, op1=ALU.mult),
                 reads=[xk, "gam"] + rkeys, writes=[ok])


def emit_outproj(C, pair, wo_src, tbs, ms=None):
    S = C.S
    for tb in tbs:
        for m in (range(NCH) if ms is None else ms):
            gi = next_g(C)
            gk = ("G", gi)
            S.op("pe", lambda e, m=m, gi=gi, tb=tb: e.matmul(C.G[:, gi, :], lhsT=C.Wo[:, pair % 2, m * 128:(m + 1) * 128],
                                                          rhs=C.uT[:, tb * 512:(tb + 1) * 512], start=True, stop=True),
                 reads=[("Wo", pair % 2), ("uT", tb)], writes=[gk])
            S.op("dve", lambda e, m=m, gi=gi, tb=tb: e.tensor_tensor(out=C.xT[:, m, tb * 512:(tb + 1) * 512],
                                                                 in0=C.G[:, gi, :], in1=C.xT[:, m, tb * 512:(tb + 1) * 512],
                                                                 op=ALU.add),
                 reads=[gk, ("x", m, tb)], writes=[("x", m, tb)])


def pend_outproj(C, pair, tb):
    for m in range(NCH):
        C.pend.append((pair, tb, m))


def pop_outproj(C, n=1):
    for _ in range(n):
        if C.pend:
            pair, tb, m = C.pend.pop(0)
            emit_outproj(C, pair, None, [tb], [m])


def flush_outproj(C):
    pop_outproj(C, len(C.pend))


def evac_copy_act(C, dst, dkey):
    def f(ps, gk):
        C.S.op("act", lambda e: e.activation(out=dst, in_=ps, func=AF.Copy), reads=[gk], writes=[dkey])
    return f


def evac_copy_dve(C, dst, dkey):
    def f(ps, gk):
        C.S.op("dve", lambda e: e.tensor_copy(out=dst, in_=ps), reads=[gk], writes=[dkey])
    return f


def evac_q(C, c0, n):
    def f(ps, gk):
        C.S.op("act", lambda e: e.activation(out=C.QT[0:64, 0, c0:c0 + n], in_=ps[0:64, :], func=AF.Copy), reads=[gk], writes=["QT"])
        C.S.op("act", lambda e: e.activation(out=C.QT[64:128, 1, c0:c0 + n], in_=ps[64:128, :], func=AF.Copy), reads=[gk], writes=["QT"])
    return f


def evac_scaled_dve(C, dst, dkey, flag_ap):
    def f(ps, gk):
        C.S.op("dve", lambda e: e.tensor_scalar(out=dst, in0=ps, scalar1=flag_ap, scalar2=None, op0=ALU.mult), reads=[gk, "vcol"], writes=[dkey])
    return f


def evac_silu(C, dst, dkey):
    def f(ps, gk):
        C.S.op("act", lambda e: e.activation(out=dst, in_=ps, func=AF.Silu), reads=[gk], writes=[dkey])
    return f


def emit_vtrans(C, tiles):
    for i in range(0, len(tiles), 4):
        _vtrans_group(C, tiles[i:i + 4])


def _vtrans_group(C, grp):
    S = C.S
    gi = next_g(C)
    gk = ("G", gi)
    gb = C.G[:, gi, :].bitcast(BF16)
    n = len(grp)
    ti0 = grp[0][0]
    assert [t[0] for t in grp] == list(range(ti0, ti0 + n))
    for j, (ti, src, edge) in enumerate(grp):
        S.op("pe", lambda e, j=j, src=src: e.transpose(gb[:, j * 128:(j + 1) * 128], src, C.ident[:, :]),
             reads=["VT", "ident"], writes=[gk])
    S.op("act", lambda e: e.activation(out=C.V[:, ti0:ti0 + n, :].rearrange("p a c -> p (a c)"), in_=gb[:, 0:n * 128], func=AF.Copy),
         reads=[gk], writes=["V"])


XWF = OWN + 2 * (HALO1 + HALO0)


def l0_decl(C, nc, xwin):
    S = C.S
    io = Ctx()
    io.xw = nc.dram_tensor("xw", [D_MODEL, xwin], F32, kind="ExternalInput").ap()
    io.gam = nc.dram_tensor("gamd", [128, 3 * NCH], F32, kind="ExternalInput").ap()
    io.w_in = nc.dram_tensor("w_in0", [NPAIR * 4, 128, 1024], F32, kind="ExternalInput").ap()
    io.w_out = nc.dram_tensor("w_out0", [NPAIR, 128, 1024], F32, kind="ExternalInput").ap()
    io.tab = nc.dram_tensor("tab0", [NPAIR, 128, 2 * NTABBLK * 128], F32, kind="ExternalInput").ap()
    io.cst = nc.dram_tensor("cst", [128, CSTW], F32, kind="ExternalInput").ap()
    io.xv = io.xw.rearrange("(c p) t -> p c t", p=128)
    io.xh = C.KV[:, :].bitcast(F32).rearrange("p (c t) -> p c t", c=NCH)
    cst = io.cst
    S.dma("sp", lambda e: e.dma_start(out=C.gam[:].rearrange("p a c -> p (a c)"), in_=io.gam), writes=["gam"], slot="c0")
    S.dma("pool", lambda e: e.dma_start(out=C.ident[:], in_=cst[:, 0:128]), writes=["ident"], slot="c1")
    S.dma("pool", lambda e: e.dma_start(out=C.vm[:].rearrange("p a c -> p (a c)"), in_=cst[:, 128:768]), writes=["vm"], slot="c2")
    S.dma("sp", lambda e: e.dma_start(out=C.vcol[:], in_=cst[:, 768:772]), writes=["vcol"], slot="c3")
    return io


def l0_pass(C, io, Tab, rs, NT, special, npair=NPAIR):
    S = C.S
    xv, xh = io.xv, io.xh
    NQ = NT * 128
    WIN = NQ + 2 * HALO0
    NTB = NQ // 512
    for c in range(NCH):
        S.dma("sp", lambda e, c=c: e.dma_start(out=C.xT[:, c, 0:NQ], in_=xv[:, c, rs:rs + NQ]),
              writes=[("x", c, tb) for tb in range(NTB)], slot=("xl", c))
    S.dma("sp", lambda e: e.dma_start(out=xh[:, :, 0:HALO0], in_=xv[:, :, rs - HALO0:rs]), writes=["xh", "KT", "VT"], slot="xh0")
    S.dma("sp", lambda e: e.dma_start(out=xh[:, :, HALO0:2 * HALO0], in_=xv[:, :, rs + NQ:rs + NQ + HALO0]),
          writes=["xh", "KT", "VT"], slot="xh1")

    wslots = {}

    def prefetch(p):
        if p >= npair:
            return
        wslots[p] = [load_wblock(C, io.w_in[p * 4 + k]) for k in range(4)]
        S.dma("pool", lambda e: e.dma_start(out=C.Wo[:, p % 2, :], in_=io.w_out[p]), writes=[("Wo", p % 2)], slot=("Wo", p % 2))

    def prefetch_tab(p):
        S.dma("pool", lambda e: e.dma_start(out=Tab[:, 0, :], in_=io.tab[p]), writes=[("Tab", k_) for k_ in TAB_OFFS], slot=("Tab", 0))

    prefetch(0)

    emit_norm(C, lambda c, t0, n: (C.xT[:, c, t0:t0 + n], ("x", c, t0 // 512)), NQ, 0,
              lambda c, t0, n: (C.hT[:, c, HALO0 + t0:HALO0 + t0 + n], "hT"))
    for half in range(2):
        emit_norm(C, lambda c, t0, n, half=half: (xh[:, c, half * HALO0 + t0:half * HALO0 + t0 + n], "xh"), HALO0, 0,
                  lambda c, t0, n, half=half: (C.hT[:, c, half * (HALO0 + NQ) + t0:half * (HALO0 + NQ) + t0 + n], "hT"))

    hkey = "hT"
    for p in range(npair):
        prefetch(p + 1)
        prefetch_tab(p)
        _l0_pair(C, p, wslots[p], Tab, NT, WIN, NTB, special)


def _l0_pair(C, p, ws, Tab, NT, WIN, NTB, special):
    S = C.S
    hkey = "hT"
    wq, wk, wv, wg = ws
    for tb in range(WIN // 512):
        proj_block(C, wk, lambda c, tb=tb: (C.hT[:, c, tb * 512:(tb + 1) * 512], hkey), 512,
                   evac_copy_act(C, C.KT[:, tb * 512:(tb + 1) * 512], "KT"))
    for tb in range(WIN // 512):
        proj_block(C, wv, lambda c, tb=tb: (C.hT[:, c, tb * 512:(tb + 1) * 512], hkey), 512,
                   evac_copy_dve(C, C.VT[:, tb * 512:(tb + 1) * 512], "VT"))
    emit_vtrans(C, [(wt, C.VT[:, wt * 128:(wt + 1) * 128], None) for wt in range(WIN // 128)])
    for tb in range(NTB):
        proj_block(C, wq, lambda c, tb=tb: (C.hT[:, c, HALO0 + tb * 512:HALO0 + (tb + 1) * 512], hkey), 512,
                   evac_q(C, tb * 512, 512))
    for tb in range(NTB):
        proj_block(C, wg, lambda c, tb=tb: (C.hT[:, c, HALO0 + tb * 512:HALO0 + (tb + 1) * 512], hkey), 512,
                   evac_silu(C, C.GT[:, tb * 512:(tb + 1) * 512], "GT"))
    for key in ((0, 1, "G", 14, 15) if special else ("G",)):
        toff_, offs_ = TAB_OFFS[key]
        a_, b_ = toff_ * 256, (toff_ + len(offs_)) * 256
        S.op("act", lambda e, a_=a_, b_=b_: e.activation(out=Tab[:, 0, a_:b_], in_=Tab[:, 0, a_:b_], func=AF.Exp),
             reads=[("Tab", key)], writes=[("Tab", key)])
    tabf = (lambda lt: tab_for(lt)) if special else (lambda lt: TAB_OFFS["G"])
    units = []
    for lt in range(NT):
        toff, offs = tabf(lt)
        for i in range(0, len(offs), 2):
            units.append((lt, toff, offs[i:i + 2], i, i == 0, i + 2 >= len(offs)))
    tkey = (lambda lt: lt if lt in TAB_OFFS else "G") if special else (lambda lt: "G")

    def qk(u):
        lt, toff, offs, kbase, first, last = units[u]
        sb, eb = u % 4, u % 4
        n = len(offs)
        for k, o in enumerate(offs):
            wt = lt + 2 + o
            S.op("pe", lambda e, wt=wt, k=k: e.matmul(
                C.Sp[:, sb, k * 256:(k + 1) * 256].rearrange("p (a c) -> p a c", a=2), lhsT=C.KT[:, wt * 128:(wt + 1) * 128],
                rhs=C.QT[:, :, lt * 128:(lt + 1) * 128], start=True, stop=True),
                reads=["KT", "QT"], writes=[("S", sb)])
        S.op("act", lambda e: e.activation(out=C.E[:, eb, 0:n * 256], in_=C.Sp[:, sb, 0:n * 256], func=AF.Exp, scale=0.125),
             reads=[("S", sb)], writes=[("E", eb)])
        t0 = (toff + kbase) * 256
        S.op(MASK_ENG if u % 3 == 2 else "dve", lambda e: e.tensor_tensor(out=C.P[:, eb, 0:n * 256], in0=C.E[:, eb, 0:n * 256],
                                                 in1=Tab[:, 0, t0:t0 + n * 256], op=ALU.mult),
             reads=[("E", eb), ("Tab", tkey(lt))], writes=[("P", eb)])

    def pv(u):
        lt, toff, offs, kbase, first, last = units[u]
        eb = u % 4
        ob = lt % 2
        for k, o in enumerate(offs):
            wt = lt + 2 + o
            S.op("pe", lambda e, wt=wt, k=k: e.matmul(
                C.O[:, ob, 0:256], lhsT=C.V[:, wt, :], rhs=C.P[:, eb, k * 256:(k + 1) * 256],
                start=(first and k == 0), stop=False), reads=["V", ("P", eb)], writes=[("O", ob)])
            for hh in range(2):
                S.op("pe", lambda e, k=k, hh=hh, fin=(last and k == len(offs) - 1 and hh == 1): e.matmul(
                    C.O[:, ob, 256:384], lhsT=C.vm[:, 3 + hh, :], rhs=C.P[:, eb, k * 256 + hh * 128:k * 256 + (hh + 1) * 128],
                    start=False, stop=fin), reads=["vm", ("P", eb)], writes=[("O", ob)])
        if last:
            td, tn = 2 * (lt % 2), 2 * (lt % 2) + 1
            S.op("act", lambda e: e.activation(out=C.tmp[:, td, :], in_=C.O[:, ob, 256:384], func=AF.Ln), reads=[("O", ob)], writes=[("tmp", td)])
            S.op("act", lambda e: e.activation(out=C.tmp[:, td, :], in_=C.tmp[:, td, :], func=AF.Exp, scale=-1.0), reads=[("tmp", td)], writes=[("tmp", td)])
            for hh in range(2):
                hs = slice(64 * hh, 64 * hh + 64)
                S.op("dve", lambda e, hs=hs, hh=hh: e.tensor_tensor(out=C.tmp[hs, tn, :], in0=C.O[hs, ob, hh * 128:(hh + 1) * 128],
                                                                 in1=C.tmp[hs, td, :], op=ALU.mult),
                     reads=[("O", ob), ("tmp", td)], writes=[("tmp", tn)])
            S.op(FIN_ENG, lambda e: e.tensor_tensor(out=C.uT[:, lt * 128:(lt + 1) * 128], in0=C.tmp[:, tn, :],
                                                    in1=C.GT[:, lt * 128:(lt + 1) * 128], op=ALU.mult),
                 reads=[("tmp", tn), "GT"], writes=[("uT", lt // 4)])
            if lt % 4 == 3:
                pend_outproj(C, p, lt // 4)

    for u0 in range(min(3, len(units))):
        qk(u0)
    for u in range(len(units)):
        if u + 3 < len(units):
            qk(u + 3)
        pv(u)
        pop_outproj(C, 1)
    flush_outproj(C)


def build_l0(nc, es, npair=NPAIR):
    C = common_setup(nc, es)
    S = C.S
    io = l0_decl(C, nc, W0)
    x1o = nc.dram_tensor("x1T", [D_MODEL, OWN], F32, kind="ExternalOutput").ap()
    h1o = nc.dram_tensor("h1T", [D_MODEL, OWN], BF16, kind="ExternalOutput").ap()
    Tab = S.sbuf("Tab", [128, 1, 2 * NTABBLK * 128], BF16)
    hout = S.sbuf("hout", [128, 2, 512], BF16)
    l0_pass(C, io, Tab, HALO0, 16, True, npair=npair)
    outs = []
    for c in range(NCH):
        outs.append(S.dma("sp", lambda e, c=c: e.dma_start(out=x1o.rearrange("(c p) t -> p c t", p=128)[:, c, :], in_=C.xT[:, c, :]),
                          reads=[("x", c, tb) for tb in range(4)], slot=("xo", c)))
    hov = h1o.rearrange("(c p) t -> p c t", p=128)
    emit_norm_out(C, 1, hout, hov, outs, 4)
    S.final_wait("sp", outs)
    S.emit()
    return C


def emit_norm_out(C, gidx, stage, dst_view, outs, ntb=4, dcol=0):
    k = [0]
    for tb in range(ntb):
        emit_norm_block_out(C, tb, gidx, stage, dst_view, outs, k, dcol)


def emit_norm_block_out(C, tb, gidx, stage, dst_view, outs, k, dcol=0):
    S = C.S
    n = 512
    t0 = tb * 512
    gi = next_g(C)
    gk = ("G", gi)
    for c in range(NCH):
        sb = c % 2
        S.op("act", lambda e, c=c, sb=sb: e.activation(out=C.sq[:, sb, :], in_=C.xT[:, c, t0:t0 + n], func=AF.Square),
             reads=[("x", c, tb)], writes=[("sq", sb)])
        S.op("pe", lambda e, c=c, sb=sb: e.matmul(C.G[:, gi, :], lhsT=C.onesf[:, :], rhs=C.sq[:, sb, :],
                                                  start=(c == 0), stop=(c == NCH - 1)),
             reads=["onesf", ("sq", sb)], writes=[gk])
    rs, rkeys = rstd_buf(C)
    S.op("act", lambda e: e.activation(out=rs[:, :], in_=C.G[:, gi, :], func=AF.Ln, scale=1.0 / D_MODEL, bias=C.epsb[:, 0:1]),
         reads=[gk, "epsb"], writes=rkeys)
    S.op("act", lambda e: e.activation(out=rs[:, :], in_=rs[:, :], func=AF.Exp, scale=-0.5), reads=rkeys, writes=rkeys)
    for c in range(NCH):
        b = k[0] % 2
        k[0] += 1
        S.op("dve", lambda e, c=c, b=b: e.scalar_tensor_tensor(out=stage[:, b, :], in0=C.xT[:, c, t0:t0 + n],
                                                             scalar=C.gam[:, gidx, c:c + 1], in1=rs[:, :],
                                                             op0=ALU.mult, op1=ALU.mult),
             reads=[("x", c, tb), "gam"] + rkeys, writes=[("stage", b)])
        outs.append(S.dma("sp", lambda e, c=c, b=b: e.dma_start(out=dst_view[:, c, dcol + t0:dcol + t0 + n], in_=stage[:, b, :]),
                          reads=[("stage", b)], writes=["hwin"], slot=("so", b)))


def build_fused(nc, es, npair=NPAIR):
    C = common_setup(nc, es)
    S = C.S
    C.big = S.sbuf("big", [128, 2 * OWN], F32)
    io = l0_decl(C, nc, XWF)
    w_in = nc.dram_tensor("w_in1", [NPAIR * 10, 128, 1024], F32, kind="ExternalInput").ap()
    w_out = nc.dram_tensor("w_out1", [NPAIR, 128, 1024], F32, kind="ExternalInput").ap()
    dtab = nc.dram_tensor("dtab", [128, 256], F32, kind="ExternalInput").ap()
    yo = nc.dram_tensor("yT", [D_MODEL, OWN], F32, kind="ExternalOutput").ap()
    hwin = nc.dram_tensor("hwin", [D_MODEL, W1], BF16).ap()
    hv = hwin.rearrange("(c p) t -> p c t", p=128)

    Tab = C.big[:, :].bitcast(BF16)[:, 0:2 * NTABBLK * 128].rearrange("p (a c) -> p a c", a=1)
    hb = S.sbuf("hb", [128, 2, NCH, 256], BF16)
    acc = C.big[:, :].rearrange("p (a t) -> p a t", a=2)
    M1 = S.sbuf("M1", [128, 3, 512], BF16)
    D = S.sbuf("D", [128, 256], BF16)
    stage = acc[:, 0, 0:1024].rearrange("p (a c) -> p a c", a=2)
    S.dma("pool", lambda e: e.dma_start(out=D[:], in_=dtab), writes=["D"], slot="c4")

    hst = hb[:, :, 0:2, :].rearrange("p a b c -> p a (b c)")
    hw_outs = []
    for rs, dcol in ((HALO0, 0), (HALO0 + HALO1 + OWN + 0, HALO1 + OWN)):
        l0_pass(C, io, Tab, rs, 8, False, npair=npair)
        emit_norm_out(C, 1, hst, hv, hw_outs, 2, dcol)
    for b_ in range(2):
        S.readers.setdefault(("hb", b_), []).extend(S.readers.get(("stage", b_), []))
        S.last_w.setdefault(("hb", b_), []).extend(S.last_w.get(("stage", b_), []))
    l0_pass(C, io, Tab, HALO0 + HALO1, 16, True, npair=npair)
    emit_norm(C, lambda c, t0, n: (C.xT[:, c, t0:t0 + n], ("x", c, t0 // 512)), OWN, 1,
              lambda c, t0, n: (C.hT[:, c, t0:t0 + n], "hT"))

    blocks = [w_in[p * 10 + k_] for p in range(npair) for k_ in range(10)]
    issued = [0]
    slots = {}

    def wget(i):
        while issued[0] < min(len(blocks), i + 5):
            slots[issued[0]] = load_wblock(C, blocks[issued[0]])
            issued[0] += 1
        return slots[i]

    hbc = [0]
    l1_body(C, npair, wget, w_out, hv, hb, hbc, acc, M1, D)
    flush_outproj(C)
    outs = []
    emit_norm_out(C, 2, stage, yo.rearrange("(c p) t -> p c t", p=128), outs)
    S.final_wait("sp", outs)
    S.emit()
    return C


def l1_body(C, npair, wget, w_out, hv, hb, hbc, acc, M1, D):
    S = C.S
    for p in range(npair):
        _l1_pair(C, p, wget, w_out, hv, hb, hbc, acc, M1, D)


def _l1_pair(C, p, wget, w_out, hv, hb, hbc, acc, M1, D):
    S = C.S
    S.dma("pool", lambda e: e.dma_start(out=C.Wo[:, p % 2, :], in_=w_out[p]), writes=[("Wo", p % 2)], slot=("Wo", p % 2))
    for g, d in enumerate(DILS):
        for hh in range(2):
            slope = 2.0 ** (-8.0 * (2 * p + hh + 1) / 16.0)
            S.op("act", lambda e, g=g, hh=hh, sc=-slope * d: e.activation(
                out=M1[:, g, :].rearrange("p (kl hh q) -> p kl hh q", kl=2, hh=2)[:, :, hh, :],
                in_=D[:, :].rearrange("p (kl q) -> p kl q", kl=2), func=AF.Exp, scale=sc),
                 reads=["D"], writes=["M1"])
    wg = wget(p * 10)
    for tb in range(4):
        proj_block(C, wg, lambda c, tb=tb: (C.hT[:, c, tb * 512:(tb + 1) * 512], "hT"), 512,
                   evac_silu(C, C.GT[:, tb * 512:(tb + 1) * 512], "GT"))
    for g, d in enumerate(DILS):
        _l1_group(C, p, g, d, wget, hv, hb, hbc, acc, M1)
    flush_outproj(C)
    for tb in range(4):
        ts_ = slice(tb * 512, (tb + 1) * 512)
        keys = [("acc", k) for k in range(tb * 4, tb * 4 + 4)]
        S.op("act", lambda e, ts_=ts_: e.activation(out=acc[:, 1, ts_], in_=acc[:, 1, ts_], func=AF.Ln), reads=keys, writes=keys)
        S.op("act", lambda e, ts_=ts_: e.activation(out=acc[:, 1, ts_], in_=acc[:, 1, ts_], func=AF.Exp, scale=-1.0), reads=keys, writes=keys)
        S.op("dve", lambda e, ts_=ts_: e.tensor_tensor(out=acc[:, 0, ts_], in0=acc[:, 0, ts_], in1=acc[:, 1, ts_], op=ALU.mult),
             reads=keys, writes=keys)
        S.op("dve", lambda e, ts_=ts_: e.tensor_tensor(out=C.uT[:, ts_], in0=acc[:, 0, ts_], in1=C.GT[:, ts_], op=ALU.mult),
             reads=keys + ["GT"], writes=[("uT", tb)])
        pend_outproj(C, p, tb)


def _l1_group(C, p, g, d, wget, hv, hb, hbc, acc, M1):
    S = C.S
    halo = 64 * d
    base = HALO1 - halo
    wq, wk, wv = wget(p * 10 + 1 + g * 3), wget(p * 10 + 2 + g * 3), wget(p * 10 + 3 + g * 3)
    hn = min(halo, 256)
    hblks = [(True, w0, hn) for w0 in range(base, HALO1, hn)] + \
            [(True, w0, hn) for w0 in range(HALO1 + OWN, HALO1 + OWN + halo, hn)]
    oblks = [(False, HALO1 + tb * 512, 512) for tb in range(4)]
    blks = []
    per = -(-len(hblks) // 4)
    for i in range(4):
        blks.extend(hblks[i * per:(i + 1) * per])
        blks.append(oblks[i])
    for (is_h, w0, n) in blks:
        col = w0 - base
        if is_h:
            hs_ = hbc[0] % 2
            hbc[0] += 1
            S.dma("sp", lambda e, hs_=hs_, w0=w0, n=n: e.dma_start(out=hb[:, hs_, :, 0:n], in_=hv[:, :, w0:w0 + n]),
                  reads=["hwin"], writes=[("hb", hs_)], slot=("hb", hs_))
            h_of = lambda c, hs_=hs_, n=n: (hb[:, hs_, c, 0:n], ("hb", hs_))
        else:
            t0 = w0 - HALO1
            h_of = lambda c, t0=t0, n=n: (C.hT[:, c, t0:t0 + n], "hT")
        proj_block(C, wk, h_of, n, evac_copy_act(C, C.KT[:, col:col + n], "KT"))
        if is_h:
            side = 0 if w0 < HALO1 else 1
            proj_block(C, wv, h_of, n, evac_scaled_dve(C, C.VT[:, col:col + n], "VT", C.vcol[:, 2 + side:3 + side]))
        else:
            proj_block(C, wv, h_of, n, evac_copy_dve(C, C.VT[:, col:col + n], "VT"))
    nq = OWN // (128 * d)
    nkt = nq + 1
    tiles = []
    for r in range(d):
        for kt in range(nkt):
            c0 = 128 * kt * d + r
            edge = None
            tiles.append((r * nkt + kt, C.VT[:, c0:c0 + 127 * d + 1:d], edge))
    emit_vtrans(C, tiles)
    for tb in range(4):
        proj_block(C, wq, lambda c, tb=tb: (C.hT[:, c, tb * 512:(tb + 1) * 512], "hT"), 512,
                   evac_q(C, tb * 512, 512))
    units = [(r, qt) for r in range(d) for qt in range(nq)]

    def qk(u):
        r, qt = units[u]
        sb, eb = u % 4, u % 4
        q0 = 128 * qt * d + r
        for kl in range(2):
            c0 = 128 * (qt + kl) * d + r
            S.op("pe", lambda e, kl=kl, c0=c0: e.matmul(
                C.Sp[:, sb, kl * 256:(kl + 1) * 256].rearrange("p (a c) -> p a c", a=2),
                lhsT=C.KT[:, c0:c0 + 127 * d + 1:d], rhs=C.QT[:, :, q0:q0 + 127 * d + 1:d], start=True, stop=True),
                reads=["KT", "QT"], writes=[("S", sb)])
        S.op("act", lambda e: e.activation(out=C.E[:, eb, 0:512], in_=C.Sp[:, sb, 0:512], func=AF.Exp, scale=0.125),
             reads=[("S", sb)], writes=[("E", eb)])
        S.op(MASK_ENG if u % 2 == 1 else "dve", lambda e: e.tensor_tensor(out=C.P[:, eb, 0:512], in0=C.E[:, eb, 0:512], in1=M1[:, g, :], op=ALU.mult),
             reads=[("E", eb), "M1"], writes=[("P", eb)])

    def pv(u):
        r, qt = units[u]
        sb = u % 4
        ob = u % 2
        q0 = 128 * qt * d + r
        for kl in range(2):
            kt = qt + kl
            ti = r * nkt + kt
            vsel = 1 if kt == 0 else (2 if kt == nkt - 1 else 0)
            S.op("pe", lambda e, kl=kl, ti=ti: e.matmul(
                C.O[:, ob, 0:256], lhsT=C.V[:, ti, :], rhs=C.P[:, sb, kl * 256:(kl + 1) * 256],
                start=(kl == 0), stop=False), reads=["V", ("P", sb)], writes=[("O", ob)])
            S.op("pe", lambda e, kl=kl, vsel=vsel: e.matmul(
                C.O[:, ob, 256:512], lhsT=C.vm[:, vsel, :], rhs=C.P[:, sb, kl * 256:(kl + 1) * 256],
                start=False, stop=(kl == 1)), reads=["vm", ("P", sb)], writes=[("O", ob)])
        blk_lo, blk_hi = q0 // 128, (q0 + 127 * d) // 128
        keys = [("acc", k) for k in range(blk_lo, blk_hi + 1)]
        for hh in range(2):
            hs = slice(64 * hh, 64 * hh + 64)
            dst = acc[hs, :, q0:q0 + 127 * d + 1:d]
            src = C.O[hs, ob, :].rearrange("p (a b c) -> p a b c", a=2, b=2)[:, :, hh, :]
            if g == 0:
                S.op("dve", lambda e, dst=dst, src=src: e.tensor_copy(out=dst, in_=src), reads=[("O", ob)], writes=keys)
            else:
                S.op("dve", lambda e, dst=dst, src=src: e.tensor_tensor(out=dst, in0=src, in1=dst, op=ALU.add),
                     reads=[("O", ob)] + keys, writes=keys)

    for u0 in range(min(3, len(units))):
        qk(u0)
    for u in range(len(units)):
        if u + 3 < len(units):
            qk(u + 3)
        pv(u)
        pop_outproj(C, 1)


def wblock(w, col0):
    return np.ascontiguousarray(w[:, col0:col0 + 128].reshape(NCH, 128, 128).transpose(1, 0, 2).reshape(128, 1024))


def gam_layout(*gs):
    return np.ascontiguousarray(np.concatenate([g.reshape(NCH, 128).T for g in gs], axis=1)).astype(np.float32)


def make_cst(j):
    cst = np.zeros((128, CSTW), np.float32)
    cst[:, 0:128] = np.eye(128, dtype=np.float32)
    vl = np.ones(128, np.float32)
    vr = np.ones(128, np.float32)
    if j == 0:
        vl[:64] = 0.0
    if j == 3:
        vr[64:] = 0.0
    cst[:, 128:256] = 1.0
    cst[:, 256:384] = vl[:, None]
    cst[:, 384:512] = vr[:, None]
    cst[:, 512:576] = 1.0
    cst[:, 704:768] = 1.0
    cst[:, 768] = vl
    cst[:, 769] = vr
    cst[:, 770] = 0.0 if j == 0 else 1.0
    cst[:, 771] = 0.0 if j == 3 else 1.0
    return cst


def make_tab(rpb, j):
    kp = np.arange(128)
    kr2, kc = kp // 64, kp % 64
    q = np.arange(128)
    qr2, qc = q // 64, q % 64
    cstart = np.clip(qc - 8, 0, 64 - 16)
    colv = (kc[:, None] >= cstart[None, :]) & (kc[:, None] < cstart[None, :] + 16)
    coff = np.clip(kc[:, None] - qc[None, :] + 15, 0, 30)
    out = np.full((16, NTABBLK, 128, 128), NEG, np.float32)
    for key, (toff, offs) in TAB_OFFS.items():
        lt = 5 if key == "G" else key
        m = 16 * j + lt
        r = 2 * m + qr2
        rs = np.clip(r - 4, 0, 128 - 8)
        for k, o in enumerate(offs):
            krow = 2 * (m + o) + kr2
            rowv = (krow[:, None] >= rs[None, :]) & (krow[:, None] < rs[None, :] + 8) & (krow[:, None] >= 0) & (krow[:, None] < 128)
            valid = rowv & colv
            roff = np.clip(krow[:, None] - r[None, :] + 7, 0, 14)
            vals = rpb[:, roff, coff]
            out[:, toff + k] = np.where(valid[None], vals, np.float32(NEG))
    out = out.reshape(NPAIR, 2, NTABBLK, 128, 128).transpose(0, 3, 2, 1, 4).reshape(NPAIR, 128, 2 * NTABBLK * 128)
    return np.ascontiguousarray(out)


def window_T(xb, t0, halo):
    out = np.zeros((xb.shape[1], OWN + 2 * halo), xb.dtype)
    lo, hi = t0 - halo, t0 + OWN + halo
    a, b = max(lo, 0), min(hi, SEQ)
    out[:, a - lo:b - lo] = xb[a:b].T
    return out


_CACHE = {}


def get_prog(name, builder):
    if name not in _CACHE:
        nc = bass.Bass("TRN2", target_bir_lowering=False)
        es = ExitStack()
        builder(nc, es)
        _CACHE[name] = (nc, es)
    return _CACHE[name][0]


def run_l0(x, norm_0, w_in_0, rpb_0, w_out_0, norm_1, norm_f, trace=False):
    nc = get_prog("l0", build_l0)
    w_in_b = np.stack([wblock(w_in_0, k * 1024 + p * 128) for p in range(NPAIR) for k in range(4)])
    w_out_b = np.ascontiguousarray(w_out_0.reshape(NPAIR, 128, 1024))
    gam = gam_layout(norm_0, norm_1, norm_f)
    tabs = [make_tab(rpb_0, j) for j in range(4)]
    csts = [make_cst(j) for j in range(4)]
    in_maps = []
    for c in range(NCORE):
        b, j = divmod(c, 4)
        in_maps.append({"xw": window_T(x[b], j * OWN, HALO0), "gamd": gam, "w_in0": w_in_b, "w_out0": w_out_b,
                        "tab0": tabs[j], "cst": csts[j]})
    res = run_bass_kernel_spmd(nc, in_maps, core_ids=list(range(NCORE)), trace=trace)
    return res


def make_dtab():
    kp = np.arange(128)[:, None]
    j = np.arange(128)[None, :]
    out = np.empty((128, 256), np.float32)
    for kl, sh in enumerate((-64, 64)):
        delta = np.abs(kp + sh - j)
        out[:, kl * 128:(kl + 1) * 128] = np.where(delta <= 64, delta, 1.0e6)
    return out


def run_l1(x1T_list, h1T_list, w_in_1, w_out_1, norm_0, norm_1, norm_f, trace=False):
    nc = get_prog("l1", build_l1)
    cols = []
    for p in range(NPAIR):
        cols.append(9216 + p * 128)
        for g in range(3):
            for k in range(3):
                cols.append(g * 3072 + k * 1024 + p * 128)
    w_in_b = np.stack([wblock(w_in_1, c0) for c0 in cols])
    w_out_b = np.ascontiguousarray(w_out_1.reshape(NPAIR, 128, 1024))
    gam = gam_layout(norm_0, norm_1, norm_f)
    dt = make_dtab()
    csts = [make_cst(j) for j in range(4)]
    in_maps = []
    for c in range(NCORE):
        b, j = divmod(c, 4)
        hw = np.zeros((D_MODEL, W1), h1T_list[c].dtype)
        hw[:, HALO1:HALO1 + OWN] = h1T_list[c]
        if j > 0:
            hw[:, 0:HALO1] = h1T_list[c - 1][:, OWN - HALO1:]
        if j < 3:
            hw[:, HALO1 + OWN:] = h1T_list[c + 1][:, 0:HALO1]
        in_maps.append({"x1w": x1T_list[c], "h1w": hw, "gamd": gam, "w_in1": w_in_b, "w_out1": w_out_b,
                        "cst": csts[j], "dtab": dt})
    return run_bass_kernel_spmd(nc, in_maps, core_ids=list(range(NCORE)), trace=trace)


def kernel(x, norm_0, w_in_0, rpb_0, w_out_0, norm_1, w_in_1, w_out_1, norm_f):
    f = lambda a: np.ascontiguousarray(np.asarray(a, dtype=np.float32))
    x, norm_0, w_in_0, rpb_0, w_out_0, norm_1, w_in_1, w_out_1, norm_f = map(
        f, (x, norm_0, w_in_0, rpb_0, w_out_0, norm_1, w_in_1, w_out_1, norm_f))
    nc, in_maps = run_fused(x, norm_0, w_in_0, rpb_0, w_out_0, norm_1, w_in_1, w_out_1, norm_f)
    res = run_bass_kernel_spmd(nc, in_maps, core_ids=list(range(NCORE)))
    out = np.empty((2, SEQ, D_MODEL), np.float32)
    for c in range(NCORE):
        b, j = divmod(c, 4)
        out[b, j * OWN:(j + 1) * OWN, :] = np.asarray(res.results[c]["yT"]).T
    return out


def make_hidx(j):
    l = j - 1 if j > 0 else j
    r = j + 1 if j < 3 else j
    p = np.arange(128, dtype=np.int64)
    out = np.zeros((128, NHIDX), np.uint32)
    for side, nbr in enumerate((l, r)):
        for c in range(NCH):
            for blk in range(4):
                tb = 4 + blk if side == 0 else blk
                out[:, side * 32 + c * 4 + blk] = (nbr * D_MODEL + c * 128 + p) * 8 + tb
    return out


def l1_cols():
    cols = []
    for p in range(NPAIR):
        cols.append(9216 + p * 128)
        for g in range(3):
            for k in range(3):
                cols.append(g * 3072 + k * 1024 + p * 128)
    return cols


def run_fused(x, norm_0, w_in_0, rpb_0, w_out_0, norm_1, w_in_1, w_out_1, norm_f, trace=False):
    nc = get_prog("fused", build_fused)
    w_in0_b = np.stack([wblock(w_in_0, k * 1024 + p * 128) for p in range(NPAIR) for k in range(4)])
    w_out0_b = np.ascontiguousarray(w_out_0.reshape(NPAIR, 128, 1024))
    w_in1_b = np.stack([wblock(w_in_1, c0) for c0 in l1_cols()])
    w_out1_b = np.ascontiguousarray(w_out_1.reshape(NPAIR, 128, 1024))
    gam = gam_layout(norm_0, norm_1, norm_f)
    dt = make_dtab()
    tabs = [make_tab(rpb_0, j) for j in range(4)]
    csts = [make_cst(j) for j in range(4)]
    hidx = [make_hidx(j) for j in range(4)]
    in_maps = []
    for c in range(NCORE):
        b, j = divmod(c, 4)
        in_maps.append({"xw": window_T(x[b], j * OWN, HALO0 + HALO1), "gamd": gam, "w_in0": w_in0_b, "w_out0": w_out0_b,
                        "tab0": tabs[j], "cst": csts[j], "w_in1": w_in1_b, "w_out1": w_out1_b, "dtab": dt})
    return nc, in_maps
```

```python
import numpy as np
from contextlib import ExitStack
import concourse.bass as bass
import concourse.mybir as mybir
from concourse.bass_utils import run_bass_kernel_spmd

F32 = mybir.dt.float32
BF16 = mybir.dt.bfloat16
AF = mybir.ActivationFunctionType
ALU = mybir.AluOpType


class Op:
    __slots__ = ("eng", "fn", "deps", "is_dma", "slot", "ord", "marked", "name")

    def __init__(self, eng, fn, is_dma=False, slot=None, name=""):
        self.eng = eng
        self.fn = fn
        self.deps = []
        self.is_dma = is_dma
        self.slot = slot
        self.ord = None
        self.marked = False
        self.name = name


class Sched:
    ENGS = ("pe", "act", "dve", "pool", "sp")

    def __init__(self, nc, es):
        self.nc = nc
        self.es = es
        self.q = {e: [] for e in self.ENGS}
        self.last_w = {}
        self.readers = {}
        self.finals = []
        self.slot_count = {}
        self.n_sb = 0

    def sbuf(self, name, shape, dtype):
        return self.es.enter_context(self.nc.sbuf_tensor(name, shape, dtype))

    def psum(self, name, shape, dtype):
        return self.es.enter_context(self.nc.psum_tensor(name, shape, dtype))

    def _add(self, op, reads, writes, deps):
        dl = []
        for r in reads:
            dl.extend(self.last_w.get(r, ()))
            self.readers.setdefault(r, []).append(op)
        for w_ in writes:
            prev = self.last_w.get(w_, [])
            dl.extend(prev)
            for rd in self.readers.get(w_, ()):
                if rd is not op:
                    dl.append(rd)
            if op.is_dma and prev and all(q.is_dma for q in prev):
                self.last_w[w_] = prev + [op]
            else:
                self.last_w[w_] = [op]
            self.readers[w_] = []
        dl.extend(deps)
        seen = set()
        for d in dl:
            if d is op or id(d) in seen:
                continue
            seen.add(id(d))
            if d.eng == "pe" and op.eng == "pe" and not d.is_dma and not op.is_dma:
                continue
            op.deps.append(d)
        self.q[op.eng].append(op)
        return op

    def op(self, eng, fn, reads=(), writes=(), deps=(), name=""):
        return self._add(Op(eng, fn, name=name), reads, writes, deps)

    def dma(self, eng, fn, reads=(), writes=(), deps=(), slot=None, name=""):
        if slot is None:
            slot = ("auto", len(self.slot_count))
        op = Op(eng, fn, is_dma=True, slot=slot, name=name)
        k = self.slot_count.get(slot, 0) + 1
        self.slot_count[slot] = k
        op.ord = k
        return self._add(op, reads, writes, deps)

    def final_wait(self, eng, ops):
        self.finals.append((eng, list(ops)))

    def emit(self):
        nc = self.nc
        for e in self.ENGS:
            for op in self.q[e]:
                for d in op.deps:
                    d.marked = True
        for _, ops in self.finals:
            for d in ops:
                d.marked = True
        for e in self.ENGS:
            for op in self.q[e]:
                if op.is_dma:
                    op.marked = True
        for e in self.ENGS:
            k = 0
            for op in self.q[e]:
                if not op.is_dma and op.marked:
                    k += 1
                    op.ord = k
        sems = {}

        def sem_of(op):
            key = ("dma", op.slot) if op.is_dma else ("eng", op.eng)
            if key not in sems:
                sems[key] = self.es.enter_context(nc.semaphore("s%d" % len(sems)))
            return sems[key]

        def val_of(op):
            return op.ord * 16 if op.is_dma else op.ord

        for e in self.ENGS:
            for op in self.q[e]:
                if op.marked:
                    sem_of(op)
        finals = {}
        for eng, ops in self.finals:
            finals.setdefault(eng, []).extend(ops)
        self.n_waits = 0

        def run(engname, engine):
            known = {}

            def need(d):
                s = sem_of(d)
                v = val_of(d)
                if known.get(id(s), 0) >= v:
                    return
                known[id(s)] = v
                engine.wait_ge(s, v)
                self.n_waits += 1

            for op in self.q[engname]:
                best = {}
                for d in op.deps:
                    s = sem_of(d)
                    if id(s) not in best or val_of(best[id(s)]) < val_of(d):
                        best[id(s)] = d
                for d in best.values():
                    need(d)
                ins = op.fn(engine)
                if op.marked:
                    ins.then_inc(sem_of(op), 16 if op.is_dma else 1)
            for d in finals.get(engname, ()):
                need(d)

        with nc.Block() as block:
            @block.sync
            def _(eng):
                run("sp", eng)

            @block.tensor
            def _(eng):
                run("pe", eng)

            @block.scalar
            def _(eng):
                run("act", eng)

            @block.vector
            def _(eng):
                run("dve", eng)

            @block.gpsimd
            def _(eng):
                run("pool", eng)


D_MODEL = 1024
SEQ = 8192
NCORE = 8
OWN = 2048
NCH = 8
NPAIR = 8
HALO0 = 256
W0 = OWN + 2 * HALO0
HALO1 = 1024
W1 = OWN + 2 * HALO1
DILS = (1, 4, 16)
NEG = -30000.0
EPS = 1e-6
TAB_OFFS = {"G": (0, (-2, -1, 0, 1, 2)), 0: (5, (-2, -1, 0, 1, 2, 3)), 1: (11, (-2, -1, 0, 1, 2)),
            14: (16, (-2, -1, 0, 1, 2)), 15: (21, (-3, -2, -1, 0, 1, 2))}
NTABBLK = 27
U32 = mybir.dt.uint32
NHIDX = 64
MASK_ENG = "pool"
FIN_ENG = "pool"
CSTW = 128 + 5 * 128 + 4


def tab_for(lt):
    return TAB_OFFS[lt] if lt in TAB_OFFS else TAB_OFFS["G"]


class Ctx:
    pass


def common_setup(nc, es, hcols=W0):
    C = Ctx()
    S = Sched(nc, es)
    C.S = S
    C.nc = nc
    C.G = S.psum("G", [128, 2, 512], F32)
    C.Sp = S.psum("Sp", [128, 4, 512], F32)
    C.O = S.psum("O", [128, 2, 512], F32)
    C.gi = 0
    C.xT = S.sbuf("xT", [128, NCH, OWN], F32)
    C.hT = S.sbuf("hT", [128, NCH, hcols], BF16)
    C.QT = S.sbuf("QT", [128, 2, OWN], BF16)
    C.KV = S.sbuf("KV", [128, 2 * W1], BF16)
    C.KT = C.KV[:, 0:W1]
    C.VT = C.KV[:, W1:2 * W1]
    C.V = S.sbuf("V", [128, 32, 128], BF16)
    C.GT = S.sbuf("GT", [128, OWN], BF16)
    C.uT = S.sbuf("uT", [128, OWN], BF16)
    C.Wb = S.sbuf("Wb", [128, 8, 1024], BF16)
    C.Wo = S.sbuf("Wo", [128, 2, 1024], BF16)
    C.E = S.sbuf("E", [128, 4, 512], BF16)
    C.P = S.sbuf("P", [128, 4, 512], BF16)
    C.sq = S.sbuf("sq", [128, 2, 512], BF16)
    C.rstd = S.sbuf("rstd", [128, 512], F32)
    C.tmp = S.sbuf("tmp", [128, 4, 128], F32)
    C.onesf = S.sbuf("onesf", [128, 128], BF16)
    C.vm = S.sbuf("vm", [128, 5, 128], BF16)
    C.vcol = S.sbuf("vcol", [128, 4], F32)
    C.gam = S.sbuf("gam", [128, 3, NCH], F32)
    C.wslot = 0
    C.pend = []
    C.pops = 2
    C.nrm = 0
    C.epsb = S.sbuf("epsb", [128, 1], F32)
    C.ident = S.sbuf("ident", [128, 128], BF16)
    S.op("dve", lambda e: e.memset(C.epsb[:], EPS), writes=["epsb"])
    S.op("dve", lambda e: e.memset(C.onesf[:], 1.0), writes=["onesf"])
    S.op("pool", lambda e: e.memset(C.QT[:], 0.0), writes=["QT"])
    return C


def next_g(C):
    gi = C.gi
    C.gi = (gi + 1) % 2
    return gi


def load_wblock(C, src_ap):
    S = C.S
    s = C.wslot
    C.wslot = (s + 1) % 8
    S.dma("pool", lambda e: e.dma_start(out=C.Wb[:, s, :], in_=src_ap), writes=[("W", s)], slot=("W", s))
    return s


def proj_block(C, ws, h_of, n, evac):
    S = C.S
    gi = next_g(C)
    gk = ("G", gi)
    for c in range(NCH):
        rhs, rk = h_of(c)
        S.op("pe", lambda e, c=c, rhs=rhs: e.matmul(C.G[:, gi, 0:n], lhsT=C.Wb[:, ws, c * 128:(c + 1) * 128],
                                                    rhs=rhs, start=(c == 0), stop=(c == NCH - 1)),
             reads=[("W", ws), rk], writes=[gk])
    evac(C.G[:, gi, 0:n], gk)
    pop_outproj(C, C.pops)


def emit_norm(C, x_of, ntok, gidx, out_of):
    t0 = 0
    while t0 < ntok:
        n = min(512, ntok - t0)
        _norm_block(C, x_of, t0, n, gidx, out_of)
        t0 += n


def rstd_buf(C):
    k = C.nrm % 2
    C.nrm += 1
    if k == 0:
        return C.rstd[:, :], [("rstd", 0)]
    return C.tmp[:, :, :].rearrange("p a c -> p (a c)"), [("rstd", 1)] + [("tmp", i) for i in range(4)]


def _norm_block(C, x_of, t0, n, gidx, out_of):
    S = C.S
    gi = next_g(C)
    gk = ("G", gi)
    rs, rkeys = rstd_buf(C)
    for c in range(NCH):
        xa, xk = x_of(c, t0, n)
        sb = c % 2
        S.op("act", lambda e, xa=xa, sb=sb: e.activation(out=C.sq[:, sb, 0:n], in_=xa, func=AF.Square),
             reads=[xk], writes=[("sq", sb)])
        S.op("pe", lambda e, c=c, sb=sb: e.matmul(C.G[:, gi, 0:n], lhsT=C.onesf[:, :], rhs=C.sq[:, sb, 0:n],
                                                  start=(c == 0), stop=(c == NCH - 1)),
             reads=["onesf", ("sq", sb)], writes=[gk])
    S.op("act", lambda e: e.activation(out=rs[:, 0:n], in_=C.G[:, gi, 0:n], func=AF.Ln,
                                       scale=1.0 / D_MODEL, bias=C.epsb[:, 0:1]),
         reads=[gk, "epsb"], writes=rkeys)
    S.op("act", lambda e: e.activation(out=rs[:, 0:n], in_=rs[:, 0:n], func=AF.Exp, scale=-0.5), reads=rkeys, writes=rkeys)
    for c in range(NCH):
        xa, xk = x_of(c, t0, n)
        oa, ok = out_of(c, t0, n)
        if False:
            S.op("pool", lambda e, xa=xa, oa=oa, c=c: e.scalar_tensor_tensor(
                out=oa, in0=xa, scalar=C.gam[:, gidx, c:c + 1], in1=rs[:, 0:n], op0=ALU.mult, op1=ALU.mult),
                 reads=[xk, "gam"] + rkeys, writes=[ok])
        else:
            S.op("dve", lambda e, xa=xa, oa=oa, c=c: e.scalar_tensor_tensor(
                out=oa, in0=xa, scalar=C.gam[:, gidx, c:c + 1], in1=rs[:, 0:n], op0=ALU.mult, op1=ALU.mult),
                 reads=[xk, "gam"] + rkeys, writes=[ok])


def emit_outproj(C, pair, wo_src, tbs, ms=None):
    S = C.S
    for tb in tbs:
        for m in (range(NCH) if ms is None else ms):
            gi = next_g(C)
            gk = ("G", gi)
            S.op("pe", lambda e, m=m, gi=gi, tb=tb: e.matmul(C.G[:, gi, :], lhsT=C.Wo[:, pair % 2, m * 128:(m + 1) * 128],
                                                          rhs=C.uT[:, tb * 512:(tb + 1) * 512], start=True, stop=True),
                 reads=[("Wo", pair % 2), ("uT", tb)], writes=[gk])
            S.op("dve", lambda e, m=m, gi=gi, tb=tb: e.tensor_tensor(out=C.xT[:, m, tb * 512:(tb + 1) * 512],
                                                                 in0=C.G[:, gi, :], in1=C.xT[:, m, tb * 512:(tb + 1) * 512],
                                                                 op=ALU.add),
                 reads=[gk, ("x", m, tb)], writes=[("x", m, tb)])


def pend_outproj(C, pair, tb):
    for m in range(NCH):
        C.pend.append((pair, tb, m))


def pop_outproj(C, n=1):
    for _ in range(n):
        if C.pend:
            pair, tb, m = C.pend.pop(0)
            emit_outproj(C, pair, None, [tb], [m])


def flush_outproj(C):
    pop_outproj(C, len(C.pend))


def evac_copy_act(C, dst, dkey):
    def f(ps, gk):
        C.S.op("act", lambda e: e.activation(out=dst, in_=ps, func=AF.Copy), reads=[gk], writes=[dkey])
    return f


def evac_copy_dve(C, dst, dkey):
    def f(ps, gk):
        C.S.op("dve", lambda e: e.tensor_copy(out=dst, in_=ps), reads=[gk], writes=[dkey])
    return f


def evac_q(C, c0, n):
    def f(ps, gk):
        C.S.op("act", lambda e: e.activation(out=C.QT[0:64, 0, c0:c0 + n], in_=ps[0:64, :], func=AF.Copy), reads=[gk], writes=["QT"])
        C.S.op("act", lambda e: e.activation(out=C.QT[64:128, 1, c0:c0 + n], in_=ps[64:128, :], func=AF.Copy), reads=[gk], writes=["QT"])
    return f


def evac_scaled_dve(C, dst, dkey, flag_ap):
    def f(ps, gk):
        C.S.op("dve", lambda e: e.tensor_scalar(out=dst, in0=ps, scalar1=flag_ap, scalar2=None, op0=ALU.mult), reads=[gk, "vcol"], writes=[dkey])
    return f


def evac_silu(C, dst, dkey):
    def f(ps, gk):
        C.S.op("act", lambda e: e.activation(out=dst, in_=ps, func=AF.Silu), reads=[gk], writes=[dkey])
    return f


def emit_vtrans(C, tiles):
    for i in range(0, len(tiles), 4):
        _vtrans_group(C, tiles[i:i + 4])


def _vtrans_group(C, grp):
    S = C.S
    gi = next_g(C)
    gk = ("G", gi)
    gb = C.G[:, gi, :].bitcast(BF16)
    n = len(grp)
    ti0 = grp[0][0]
    assert [t[0] for t in grp] == list(range(ti0, ti0 + n))
    for j, (ti, src, edge) in enumerate(grp):
        S.op("pe", lambda e, j=j, src=src: e.transpose(gb[:, j * 128:(j + 1) * 128], src, C.ident[:, :]),
             reads=["VT", "ident"], writes=[gk])
    S.op("act", lambda e: e.activation(out=C.V[:, ti0:ti0 + n, :].rearrange("p a c -> p (a c)"), in_=gb[:, 0:n * 128], func=AF.Copy),
         reads=[gk], writes=["V"])


XWF = OWN + 2 * (HALO1 + HALO0)


def l0_decl(C, nc, xwin):
    S = C.S
    io = Ctx()
    io.xw = nc.dram_tensor("xw", [D_MODEL, xwin], F32, kind="ExternalInput").ap()
    io.gam = nc.dram_tensor("gamd", [128, 3 * NCH], F32, kind="ExternalInput").ap()
    io.w_in = nc.dram_tensor("w_in0", [NPAIR * 4, 128, 1024], F32, kind="ExternalInput").ap()
    io.w_out = nc.dram_tensor("w_out0", [NPAIR, 128, 1024], F32, kind="ExternalInput").ap()
    io.tab = nc.dram_tensor("tab0", [NPAIR, 128, 2 * NTABBLK * 128], F32, kind="ExternalInput").ap()
    io.cst = nc.dram_tensor("cst", [128, CSTW], F32, kind="ExternalInput").ap()
    io.xv = io.xw.rearrange("(c p) t -> p c t", p=128)
    io.xh = C.KV[:, :].bitcast(F32).rearrange("p (c t) -> p c t", c=NCH)
    cst = io.cst
    S.dma("sp", lambda e: e.dma_start(out=C.gam[:].rearrange("p a c -> p (a c)"), in_=io.gam), writes=["gam"], slot="c0")
    S.dma("pool", lambda e: e.dma_start(out=C.ident[:], in_=cst[:, 0:128]), writes=["ident"], slot="c1")
    S.dma("pool", lambda e: e.dma_start(out=C.vm[:].rearrange("p a c -> p (a c)"), in_=cst[:, 128:768]), writes=["vm"], slot="c2")
    S.dma("sp", lambda e: e.dma_start(out=C.vcol[:], in_=cst[:, 768:772]), writes=["vcol"], slot="c3")
    return io


def l0_pass(C, io, Tab, rs, NT, special, npair=NPAIR):
    S = C.S
    xv, xh = io.xv, io.xh
    NQ = NT * 128
    WIN = NQ + 2 * HALO0
    NTB = NQ // 512
    for c in range(NCH):
        S.dma("sp", lambda e, c=c: e.dma_start(out=C.xT[:, c, 0:NQ], in_=xv[:, c, rs:rs + NQ]),
              writes=[("x", c, tb) for tb in range(NTB)], slot=("xl", c))
    S.dma("sp", lambda e: e.dma_start(out=xh[:, :, 0:HALO0], in_=xv[:, :, rs - HALO0:rs]), writes=["xh", "KT", "VT"], slot="xh0")
    S.dma("sp", lambda e: e.dma_start(out=xh[:, :, HALO0:2 * HALO0], in_=xv[:, :, rs + NQ:rs + NQ + HALO0]),
          writes=["xh", "KT", "VT"], slot="xh1")

    wslots = {}

    def prefetch(p):
        if p >= npair:
            return
        wslots[p] = [load_wblock(C, io.w_in[p * 4 + k]) for k in range(4)]

    def load_wo(p):
        S.dma("pool", lambda e: e.dma_start(out=C.Wo[:, p % 2, :], in_=io.w_out[p]), writes=[("Wo", p % 2)], slot=("Wo", p % 2))

    def prefetch_tab(p):
        S.dma("pool", lambda e: e.dma_start(out=Tab[:, 0, :], in_=io.tab[p]), writes=[("Tab", k_) for k_ in TAB_OFFS], slot=("Tab", 0))

    prefetch(0)

    emit_norm(C, lambda c, t0, n: (C.xT[:, c, t0:t0 + n], ("x", c, t0 // 512)), NQ, 0,
              lambda c, t0, n: (C.hT[:, c, HALO0 + t0:HALO0 + t0 + n], "hT"))
    for half in range(2):
        emit_norm(C, lambda c, t0, n, half=half: (xh[:, c, half * HALO0 + t0:half * HALO0 + t0 + n], "xh"), HALO0, 0,
                  lambda c, t0, n, half=half: (C.hT[:, c, half * (HALO0 + NQ) + t0:half * (HALO0 + NQ) + t0 + n], "hT"))

    hkey = "hT"
    for p in range(npair):
        prefetch(p + 1)
        load_wo(p)
        prefetch_tab(p)
        _l0_pair(C, p, wslots[p], Tab, NT, WIN, NTB, special)
    flush_outproj(C)


def _l0_pair(C, p, ws, Tab, NT, WIN, NTB, special):
    S = C.S
    hkey = "hT"
    wq, wk, wv, wg = ws
    for tb in range(WIN // 512):
        proj_block(C, wk, lambda c, tb=tb: (C.hT[:, c, tb * 512:(tb + 1) * 512], hkey), 512,
                   evac_copy_act(C, C.KT[:, tb * 512:(tb + 1) * 512], "KT"))
    for tb in range(WIN // 512):
        proj_block(C, wv, lambda c, tb=tb: (C.hT[:, c, tb * 512:(tb + 1) * 512], hkey), 512,
                   evac_copy_dve(C, C.VT[:, tb * 512:(tb + 1) * 512], "VT"))
    emit_vtrans(C, [(wt, C.VT[:, wt * 128:(wt + 1) * 128], None) for wt in range(WIN // 128)])
    for tb in range(NTB):
        proj_block(C, wq, lambda c, tb=tb: (C.hT[:, c, HALO0 + tb * 512:HALO0 + (tb + 1) * 512], hkey), 512,
                   evac_q(C, tb * 512, 512))
    for tb in range(NTB):
        proj_block(C, wg, lambda c, tb=tb: (C.hT[:, c, HALO0 + tb * 512:HALO0 + (tb + 1) * 512], hkey), 512,
                   evac_silu(C, C.GT[:, tb * 512:(tb + 1) * 512], "GT"))
    for key in ((0, 1, "G", 14, 15) if special else ("G",)):
        toff_, offs_ = TAB_OFFS[key]
        a_, b_ = toff_ * 256, (toff_ + len(offs_)) * 256
        S.op("act", lambda e, a_=a_, b_=b_: e.activation(out=Tab[:, 0, a_:b_], in_=Tab[:, 0, a_:b_], func=AF.Exp),
             reads=[("Tab", key)], writes=[("Tab", key)])
    tabf = (lambda lt: tab_for(lt)) if special else (lambda lt: TAB_OFFS["G"])
    units = []
    for lt in range(NT):
        toff, offs = tabf(lt)
        for i in range(0, len(offs), 2):
            units.append((lt, toff, offs[i:i + 2], i, i == 0, i + 2 >= len(offs)))
    tkey = (lambda lt: lt if lt in TAB_OFFS else "G") if special else (lambda lt: "G")

    def qk(u):
        lt, toff, offs, kbase, first, last = units[u]
        sb, eb = u % 4, u % 4
        n = len(offs)
        for k, o in enumerate(offs):
            wt = lt + 2 + o
            S.op("pe", lambda e, wt=wt, k=k: e.matmul(
                C.Sp[:, sb, k * 256:(k + 1) * 256].rearrange("p (a c) -> p a c", a=2), lhsT=C.KT[:, wt * 128:(wt + 1) * 128],
                rhs=C.QT[:, :, lt * 128:(lt + 1) * 128], start=True, stop=True),
                reads=["KT", "QT"], writes=[("S", sb)])
        S.op("act", lambda e: e.activation(out=C.E[:, eb, 0:n * 256], in_=C.Sp[:, sb, 0:n * 256], func=AF.Exp, scale=0.125),
             reads=[("S", sb)], writes=[("E", eb)])
        t0 = (toff + kbase) * 256
        S.op(MASK_ENG if u % 3 == 2 else "dve", lambda e: e.tensor_tensor(out=C.P[:, eb, 0:n * 256], in0=C.E[:, eb, 0:n * 256],
                                                 in1=Tab[:, 0, t0:t0 + n * 256], op=ALU.mult),
             reads=[("E", eb), ("Tab", tkey(lt))], writes=[("P", eb)])

    def pv(u):
        lt, toff, offs, kbase, first, last = units[u]
        eb = u % 4
        ob = lt % 2
        for k, o in enumerate(offs):
            wt = lt + 2 + o
            S.op("pe", lambda e, wt=wt, k=k: e.matmul(
                C.O[:, ob, 0:256], lhsT=C.V[:, wt, :], rhs=C.P[:, eb, k * 256:(k + 1) * 256],
                start=(first and k == 0), stop=False), reads=["V", ("P", eb)], writes=[("O", ob)])
            for hh in range(2):
                S.op("pe", lambda e, k=k, hh=hh, fin=(last and k == len(offs) - 1 and hh == 1): e.matmul(
                    C.O[:, ob, 256:384], lhsT=C.vm[:, 3 + hh, :], rhs=C.P[:, eb, k * 256 + hh * 128:k * 256 + (hh + 1) * 128],
                    start=False, stop=fin), reads=["vm", ("P", eb)], writes=[("O", ob)])
        if last:
            td, tn = 2 * (lt % 2), 2 * (lt % 2) + 1
            S.op("act", lambda e: e.activation(out=C.tmp[:, td, :], in_=C.O[:, ob, 256:384], func=AF.Ln), reads=[("O", ob)], writes=[("tmp", td)])
            S.op("act", lambda e: e.activation(out=C.tmp[:, td, :], in_=C.tmp[:, td, :], func=AF.Exp, scale=-1.0), reads=[("tmp", td)], writes=[("tmp", td)])
            for hh in range(2):
                hs = slice(64 * hh, 64 * hh + 64)
                S.op("dve", lambda e, hs=hs, hh=hh: e.tensor_tensor(out=C.tmp[hs, tn, :], in0=C.O[hs, ob, hh * 128:(hh + 1) * 128],
                                                                 in1=C.tmp[hs, td, :], op=ALU.mult),
                     reads=[("O", ob), ("tmp", td)], writes=[("tmp", tn)])
            S.op(FIN_ENG, lambda e: e.tensor_tensor(out=C.uT[:, lt * 128:(lt + 1) * 128], in0=C.tmp[:, tn, :],
                                                    in1=C.GT[:, lt * 128:(lt + 1) * 128], op=ALU.mult),
                 reads=[("tmp", tn), "GT"], writes=[("uT", lt // 4)])
            if lt % 4 == 3:
                pend_outproj(C, p, lt // 4)

    for u0 in range(min(3, len(units))):
        qk(u0)
    for u in range(len(units)):
        if u + 3 < len(units):
            qk(u + 3)
        pv(u)


def build_l0(nc, es, npair=NPAIR):
    C = common_setup(nc, es)
    S = C.S
    io = l0_decl(C, nc, W0)
    x1o = nc.dram_tensor("x1T", [D_MODEL, OWN], F32, kind="ExternalOutput").ap()
    h1o = nc.dram_tensor("h1T", [D_MODEL, OWN], BF16, kind="ExternalOutput").ap()
    Tab = S.sbuf("Tab", [128, 1, 2 * NTABBLK * 128], BF16)
    hout = S.sbuf("hout", [128, 2, 512], BF16)
    l0_pass(C, io, Tab, HALO0, 16, True, npair=npair)
    outs = []
    for c in range(NCH):
        outs.append(S.dma("sp", lambda e, c=c: e.dma_start(out=x1o.rearrange("(c p) t -> p c t", p=128)[:, c, :], in_=C.xT[:, c, :]),
                          reads=[("x", c, tb) for tb in range(4)], slot=("xo", c)))
    hov = h1o.rearrange("(c p) t -> p c t", p=128)
    emit_norm_out(C, 1, hout, hov, outs, 4)
    S.final_wait("sp", outs)
    S.emit()
    return C


def emit_norm_out(C, gidx, stage, dst_view, outs, ntb=4, dcol=0):
    k = [0]
    for tb in range(ntb):
        emit_norm_block_out(C, tb, gidx, stage, dst_view, outs, k, dcol)


def emit_norm_block_out(C, tb, gidx, stage, dst_view, outs, k, dcol=0):
    S = C.S
    n = 512
    t0 = tb * 512
    gi = next_g(C)
    gk = ("G", gi)
    for c in range(NCH):
        sb = c % 2
        S.op("act", lambda e, c=c, sb=sb: e.activation(out=C.sq[:, sb, :], in_=C.xT[:, c, t0:t0 + n], func=AF.Square),
             reads=[("x", c, tb)], writes=[("sq", sb)])
        S.op("pe", lambda e, c=c, sb=sb: e.matmul(C.G[:, gi, :], lhsT=C.onesf[:, :], rhs=C.sq[:, sb, :],
                                                  start=(c == 0), stop=(c == NCH - 1)),
             reads=["onesf", ("sq", sb)], writes=[gk])
    rs, rkeys = rstd_buf(C)
    S.op("act", lambda e: e.activation(out=rs[:, :], in_=C.G[:, gi, :], func=AF.Ln, scale=1.0 / D_MODEL, bias=C.epsb[:, 0:1]),
         reads=[gk, "epsb"], writes=rkeys)
    S.op("act", lambda e: e.activation(out=rs[:, :], in_=rs[:, :], func=AF.Exp, scale=-0.5), reads=rkeys, writes=rkeys)
    for c in range(NCH):
        b = k[0] % 2
        k[0] += 1
        S.op("dve", lambda e, c=c, b=b: e.scalar_tensor_tensor(out=stage[:, b, :], in0=C.xT[:, c, t0:t0 + n],
                                                             scalar=C.gam[:, gidx, c:c + 1], in1=rs[:, :],
                                                             op0=ALU.mult, op1=ALU.mult),
             reads=[("x", c, tb), "gam"] + rkeys, writes=[("stage", b)])
        outs.append(S.dma("sp", lambda e, c=c, b=b: e.dma_start(out=dst_view[:, c, dcol + t0:dcol + t0 + n], in_=stage[:, b, :]),
                          reads=[("stage", b)], writes=["hwin"], slot=("so", b)))


def build_fused(nc, es, npair=NPAIR):
    C = common_setup(nc, es)
    S = C.S
    C.big = S.sbuf("big", [128, 2 * OWN], F32)
    io = l0_decl(C, nc, XWF)
    w_in = nc.dram_tensor("w_in1", [NPAIR * 10, 128, 1024], F32, kind="ExternalInput").ap()
    w_out = nc.dram_tensor("w_out1", [NPAIR, 128, 1024], F32, kind="ExternalInput").ap()
    dtab = nc.dram_tensor("dtab", [128, 256], F32, kind="ExternalInput").ap()
    yo = nc.dram_tensor("yT", [D_MODEL, OWN], F32, kind="ExternalOutput").ap()
    hwin = nc.dram_tensor("hwin", [D_MODEL, W1], BF16).ap()
    hv = hwin.rearrange("(c p) t -> p c t", p=128)

    Tab = C.big[:, :].bitcast(BF16)[:, 0:2 * NTABBLK * 128].rearrange("p (a c) -> p a c", a=1)
    hb = S.sbuf("hb", [128, 2, NCH, 256], BF16)
    acc = C.big[:, :].rearrange("p (a t) -> p a t", a=2)
    M1 = S.sbuf("M1", [128, 3, 512], BF16)
    D = S.sbuf("D", [128, 256], BF16)
    stage = acc[:, 0, 0:1024].rearrange("p (a c) -> p a c", a=2)
    S.dma("pool", lambda e: e.dma_start(out=D[:], in_=dtab), writes=["D"], slot="c4")

    hst = hb[:, :, 0:2, :].rearrange("p a b c -> p a (b c)")
    hw_outs = []
    for rs, dcol in ((HALO0, 0), (HALO0 + HALO1 + OWN + 0, HALO1 + OWN)):
        l0_pass(C, io, Tab, rs, 8, False, npair=npair)
        emit_norm_out(C, 1, hst, hv, hw_outs, 2, dcol)
    for b_ in range(2):
        S.readers.setdefault(("hb", b_), []).extend(S.readers.get(("stage", b_), []))
        S.last_w.setdefault(("hb", b_), []).extend(S.last_w.get(("stage", b_), []))
    l0_pass(C, io, Tab, HALO0 + HALO1, 16, True, npair=npair)
    emit_norm(C, lambda c, t0, n: (C.xT[:, c, t0:t0 + n], ("x", c, t0 // 512)), OWN, 1,
              lambda c, t0, n: (C.hT[:, c, t0:t0 + n], "hT"))

    blocks = [w_in[p * 10 + k_] for p in range(npair) for k_ in range(10)]
    issued = [0]
    slots = {}

    def wget(i):
        while issued[0] < min(len(blocks), i + 5):
            slots[issued[0]] = load_wblock(C, blocks[issued[0]])
            issued[0] += 1
        return slots[i]

    hbc = [0]
    l1_body(C, npair, wget, w_out, hv, hb, hbc, acc, M1, D)
    flush_outproj(C)
    outs = []
    emit_norm_out(C, 2, stage, yo.rearrange("(c p) t -> p c t", p=128), outs)
    S.final_wait("sp", outs)
    S.emit()
    return C


def l1_body(C, npair, wget, w_out, hv, hb, hbc, acc, M1, D):
    S = C.S
    for p in range(npair):
        _l1_pair(C, p, wget, w_out, hv, hb, hbc, acc, M1, D)


def _l1_pair(C, p, wget, w_out, hv, hb, hbc, acc, M1, D):
    S = C.S
    S.dma("pool", lambda e: e.dma_start(out=C.Wo[:, p % 2, :], in_=w_out[p]), writes=[("Wo", p % 2)], slot=("Wo", p % 2))
    for g, d in enumerate(DILS):
        for hh in range(2):
            slope = 2.0 ** (-8.0 * (2 * p + hh + 1) / 16.0)
            S.op("act", lambda e, g=g, hh=hh, sc=-slope * d: e.activation(
                out=M1[:, g, :].rearrange("p (kl hh q) -> p kl hh q", kl=2, hh=2)[:, :, hh, :],
                in_=D[:, :].rearrange("p (kl q) -> p kl q", kl=2), func=AF.Exp, scale=sc),
                 reads=["D"], writes=["M1"])
    wg = wget(p * 10)
    for tb in range(4):
        proj_block(C, wg, lambda c, tb=tb: (C.hT[:, c, tb * 512:(tb + 1) * 512], "hT"), 512,
                   evac_silu(C, C.GT[:, tb * 512:(tb + 1) * 512], "GT"))
    for g, d in enumerate(DILS):
        _l1_group(C, p, g, d, wget, hv, hb, hbc, acc, M1)
    flush_outproj(C)
    for tb in range(4):
        ts_ = slice(tb * 512, (tb + 1) * 512)
        keys = [("acc", k) for k in range(tb * 4, tb * 4 + 4)]
        S.op("act", lambda e, ts_=ts_: e.activation(out=acc[:, 1, ts_], in_=acc[:, 1, ts_], func=AF.Ln), reads=keys, writes=keys)
        S.op("act", lambda e, ts_=ts_: e.activation(out=acc[:, 1, ts_], in_=acc[:, 1, ts_], func=AF.Exp, scale=-1.0), reads=keys, writes=keys)
        S.op("dve", lambda e, ts_=ts_: e.tensor_tensor(out=acc[:, 0, ts_], in0=acc[:, 0, ts_], in1=acc[:, 1, ts_], op=ALU.mult),
             reads=keys, writes=keys)
        S.op("dve", lambda e, ts_=ts_: e.tensor_tensor(out=C.uT[:, ts_], in0=acc[:, 0, ts_], in1=C.GT[:, ts_], op=ALU.mult),
             reads=keys + ["GT"], writes=[("uT", tb)])
        pend_outproj(C, p, tb)


def _l1_group(C, p, g, d, wget, hv, hb, hbc, acc, M1):
    S = C.S
    halo = 64 * d
    base = HALO1 - halo
    wq, wk, wv = wget(p * 10 + 1 + g * 3), wget(p * 10 + 2 + g * 3), wget(p * 10 + 3 + g * 3)
    hn = min(halo, 256)
    hblks = [(True, w0, hn) for w0 in range(base, HALO1, hn)] + \
            [(True, w0, hn) for w0 in range(HALO1 + OWN, HALO1 + OWN + halo, hn)]
    oblks = [(False, HALO1 + tb * 512, 512) for tb in range(4)]
    blks = []
    per = -(-len(hblks) // 4)
    for i in range(4):
        blks.extend(hblks[i * per:(i + 1) * per])
        blks.append(oblks[i])
    for (is_h, w0, n) in blks:
        col = w0 - base
        if is_h:
            hs_ = hbc[0] % 2
            hbc[0] += 1
            S.dma("sp", lambda e, hs_=hs_, w0=w0, n=n: e.dma_start(out=hb[:, hs_, :, 0:n], in_=hv[:, :, w0:w0 + n]),
                  reads=["hwin"], writes=[("hb", hs_)], slot=("hb", hs_))
            h_of = lambda c, hs_=hs_, n=n: (hb[:, hs_, c, 0:n], ("hb", hs_))
        else:
            t0 = w0 - HALO1
            h_of = lambda c, t0=t0, n=n: (C.hT[:, c, t0:t0 + n], "hT")
        proj_block(C, wk, h_of, n, evac_copy_act(C, C.KT[:, col:col + n], "KT"))
        if is_h:
            side = 0 if w0 < HALO1 else 1
            proj_block(C, wv, h_of, n, evac_scaled_dve(C, C.VT[:, col:col + n], "VT", C.vcol[:, 2 + side:3 + side]))
        else:
            proj_block(C, wv, h_of, n, evac_copy_dve(C, C.VT[:, col:col + n], "VT"))
    nq = OWN // (128 * d)
    nkt = nq + 1
    tiles = []
    for r in range(d):
        for kt in range(nkt):
            c0 = 128 * kt * d + r
            edge = None
            tiles.append((r * nkt + kt, C.VT[:, c0:c0 + 127 * d + 1:d], edge))
    emit_vtrans(C, tiles)
    for tb in range(4):
        proj_block(C, wq, lambda c, tb=tb: (C.hT[:, c, tb * 512:(tb + 1) * 512], "hT"), 512,
                   evac_q(C, tb * 512, 512))
    units = [(r, qt) for r in range(d) for qt in range(nq)]

    def qk(u):
        r, qt = units[u]
        sb, eb = u % 4, u % 4
        q0 = 128 * qt * d + r
        for kl in range(2):
            c0 = 128 * (qt + kl) * d + r
            S.op("pe", lambda e, kl=kl, c0=c0: e.matmul(
                C.Sp[:, sb, kl * 256:(kl + 1) * 256].rearrange("p (a c) -> p a c", a=2),
                lhsT=C.KT[:, c0:c0 + 127 * d + 1:d], rhs=C.QT[:, :, q0:q0 + 127 * d + 1:d], start=True, stop=True),
                reads=["KT", "QT"], writes=[("S", sb)])
        S.op("act", lambda e: e.activation(out=C.E[:, eb, 0:512], in_=C.Sp[:, sb, 0:512], func=AF.Exp, scale=0.125),
             reads=[("S", sb)], writes=[("E", eb)])
        S.op(MASK_ENG if u % 2 == 1 else "dve", lambda e: e.tensor_tensor(out=C.P[:, eb, 0:512], in0=C.E[:, eb, 0:512], in1=M1[:, g, :], op=ALU.mult),
             reads=[("E", eb), "M1"], writes=[("P", eb)])

    def pv(u):
        r, qt = units[u]
        sb = u % 4
        ob = u % 2
        q0 = 128 * qt * d + r
        for kl in range(2):
            kt = qt + kl
            ti = r * nkt + kt
            vsel = 1 if kt == 0 else (2 if kt == nkt - 1 else 0)
            S.op("pe", lambda e, kl=kl, ti=ti: e.matmul(
                C.O[:, ob, 0:256], lhsT=C.V[:, ti, :], rhs=C.P[:, sb, kl * 256:(kl + 1) * 256],
                start=(kl == 0), stop=False), reads=["V", ("P", sb)], writes=[("O", ob)])
            S.op("pe", lambda e, kl=kl, vsel=vsel: e.matmul(
                C.O[:, ob, 256:512], lhsT=C.vm[:, vsel, :], rhs=C.P[:, sb, kl * 256:(kl + 1) * 256],
                start=False, stop=(kl == 1)), reads=["vm", ("P", sb)], writes=[("O", ob)])
        blk_lo, blk_hi = q0 // 128, (q0 + 127 * d) // 128
        keys = [("acc", k) for k in range(blk_lo, blk_hi + 1)]
        for hh in range(2):
            hs = slice(64 * hh, 64 * hh + 64)
            dst = acc[hs, :, q0:q0 + 127 * d + 1:d]
            src = C.O[hs, ob, :].rearrange("p (a b c) -> p a b c", a=2, b=2)[:, :, hh, :]
            if g == 0:
                S.op("dve", lambda e, dst=dst, src=src: e.tensor_copy(out=dst, in_=src), reads=[("O", ob)], writes=keys)
            else:
                S.op("dve", lambda e, dst=dst, src=src: e.tensor_tensor(out=dst, in0=src, in1=dst, op=ALU.add),
                     reads=[("O", ob)] + keys, writes=keys)

    for u0 in range(min(3, len(units))):
        qk(u0)
    for u in range(len(units)):
        if u + 3 < len(units):
            qk(u + 3)
        pv(u)


def wblock(w, col0):
    return np.ascontiguousarray(w[:, col0:col0 + 128].reshape(NCH, 128, 128).transpose(1, 0, 2).reshape(128, 1024))


def gam_layout(*gs):
    return np.ascontiguousarray(np.concatenate([g.reshape(NCH, 128).T for g in gs], axis=1)).astype(np.float32)


def make_cst(j):
    cst = np.zeros((128, CSTW), np.float32)
    cst[:, 0:128] = np.eye(128, dtype=np.float32)
    vl = np.ones(128, np.float32)
    vr = np.ones(128, np.float32)
    if j == 0:
        vl[:64] = 0.0
    if j == 3:
        vr[64:] = 0.0
    cst[:, 128:256] = 1.0
    cst[:, 256:384] = vl[:, None]
    cst[:, 384:512] = vr[:, None]
    cst[:, 512:576] = 1.0
    cst[:, 704:768] = 1.0
    cst[:, 768] = vl
    cst[:, 769] = vr
    cst[:, 770] = 0.0 if j == 0 else 1.0
    cst[:, 771] = 0.0 if j == 3 else 1.0
    return cst


def make_tab(rpb, j):
    kp = np.arange(128)
    kr2, kc = kp // 64, kp % 64
    q = np.arange(128)
    qr2, qc = q // 64, q % 64
    cstart = np.clip(qc - 8, 0, 64 - 16)
    colv = (kc[:, None] >= cstart[None, :]) & (kc[:, None] < cstart[None, :] + 16)
    coff = np.clip(kc[:, None] - qc[None, :] + 15, 0, 30)
    out = np.full((16, NTABBLK, 128, 128), NEG, np.float32)
    for key, (toff, offs) in TAB_OFFS.items():
        lt = 5 if key == "G" else key
        m = 16 * j + lt
        r = 2 * m + qr2
        rs = np.clip(r - 4, 0, 128 - 8)
        for k, o in enumerate(offs):
            krow = 2 * (m + o) + kr2
            rowv = (krow[:, None] >= rs[None, :]) & (krow[:, None] < rs[None, :] + 8) & (krow[:, None] >= 0) & (krow[:, None] < 128)
            valid = rowv & colv
            roff = np.clip(krow[:, None] - r[None, :] + 7, 0, 14)
            vals = rpb[:, roff, coff]
            out[:, toff + k] = np.where(valid[None], vals, np.float32(NEG))
    out = out.reshape(NPAIR, 2, NTABBLK, 128, 128).transpose(0, 3, 2, 1, 4).reshape(NPAIR, 128, 2 * NTABBLK * 128)
    return np.ascontiguousarray(out)


def window_T(xb, t0, halo):
    out = np.zeros((xb.shape[1], OWN + 2 * halo), xb.dtype)
    lo, hi = t0 - halo, t0 + OWN + halo
    a, b = max(lo, 0), min(hi, SEQ)
    out[:, a - lo:b - lo] = xb[a:b].T
    return out


_CACHE = {}


def get_prog(name, builder):
    if name not in _CACHE:
        nc = bass.Bass("TRN2", target_bir_lowering=False)
        es = ExitStack()
        builder(nc, es)
        _CACHE[name] = (nc, es)
    return _CACHE[name][0]


def run_l0(x, norm_0, w_in_0, rpb_0, w_out_0, norm_1, norm_f, trace=False):
    nc = get_prog("l0", build_l0)
    w_in_b = np.stack([wblock(w_in_0, k * 1024 + p * 128) for p in range(NPAIR) for k in range(4)])
    w_out_b = np.ascontiguousarray(w_out_0.reshape(NPAIR, 128, 1024))
    gam = gam_layout(norm_0, norm_1, norm_f)
    tabs = [make_tab(rpb_0, j) for j in range(4)]
    csts = [make_cst(j) for j in range(4)]
    in_maps = []
    for c in range(NCORE):
        b, j = divmod(c, 4)
        in_maps.append({"xw": window_T(x[b], j * OWN, HALO0), "gamd": gam, "w_in0": w_in_b, "w_out0": w_out_b,
                        "tab0": tabs[j], "cst": csts[j]})
    res = run_bass_kernel_spmd(nc, in_maps, core_ids=list(range(NCORE)), trace=trace)
    return res


def make_dtab():
    kp = np.arange(128)[:, None]
    j = np.arange(128)[None, :]
    out = np.empty((128, 256), np.float32)
    for kl, sh in enumerate((-64, 64)):
        delta = np.abs(kp + sh - j)
        out[:, kl * 128:(kl + 1) * 128] = np.where(delta <= 64, delta, 1.0e6)
    return out


def run_l1(x1T_list, h1T_list, w_in_1, w_out_1, norm_0, norm_1, norm_f, trace=False):
    nc = get_prog("l1", build_l1)
    cols = []
    for p in range(NPAIR):
        cols.append(9216 + p * 128)
        for g in range(3):
            for k in range(3):
                cols.append(g * 3072 + k * 1024 + p * 128)
    w_in_b = np.stack([wblock(w_in_1, c0) for c0 in cols])
    w_out_b = np.ascontiguousarray(w_out_1.reshape(NPAIR, 128, 1024))
    gam = gam_layout(norm_0, norm_1, norm_f)
    dt = make_dtab()
    csts = [make_cst(j) for j in range(4)]
    in_maps = []
    for c in range(NCORE):
        b, j = divmod(c, 4)
        hw = np.zeros((D_MODEL, W1), h1T_list[c].dtype)
        hw[:, HALO1:HALO1 + OWN] = h1T_list[c]
        if j > 0:
            hw[:, 0:HALO1] = h1T_list[c - 1][:, OWN - HALO1:]
        if j < 3:
            hw[:, HALO1 + OWN:] = h1T_list[c + 1][:, 0:HALO1]
        in_maps.append({"x1w": x1T_list[c], "h1w": hw, "gamd": gam, "w_in1": w_in_b, "w_out1": w_out_b,
                        "cst": csts[j], "dtab": dt})
    return run_bass_kernel_spmd(nc, in_maps, core_ids=list(range(NCORE)), trace=trace)


def kernel(x, norm_0, w_in_0, rpb_0, w_out_0, norm_1, w_in_1, w_out_1, norm_f):
    f = lambda a: np.ascontiguousarray(np.asarray(a, dtype=np.float32))
    x, norm_0, w_in_0, rpb_0, w_out_0, norm_1, w_in_1, w_out_1, norm_f = map(
        f, (x, norm_0, w_in_0, rpb_0, w_out_0, norm_1, w_in_1, w_out_1, norm_f))
    nc, in_maps = run_fused(x, norm_0, w_in_0, rpb_0, w_out_0, norm_1, w_in_1, w_out_1, norm_f)
    res = run_bass_kernel_spmd(nc, in_maps, core_ids=list(range(NCORE)))
    out = np.empty((2, SEQ, D_MODEL), np.float32)
    for c in range(NCORE):
        b, j = divmod(c, 4)
        out[b, j * OWN:(j + 1) * OWN, :] = np.asarray(res.results[c]["yT"]).T
    return out


def make_hidx(j):
    l = j - 1 if j > 0 else j
    r = j + 1 if j < 3 else j
    p = np.arange(128, dtype=np.int64)
    out = np.zeros((128, NHIDX), np.uint32)
    for side, nbr in enumerate((l, r)):
        for c in range(NCH):
            for blk in range(4):
                tb = 4 + blk if side == 0 else blk
                out[:, side * 32 + c * 4 + blk] = (nbr * D_MODEL + c * 128 + p) * 8 + tb
    return out


def l1_cols():
    cols = []
    for p in range(NPAIR):
        cols.append(9216 + p * 128)
        for g in range(3):
            for k in range(3):
                cols.append(g * 3072 + k * 1024 + p * 128)
    return cols


def run_fused(x, norm_0, w_in_0, rpb_0, w_out_0, norm_1, w_in_1, w_out_1, norm_f, trace=False):
    nc = get_prog("fused", build_fused)
    w_in0_b = np.stack([wblock(w_in_0, k * 1024 + p * 128) for p in range(NPAIR) for k in range(4)])
    w_out0_b = np.ascontiguousarray(w_out_0.reshape(NPAIR, 128, 1024))
    w_in1_b = np.stack([wblock(w_in_1, c0) for c0 in l1_cols()])
    w_out1_b = np.ascontiguousarray(w_out_1.reshape(NPAIR, 128, 1024))
    gam = gam_layout(norm_0, norm_1, norm_f)
    dt = make_dtab()
    tabs = [make_tab(rpb_0, j) for j in range(4)]
    csts = [make_cst(j) for j in range(4)]
    hidx = [make_hidx(j) for j in range(4)]
    in_maps = []
    for c in range(NCORE):
        b, j = divmod(c, 4)
        in_maps.append({"xw": window_T(x[b], j * OWN, HALO0 + HALO1), "gamd": gam, "w_in0": w_in0_b, "w_out0": w_out0_b,
                        "tab0": tabs[j], "cst": csts[j], "w_in1": w_in1_b, "w_out1": w_out1_b, "dtab": dt})
    return nc, in_maps
```

```python
import numpy as np
from contextlib import ExitStack
import concourse.bass as bass
import concourse.mybir as mybir
from concourse.bass_utils import run_bass_kernel_spmd

F32 = mybir.dt.float32
BF16 = mybir.dt.bfloat16
AF = mybir.ActivationFunctionType
ALU = mybir.AluOpType


class Op:
    __slots__ = ("eng", "fn", "deps", "is_dma", "slot", "ord", "marked", "name")

    def __init__(self, eng, fn, is_dma=False, slot=None, name=""):
        self.eng = eng
        self.fn = fn
        self.deps = []
        self.is_dma = is_dma
        self.slot = slot
        self.ord = None
        self.marked = False
        self.name = name


class Sched:
    ENGS = ("pe", "act", "dve", "pool", "sp")

    def __init__(self, nc, es):
        self.nc = nc
        self.es = es
        self.q = {e: [] for e in self.ENGS}
        self.last_w = {}
        self.readers = {}
        self.finals = []
        self.slot_count = {}
        self.n_sb = 0

    def sbuf(self, name, shape, dtype):
        return self.es.enter_context(self.nc.sbuf_tensor(name, shape, dtype))

    def psum(self, name, shape, dtype):
        return self.es.enter_context(self.nc.psum_tensor(name, shape, dtype))

    def _add(self, op, reads, writes, deps):
        dl = []
        for r in reads:
            dl.extend(self.last_w.get(r, ()))
            self.readers.setdefault(r, []).append(op)
        for w_ in writes:
            prev = self.last_w.get(w_, [])
            dl.extend(prev)
            for rd in self.readers.get(w_, ()):
                if rd is not op:
                    dl.append(rd)
            if op.is_dma and prev and all(q.is_dma for q in prev):
                self.last_w[w_] = prev + [op]
            else:
                self.last_w[w_] = [op]
            self.readers[w_] = []
        dl.extend(deps)
        seen = set()
        for d in dl:
            if d is op or id(d) in seen:
                continue
            seen.add(id(d))
            if d.eng == "pe" and op.eng == "pe" and not d.is_dma and not op.is_dma:
                continue
            op.deps.append(d)
        self.q[op.eng].append(op)
        return op

    def op(self, eng, fn, reads=(), writes=(), deps=(), name=""):
        return self._add(Op(eng, fn, name=name), reads, writes, deps)

    def dma(self, eng, fn, reads=(), writes=(), deps=(), slot=None, name=""):
        if slot is None:
            slot = ("auto", len(self.slot_count))
        op = Op(eng, fn, is_dma=True, slot=slot, name=name)
        k = self.slot_count.get(slot, 0) + 1
        self.slot_count[slot] = k
        op.ord = k
        return self._add(op, reads, writes, deps)

    def final_wait(self, eng, ops):
        self.finals.append((eng, list(ops)))

    def emit(self):
        nc = self.nc
        for e in self.ENGS:
            for op in self.q[e]:
                for d in op.deps:
                    d.marked = True
        for _, ops in self.finals:
            for d in ops:
                d.marked = True
        for e in self.ENGS:
            for op in self.q[e]:
                if op.is_dma:
                    op.marked = True
        for e in self.ENGS:
            k = 0
            for op in self.q[e]:
                if not op.is_dma and op.marked:
                    k += 1
                    op.ord = k
        sems = {}

        def sem_of(op):
            key = ("dma", op.slot) if op.is_dma else ("eng", op.eng)
            if key not in sems:
                sems[key] = self.es.enter_context(nc.semaphore("s%d" % len(sems)))
            return sems[key]

        def val_of(op):
            return op.ord * 16 if op.is_dma else op.ord

        for e in self.ENGS:
            for op in self.q[e]:
                if op.marked:
                    sem_of(op)
        finals = {}
        for eng, ops in self.finals:
            finals.setdefault(eng, []).extend(ops)
        self.n_waits = 0

        def run(engname, engine):
            known = {}

            def need(d):
                s = sem_of(d)
                v = val_of(d)
                if known.get(id(s), 0) >= v:
                    return
                known[id(s)] = v
                engine.wait_ge(s, v)
                self.n_waits += 1

            for op in self.q[engname]:
                best = {}
                for d in op.deps:
                    s = sem_of(d)
                    if id(s) not in best or val_of(best[id(s)]) < val_of(d):
                        best[id(s)] = d
                for d in best.values():
                    need(d)
                ins = op.fn(engine)
                if op.marked:
                    ins.then_inc(sem_of(op), 16 if op.is_dma else 1)
            for d in finals.get(engname, ()):
                need(d)

        with nc.Block() as block:
            @block.sync
            def _(eng):
                run("sp", eng)

            @block.tensor
            def _(eng):
                run("pe", eng)

            @block.scalar
            def _(eng):
                run("act", eng)

            @block.vector
            def _(eng):
                run("dve", eng)

            @block.gpsimd
            def _(eng):
                run("pool", eng)


D_MODEL = 1024
SEQ = 8192
NCORE = 8
OWN = 2048
NCH = 8
NPAIR = 8
HALO0 = 256
W0 = OWN + 2 * HALO0
HALO1 = 1024
W1 = OWN + 2 * HALO1
DILS = (1, 4, 16)
NEG = -30000.0
EPS = 1e-6
TAB_OFFS = {"G": (0, (-2, -1, 0, 1, 2)), 0: (5, (-2, -1, 0, 1, 2, 3)), 1: (11, (-2, -1, 0, 1, 2)),
            14: (16, (-2, -1, 0, 1, 2)), 15: (21, (-3, -2, -1, 0, 1, 2))}
NTABBLK = 27
U32 = mybir.dt.uint32
NHIDX = 64
MASK_ENG = "pool"
FIN_ENG = "pool"
CSTW = 128 + 5 * 128 + 4


def tab_for(lt):
    return TAB_OFFS[lt] if lt in TAB_OFFS else TAB_OFFS["G"]


class Ctx:
    pass


def common_setup(nc, es, hcols=W0):
    C = Ctx()
    S = Sched(nc, es)
    C.S = S
    C.nc = nc
    C.G = S.psum("G", [128, 2, 512], F32)
    C.Sp = S.psum("Sp", [128, 4, 512], F32)
    C.O = S.psum("O", [128, 2, 512], F32)
    C.gi = 0
    C.xT = S.sbuf("xT", [128, NCH, OWN], F32)
    C.hT = S.sbuf("hT", [128, NCH, hcols], BF16)
    C.QT = S.sbuf("QT", [128, 2, OWN], BF16)
    C.KV = S.sbuf("KV", [128, 2 * W1], BF16)
    C.KT = C.KV[:, 0:W1]
    C.VT = C.KV[:, W1:2 * W1]
    C.V = S.sbuf("V", [128, 32, 128], BF16)
    C.GT = S.sbuf("GT", [128, OWN], BF16)
    C.uT = S.sbuf("uT", [128, OWN], BF16)
    C.Wb = S.sbuf("Wb", [128, 8, 1024], BF16)
    C.Wo = S.sbuf("Wo", [128, 2, 1024], BF16)
    C.E = S.sbuf("E", [128, 4, 512], BF16)
    C.P = S.sbuf("P", [128, 4, 512], BF16)
    C.sq = S.sbuf("sq", [128, 2, 512], BF16)
    C.rstd = S.sbuf("rstd", [128, 512], F32)
    C.tmp = S.sbuf("tmp", [128, 4, 128], F32)
    C.onesf = S.sbuf("onesf", [128, 128], BF16)
    C.vm = S.sbuf("vm", [128, 5, 128], BF16)
    C.vcol = S.sbuf("vcol", [128, 4], F32)
    C.gam = S.sbuf("gam", [128, 3, NCH], F32)
    C.wslot = 0
    C.pend = []
    C.oi = 0
    C.pops = 2
    C.nrm = 0
    C.epsb = S.sbuf("epsb", [128, 1], F32)
    C.ident = S.sbuf("ident", [128, 128], BF16)
    S.op("dve", lambda e: e.memset(C.epsb[:], EPS), writes=["epsb"])
    S.op("dve", lambda e: e.memset(C.onesf[:], 1.0), writes=["onesf"])
    S.op("pool", lambda e: e.memset(C.QT[:], 0.0), writes=["QT"])
    return C


def next_g(C):
    gi = C.gi
    C.gi = (gi + 1) % 2
    return gi


def load_wblock(C, src_ap):
    S = C.S
    s = C.wslot
    C.wslot = (s + 1) % 8
    S.dma("pool", lambda e: e.dma_start(out=C.Wb[:, s, :], in_=src_ap), writes=[("W", s)], slot=("W", s))
    return s


def proj_block(C, ws, h_of, n, evac):
    S = C.S
    gi = next_g(C)
    gk = ("G", gi)
    for c in range(NCH):
        rhs, rk = h_of(c)
        S.op("pe", lambda e, c=c, rhs=rhs: e.matmul(C.G[:, gi, 0:n], lhsT=C.Wb[:, ws, c * 128:(c + 1) * 128],
                                                    rhs=rhs, start=(c == 0), stop=(c == NCH - 1)),
             reads=[("W", ws), rk], writes=[gk])
    evac(C.G[:, gi, 0:n], gk)
    pop_outproj(C, C.pops)


def emit_norm(C, x_of, ntok, gidx, out_of):
    t0 = 0
    while t0 < ntok:
        n = min(512, ntok - t0)
        _norm_block(C, x_of, t0, n, gidx, out_of)
        t0 += n


def rstd_buf(C):
    k = C.nrm % 2
    C.nrm += 1
    if k == 0:
        return C.rstd[:, :], [("rstd", 0)]
    return C.tmp[:, :, :].rearrange("p a c -> p (a c)"), [("rstd", 1)] + [("tmp", i) for i in range(4)]


def _norm_block(C, x_of, t0, n, gidx, out_of):
    S = C.S
    gi = next_g(C)
    gk = ("G", gi)
    rs, rkeys = rstd_buf(C)
    for c in range(NCH):
        xa, xk = x_of(c, t0, n)
        sb = c % 2
        S.op("act", lambda e, xa=xa, sb=sb: e.activation(out=C.sq[:, sb, 0:n], in_=xa, func=AF.Square),
             reads=[xk], writes=[("sq", sb)])
        S.op("pe", lambda e, c=c, sb=sb: e.matmul(C.G[:, gi, 0:n], lhsT=C.onesf[:, :], rhs=C.sq[:, sb, 0:n],
                                                  start=(c == 0), stop=(c == NCH - 1)),
             reads=["onesf", ("sq", sb)], writes=[gk])
    S.op("act", lambda e: e.activation(out=rs[:, 0:n], in_=C.G[:, gi, 0:n], func=AF.Ln,
                                       scale=1.0 / D_MODEL, bias=C.epsb[:, 0:1]),
         reads=[gk, "epsb"], writes=rkeys)
    S.op("act", lambda e: e.activation(out=rs[:, 0:n], in_=rs[:, 0:n], func=AF.Exp, scale=-0.5), reads=rkeys, writes=rkeys)
    for c in range(NCH):
        xa, xk = x_of(c, t0, n)
        oa, ok = out_of(c, t0, n)
        if False:
            S.op("pool", lambda e, xa=xa, oa=oa, c=c: e.scalar_tensor_tensor(
                out=oa, in0=xa, scalar=C.gam[:, gidx, c:c + 1], in1=rs[:, 0:n], op0=ALU.mult, op1=ALU.mult),
                 reads=[xk, "gam"] + rkeys, writes=[ok])
        else:
            S.op("dve", lambda e, xa=xa, oa=oa, c=c: e.scalar_tensor_tensor(
                out=oa, in0=xa, scalar=C.gam[:, gidx, c:c + 1], in1=rs[:, 0:n], op0=ALU.mult, op1=ALU.mult),
                 reads=[xk, "gam"] + rkeys, writes=[ok])


def emit_outproj(C, pair, wo_src, tbs, ms=None):
    S = C.S
    for tb in tbs:
        for m in (range(NCH) if ms is None else ms):
            gi = C.oi
            C.oi = (gi + 1) % 4
            gk = ("S", gi)
            S.op("pe", lambda e, m=m, gi=gi, tb=tb: e.matmul(C.Sp[:, gi, :], lhsT=C.Wo[:, pair % 2, m * 128:(m + 1) * 128],
                                                          rhs=C.uT[:, tb * 512:(tb + 1) * 512], start=True, stop=True),
                 reads=[("Wo", pair % 2), ("uT", tb)], writes=[gk])
            S.op("dve", lambda e, m=m, gi=gi, tb=tb: e.tensor_tensor(out=C.xT[:, m, tb * 512:(tb + 1) * 512],
                                                                 in0=C.Sp[:, gi, :], in1=C.xT[:, m, tb * 512:(tb + 1) * 512],
                                                                 op=ALU.add),
                 reads=[gk, ("x", m, tb)], writes=[("x", m, tb)])


def pend_outproj(C, pair, tb):
    for m in range(NCH):
        C.pend.append((pair, tb, m))


def pop_outproj(C, n=1):
    for _ in range(n):
        if C.pend:
            pair, tb, m = C.pend.pop(0)
            emit_outproj(C, pair, None, [tb], [m])


def flush_outproj(C):
    pop_outproj(C, len(C.pend))


def evac_copy_act(C, dst, dkey):
    def f(ps, gk):
        C.S.op("act", lambda e: e.activation(out=dst, in_=ps, func=AF.Copy), reads=[gk], writes=[dkey])
    return f


def evac_copy_dve(C, dst, dkey):
    def f(ps, gk):
        C.S.op("dve", lambda e: e.tensor_copy(out=dst, in_=ps), reads=[gk], writes=[dkey])
    return f


def evac_q(C, c0, n):
    def f(ps, gk):
        C.S.op("act", lambda e: e.activation(out=C.QT[0:64, 0, c0:c0 + n], in_=ps[0:64, :], func=AF.Copy), reads=[gk], writes=["QT"])
        C.S.op("act", lambda e: e.activation(out=C.QT[64:128, 1, c0:c0 + n], in_=ps[64:128, :], func=AF.Copy), reads=[gk], writes=["QT"])
    return f


def evac_scaled_dve(C, dst, dkey, flag_ap):
    def f(ps, gk):
        C.S.op("dve", lambda e: e.tensor_scalar(out=dst, in0=ps, scalar1=flag_ap, scalar2=None, op0=ALU.mult), reads=[gk, "vcol"], writes=[dkey])
    return f


def evac_silu(C, dst, dkey):
    def f(ps, gk):
        C.S.op("act", lambda e: e.activation(out=dst, in_=ps, func=AF.Silu), reads=[gk], writes=[dkey])
    return f


def emit_vtrans(C, tiles):
    for i in range(0, len(tiles), 4):
        _vtrans_group(C, tiles[i:i + 4])


def _vtrans_group(C, grp):
    S = C.S
    gi = next_g(C)
    gk = ("G", gi)
    gb = C.G[:, gi, :].bitcast(BF16)
    n = len(grp)
    ti0 = grp[0][0]
    assert [t[0] for t in grp] == list(range(ti0, ti0 + n))
    for j, (ti, src, edge) in enumerate(grp):
        S.op("pe", lambda e, j=j, src=src: e.transpose(gb[:, j * 128:(j + 1) * 128], src, C.ident[:, :]),
             reads=["VT", "ident"], writes=[gk])
    S.op("act", lambda e: e.activation(out=C.V[:, ti0:ti0 + n, :].rearrange("p a c -> p (a c)"), in_=gb[:, 0:n * 128], func=AF.Copy),
         reads=[gk], writes=["V"])


XWF = OWN + 2 * (HALO1 + HALO0)


def l0_decl(C, nc, xwin):
    S = C.S
    io = Ctx()
    io.xw = nc.dram_tensor("xw", [D_MODEL, xwin], F32, kind="ExternalInput").ap()
    io.gam = nc.dram_tensor("gamd", [128, 3 * NCH], F32, kind="ExternalInput").ap()
    io.w_in = nc.dram_tensor("w_in0", [NPAIR * 4, 128, 1024], F32, kind="ExternalInput").ap()
    io.w_out = nc.dram_tensor("w_out0", [NPAIR, 128, 1024], F32, kind="ExternalInput").ap()
    io.tab = nc.dram_tensor("tab0", [NPAIR, 128, 2 * NTABBLK * 128], F32, kind="ExternalInput").ap()
    io.cst = nc.dram_tensor("cst", [128, CSTW], F32, kind="ExternalInput").ap()
    io.xv = io.xw.rearrange("(c p) t -> p c t", p=128)
    io.xh = C.KV[:, :].bitcast(F32).rearrange("p (c t) -> p c t", c=NCH)
    cst = io.cst
    S.dma("sp", lambda e: e.dma_start(out=C.gam[:].rearrange("p a c -> p (a c)"), in_=io.gam), writes=["gam"], slot="c0")
    S.dma("pool", lambda e: e.dma_start(out=C.ident[:], in_=cst[:, 0:128]), writes=["ident"], slot="c1")
    S.dma("pool", lambda e: e.dma_start(out=C.vm[:].rearrange("p a c -> p (a c)"), in_=cst[:, 128:768]), writes=["vm"], slot="c2")
    S.dma("sp", lambda e: e.dma_start(out=C.vcol[:], in_=cst[:, 768:772]), writes=["vcol"], slot="c3")
    return io


def l0_pass(C, io, Tab, rs, NT, special, npair=NPAIR):
    S = C.S
    xv, xh = io.xv, io.xh
    NQ = NT * 128
    WIN = NQ + 2 * HALO0
    NTB = NQ // 512
    for c in range(NCH):
        S.dma("sp", lambda e, c=c: e.dma_start(out=C.xT[:, c, 0:NQ], in_=xv[:, c, rs:rs + NQ]),
              writes=[("x", c, tb) for tb in range(NTB)], slot=("xl", c))
    S.dma("sp", lambda e: e.dma_start(out=xh[:, :, 0:HALO0], in_=xv[:, :, rs - HALO0:rs]), writes=["xh", "KT", "VT"], slot="xh0")
    S.dma("sp", lambda e: e.dma_start(out=xh[:, :, HALO0:2 * HALO0], in_=xv[:, :, rs + NQ:rs + NQ + HALO0]),
          writes=["xh", "KT", "VT"], slot="xh1")

    wslots = {}

    def prefetch(p):
        if p >= npair:
            return
        wslots[p] = [load_wblock(C, io.w_in[p * 4 + k]) for k in range(4)]

    def load_wo(p):
        S.dma("pool", lambda e: e.dma_start(out=C.Wo[:, p % 2, :], in_=io.w_out[p]), writes=[("Wo", p % 2)], slot=("Wo", p % 2))

    def prefetch_tab(p):
        S.dma("pool", lambda e: e.dma_start(out=Tab[:, 0, :], in_=io.tab[p]), writes=[("Tab", k_) for k_ in TAB_OFFS], slot=("Tab", 0))

    prefetch(0)

    emit_norm(C, lambda c, t0, n: (C.xT[:, c, t0:t0 + n], ("x", c, t0 // 512)), NQ, 0,
              lambda c, t0, n: (C.hT[:, c, HALO0 + t0:HALO0 + t0 + n], "hT"))
    for half in range(2):
        emit_norm(C, lambda c, t0, n, half=half: (xh[:, c, half * HALO0 + t0:half * HALO0 + t0 + n], "xh"), HALO0, 0,
                  lambda c, t0, n, half=half: (C.hT[:, c, half * (HALO0 + NQ) + t0:half * (HALO0 + NQ) + t0 + n], "hT"))

    hkey = "hT"
    for p in range(npair):
        prefetch(p + 1)
        load_wo(p)
        prefetch_tab(p)
        _l0_pair(C, p, wslots[p], Tab, NT, WIN, NTB, special)
    flush_outproj(C)


def _l0_pair(C, p, ws, Tab, NT, WIN, NTB, special):
    S = C.S
    hkey = "hT"
    wq, wk, wv, wg = ws
    for tb in range(WIN // 512):
        proj_block(C, wk, lambda c, tb=tb: (C.hT[:, c, tb * 512:(tb + 1) * 512], hkey), 512,
                   evac_copy_act(C, C.KT[:, tb * 512:(tb + 1) * 512], "KT"))
    for tb in range(WIN // 512):
        proj_block(C, wv, lambda c, tb=tb: (C.hT[:, c, tb * 512:(tb + 1) * 512], hkey), 512,
                   evac_copy_dve(C, C.VT[:, tb * 512:(tb + 1) * 512], "VT"))
    emit_vtrans(C, [(wt, C.VT[:, wt * 128:(wt + 1) * 128], None) for wt in range(WIN // 128)])
    for tb in range(NTB):
        proj_block(C, wq, lambda c, tb=tb: (C.hT[:, c, HALO0 + tb * 512:HALO0 + (tb + 1) * 512], hkey), 512,
                   evac_q(C, tb * 512, 512))
    for tb in range(NTB):
        proj_block(C, wg, lambda c, tb=tb: (C.hT[:, c, HALO0 + tb * 512:HALO0 + (tb + 1) * 512], hkey), 512,
                   evac_silu(C, C.GT[:, tb * 512:(tb + 1) * 512], "GT"))
    for key in ((0, 1, "G", 14, 15) if special else ("G",)):
        toff_, offs_ = TAB_OFFS[key]
        a_, b_ = toff_ * 256, (toff_ + len(offs_)) * 256
        S.op("act", lambda e, a_=a_, b_=b_: e.activation(out=Tab[:, 0, a_:b_], in_=Tab[:, 0, a_:b_], func=AF.Exp),
             reads=[("Tab", key)], writes=[("Tab", key)])
    tabf = (lambda lt: tab_for(lt)) if special else (lambda lt: TAB_OFFS["G"])
    units = []
    for lt in range(NT):
        toff, offs = tabf(lt)
        for i in range(0, len(offs), 2):
            units.append((lt, toff, offs[i:i + 2], i, i == 0, i + 2 >= len(offs)))
    tkey = (lambda lt: lt if lt in TAB_OFFS else "G") if special else (lambda lt: "G")

    def qk(u):
        lt, toff, offs, kbase, first, last = units[u]
        sb, eb = u % 4, u % 4
        n = len(offs)
        for k, o in enumerate(offs):
            wt = lt + 2 + o
            S.op("pe", lambda e, wt=wt, k=k: e.matmul(
                C.Sp[:, sb, k * 256:(k + 1) * 256].rearrange("p (a c) -> p a c", a=2), lhsT=C.KT[:, wt * 128:(wt + 1) * 128],
                rhs=C.QT[:, :, lt * 128:(lt + 1) * 128], start=True, stop=True),
                reads=["KT", "QT"], writes=[("S", sb)])
        S.op("act", lambda e: e.activation(out=C.E[:, eb, 0:n * 256], in_=C.Sp[:, sb, 0:n * 256], func=AF.Exp, scale=0.125),
             reads=[("S", sb)], writes=[("E", eb)])
        t0 = (toff + kbase) * 256
        S.op(MASK_ENG if u % 3 == 2 else "dve", lambda e: e.tensor_tensor(out=C.P[:, eb, 0:n * 256], in0=C.E[:, eb, 0:n * 256],
                                                 in1=Tab[:, 0, t0:t0 + n * 256], op=ALU.mult),
             reads=[("E", eb), ("Tab", tkey(lt))], writes=[("P", eb)])

    def pv(u):
        lt, toff, offs, kbase, first, last = units[u]
        eb = u % 4
        ob = lt % 2
        for k, o in enumerate(offs):
            wt = lt + 2 + o
            S.op("pe", lambda e, wt=wt, k=k: e.matmul(
                C.O[:, ob, 0:256], lhsT=C.V[:, wt, :], rhs=C.P[:, eb, k * 256:(k + 1) * 256],
                start=(first and k == 0), stop=False), reads=["V", ("P", eb)], writes=[("O", ob)])
            for hh in range(2):
                S.op("pe", lambda e, k=k, hh=hh, fin=(last and k == len(offs) - 1 and hh == 1): e.matmul(
                    C.O[:, ob, 256:384], lhsT=C.vm[:, 3 + hh, :], rhs=C.P[:, eb, k * 256 + hh * 128:k * 256 + (hh + 1) * 128],
                    start=False, stop=fin), reads=["vm", ("P", eb)], writes=[("O", ob)])
        if last:
            td, tn = 2 * (lt % 2), 2 * (lt % 2) + 1
            S.op("act", lambda e: e.activation(out=C.tmp[:, td, :], in_=C.O[:, ob, 256:384], func=AF.Ln), reads=[("O", ob)], writes=[("tmp", td)])
            S.op("act", lambda e: e.activation(out=C.tmp[:, td, :], in_=C.tmp[:, td, :], func=AF.Exp, scale=-1.0), reads=[("tmp", td)], writes=[("tmp", td)])
            for hh in range(2):
                hs = slice(64 * hh, 64 * hh + 64)
                S.op("dve", lambda e, hs=hs, hh=hh: e.tensor_tensor(out=C.tmp[hs, tn, :], in0=C.O[hs, ob, hh * 128:(hh + 1) * 128],
                                                                 in1=C.tmp[hs, td, :], op=ALU.mult),
                     reads=[("O", ob), ("tmp", td)], writes=[("tmp", tn)])
            S.op(FIN_ENG, lambda e: e.tensor_tensor(out=C.uT[:, lt * 128:(lt + 1) * 128], in0=C.tmp[:, tn, :],
                                                    in1=C.GT[:, lt * 128:(lt + 1) * 128], op=ALU.mult),
                 reads=[("tmp", tn), "GT"], writes=[("uT", lt // 4)])
            if lt % 4 == 3:
                pend_outproj(C, p, lt // 4)

    for u0 in range(min(3, len(units))):
        qk(u0)
    for u in range(len(units)):
        if u + 3 < len(units):
            qk(u + 3)
        pv(u)


def build_l0(nc, es, npair=NPAIR):
    C = common_setup(nc, es)
    S = C.S
    io = l0_decl(C, nc, W0)
    x1o = nc.dram_tensor("x1T", [D_MODEL, OWN], F32, kind="ExternalOutput").ap()
    h1o = nc.dram_tensor("h1T", [D_MODEL, OWN], BF16, kind="ExternalOutput").ap()
    Tab = S.sbuf("Tab", [128, 1, 2 * NTABBLK * 128], BF16)
    hout = S.sbuf("hout", [128, 2, 512], BF16)
    l0_pass(C, io, Tab, HALO0, 16, True, npair=npair)
    outs = []
    for c in range(NCH):
        outs.append(S.dma("sp", lambda e, c=c: e.dma_start(out=x1o.rearrange("(c p) t -> p c t", p=128)[:, c, :], in_=C.xT[:, c, :]),
                          reads=[("x", c, tb) for tb in range(4)], slot=("xo", c)))
    hov = h1o.rearrange("(c p) t -> p c t", p=128)
    emit_norm_out(C, 1, hout, hov, outs, 4)
    S.final_wait("sp", outs)
    S.emit()
    return C


def emit_norm_out(C, gidx, stage, dst_view, outs, ntb=4, dcol=0):
    k = [0]
    for tb in range(ntb):
        emit_norm_block_out(C, tb, gidx, stage, dst_view, outs, k, dcol)


def emit_norm_block_out(C, tb, gidx, stage, dst_view, outs, k, dcol=0):
    S = C.S
    n = 512
    t0 = tb * 512
    gi = next_g(C)
    gk = ("G", gi)
    for c in range(NCH):
        sb = c % 2
        S.op("act", lambda e, c=c, sb=sb: e.activation(out=C.sq[:, sb, :], in_=C.xT[:, c, t0:t0 + n], func=AF.Square),
             reads=[("x", c, tb)], writes=[("sq", sb)])
        S.op("pe", lambda e, c=c, sb=sb: e.matmul(C.G[:, gi, :], lhsT=C.onesf[:, :], rhs=C.sq[:, sb, :],
                                                  start=(c == 0), stop=(c == NCH - 1)),
             reads=["onesf", ("sq", sb)], writes=[gk])
    rs, rkeys = rstd_buf(C)
    S.op("act", lambda e: e.activation(out=rs[:, :], in_=C.G[:, gi, :], func=AF.Ln, scale=1.0 / D_MODEL, bias=C.epsb[:, 0:1]),
         reads=[gk, "epsb"], writes=rkeys)
    S.op("act", lambda e: e.activation(out=rs[:, :], in_=rs[:, :], func=AF.Exp, scale=-0.5), reads=rkeys, writes=rkeys)
    for c in range(NCH):
        b = k[0] % 2
        k[0] += 1
        S.op("dve", lambda e, c=c, b=b: e.scalar_tensor_tensor(out=stage[:, b, :], in0=C.xT[:, c, t0:t0 + n],
                                                             scalar=C.gam[:, gidx, c:c + 1], in1=rs[:, :],
                                                             op0=ALU.mult, op1=ALU.mult),
             reads=[("x", c, tb), "gam"] + rkeys, writes=[("stage", b)])
        outs.append(S.dma("sp", lambda e, c=c, b=b: e.dma_start(out=dst_view[:, c, dcol + t0:dcol + t0 + n], in_=stage[:, b, :]),
                          reads=[("stage", b)], writes=["hwin"], slot=("so", b)))


def build_fused(nc, es, npair=NPAIR):
    C = common_setup(nc, es)
    S = C.S
    C.big = S.sbuf("big", [128, 2 * OWN], F32)
    io = l0_decl(C, nc, XWF)
    w_in = nc.dram_tensor("w_in1", [NPAIR * 10, 128, 1024], F32, kind="ExternalInput").ap()
    w_out = nc.dram_tensor("w_out1", [NPAIR, 128, 1024], F32, kind="ExternalInput").ap()
    dtab = nc.dram_tensor("dtab", [128, 256], F32, kind="ExternalInput").ap()
    yo = nc.dram_tensor("yT", [D_MODEL, OWN], F32, kind="ExternalOutput").ap()
    hwin = nc.dram_tensor("hwin", [D_MODEL, W1], BF16).ap()
    hv = hwin.rearrange("(c p) t -> p c t", p=128)

    Tab = C.big[:, :].bitcast(BF16)[:, 0:2 * NTABBLK * 128].rearrange("p (a c) -> p a c", a=1)
    hb = S.sbuf("hb", [128, 2, NCH, 256], BF16)
    acc = C.big[:, :].rearrange("p (a t) -> p a t", a=2)
    M1 = S.sbuf("M1", [128, 3, 512], BF16)
    D = S.sbuf("D", [128, 256], BF16)
    stage = acc[:, 0, 0:1024].rearrange("p (a c) -> p a c", a=2)
    S.dma("pool", lambda e: e.dma_start(out=D[:], in_=dtab), writes=["D"], slot="c4")

    hst = hb[:, :, 0:2, :].rearrange("p a b c -> p a (b c)")
    hw_outs = []
    for rs, dcol in ((HALO0, 0), (HALO0 + HALO1 + OWN + 0, HALO1 + OWN)):
        l0_pass(C, io, Tab, rs, 8, False, npair=npair)
        emit_norm_out(C, 1, hst, hv, hw_outs, 2, dcol)
    for b_ in range(2):
        S.readers.setdefault(("hb", b_), []).extend(S.readers.get(("stage", b_), []))
        S.last_w.setdefault(("hb", b_), []).extend(S.last_w.get(("stage", b_), []))
    l0_pass(C, io, Tab, HALO0 + HALO1, 16, True, npair=npair)
    emit_norm(C, lambda c, t0, n: (C.xT[:, c, t0:t0 + n], ("x", c, t0 // 512)), OWN, 1,
              lambda c, t0, n: (C.hT[:, c, t0:t0 + n], "hT"))

    blocks = [w_in[p * 10 + k_] for p in range(npair) for k_ in range(10)]
    issued = [0]
    slots = {}

    def wget(i):
        while issued[0] < min(len(blocks), i + 5):
            slots[issued[0]] = load_wblock(C, blocks[issued[0]])
            issued[0] += 1
        return slots[i]

    hbc = [0]
    l1_body(C, npair, wget, w_out, hv, hb, hbc, acc, M1, D)
    flush_outproj(C)
    outs = []
    emit_norm_out(C, 2, stage, yo.rearrange("(c p) t -> p c t", p=128), outs)
    S.final_wait("sp", outs)
    S.emit()
    return C


def l1_body(C, npair, wget, w_out, hv, hb, hbc, acc, M1, D):
    S = C.S
    for p in range(npair):
        _l1_pair(C, p, wget, w_out, hv, hb, hbc, acc, M1, D)


def _l1_pair(C, p, wget, w_out, hv, hb, hbc, acc, M1, D):
    S = C.S
    S.dma("pool", lambda e: e.dma_start(out=C.Wo[:, p % 2, :], in_=w_out[p]), writes=[("Wo", p % 2)], slot=("Wo", p % 2))
    for g, d in enumerate(DILS):
        for hh in range(2):
            slope = 2.0 ** (-8.0 * (2 * p + hh + 1) / 16.0)
            S.op("act", lambda e, g=g, hh=hh, sc=-slope * d: e.activation(
                out=M1[:, g, :].rearrange("p (kl hh q) -> p kl hh q", kl=2, hh=2)[:, :, hh, :],
                in_=D[:, :].rearrange("p (kl q) -> p kl q", kl=2), func=AF.Exp, scale=sc),
                 reads=["D"], writes=["M1"])
    wg = wget(p * 10)
    for tb in range(4):
        proj_block(C, wg, lambda c, tb=tb: (C.hT[:, c, tb * 512:(tb + 1) * 512], "hT"), 512,
                   evac_silu(C, C.GT[:, tb * 512:(tb + 1) * 512], "GT"))
    for g, d in enumerate(DILS):
        _l1_group(C, p, g, d, wget, hv, hb, hbc, acc, M1)
    flush_outproj(C)
    for tb in range(4):
        ts_ = slice(tb * 512, (tb + 1) * 512)
        keys = [("acc", k) for k in range(tb * 4, tb * 4 + 4)]
        S.op("act", lambda e, ts_=ts_: e.activation(out=acc[:, 1, ts_], in_=acc[:, 1, ts_], func=AF.Ln), reads=keys, writes=keys)
        S.op("act", lambda e, ts_=ts_: e.activation(out=acc[:, 1, ts_], in_=acc[:, 1, ts_], func=AF.Exp, scale=-1.0), reads=keys, writes=keys)
        S.op("dve", lambda e, ts_=ts_: e.tensor_tensor(out=acc[:, 0, ts_], in0=acc[:, 0, ts_], in1=acc[:, 1, ts_], op=ALU.mult),
             reads=keys, writes=keys)
        S.op("dve", lambda e, ts_=ts_: e.tensor_tensor(out=C.uT[:, ts_], in0=acc[:, 0, ts_], in1=C.GT[:, ts_], op=ALU.mult),
             reads=keys + ["GT"], writes=[("uT", tb)])
        pend_outproj(C, p, tb)


def _l1_group(C, p, g, d, wget, hv, hb, hbc, acc, M1):
    S = C.S
    halo = 64 * d
    base = HALO1 - halo
    wq, wk, wv = wget(p * 10 + 1 + g * 3), wget(p * 10 + 2 + g * 3), wget(p * 10 + 3 + g * 3)
    hn = min(halo, 256)
    hblks = [(True, w0, hn) for w0 in range(base, HALO1, hn)] + \
            [(True, w0, hn) for w0 in range(HALO1 + OWN, HALO1 + OWN + halo, hn)]
    oblks = [(False, HALO1 + tb * 512, 512) for tb in range(4)]
    blks = []
    per = -(-len(hblks) // 4)
    for i in range(4):
        blks.extend(hblks[i * per:(i + 1) * per])
        blks.append(oblks[i])
    for (is_h, w0, n) in blks:
        col = w0 - base
        if is_h:
            hs_ = hbc[0] % 2
            hbc[0] += 1
            S.dma("sp", lambda e, hs_=hs_, w0=w0, n=n: e.dma_start(out=hb[:, hs_, :, 0:n], in_=hv[:, :, w0:w0 + n]),
                  reads=["hwin"], writes=[("hb", hs_)], slot=("hb", hs_))
            h_of = lambda c, hs_=hs_, n=n: (hb[:, hs_, c, 0:n], ("hb", hs_))
        else:
            t0 = w0 - HALO1
            h_of = lambda c, t0=t0, n=n: (C.hT[:, c, t0:t0 + n], "hT")
        proj_block(C, wk, h_of, n, evac_copy_act(C, C.KT[:, col:col + n], "KT"))
        if is_h:
            side = 0 if w0 < HALO1 else 1
            proj_block(C, wv, h_of, n, evac_scaled_dve(C, C.VT[:, col:col + n], "VT", C.vcol[:, 2 + side:3 + side]))
        else:
            proj_block(C, wv, h_of, n, evac_copy_dve(C, C.VT[:, col:col + n], "VT"))
    nq = OWN // (128 * d)
    nkt = nq + 1
    tiles = []
    for r in range(d):
        for kt in range(nkt):
            c0 = 128 * kt * d + r
            edge = None
            tiles.append((r * nkt + kt, C.VT[:, c0:c0 + 127 * d + 1:d], edge))
    emit_vtrans(C, tiles)
    for tb in range(4):
        proj_block(C, wq, lambda c, tb=tb: (C.hT[:, c, tb * 512:(tb + 1) * 512], "hT"), 512,
                   evac_q(C, tb * 512, 512))
    units = [(r, qt) for r in range(d) for qt in range(nq)]

    def qk(u):
        r, qt = units[u]
        sb, eb = u % 4, u % 4
        q0 = 128 * qt * d + r
        for kl in range(2):
            c0 = 128 * (qt + kl) * d + r
            S.op("pe", lambda e, kl=kl, c0=c0: e.matmul(
                C.Sp[:, sb, kl * 256:(kl + 1) * 256].rearrange("p (a c) -> p a c", a=2),
                lhsT=C.KT[:, c0:c0 + 127 * d + 1:d], rhs=C.QT[:, :, q0:q0 + 127 * d + 1:d], start=True, stop=True),
                reads=["KT", "QT"], writes=[("S", sb)])
        S.op("act", lambda e: e.activation(out=C.E[:, eb, 0:512], in_=C.Sp[:, sb, 0:512], func=AF.Exp, scale=0.125),
             reads=[("S", sb)], writes=[("E", eb)])
        S.op(MASK_ENG if u % 2 == 1 else "dve", lambda e: e.tensor_tensor(out=C.P[:, eb, 0:512], in0=C.E[:, eb, 0:512], in1=M1[:, g, :], op=ALU.mult),
             reads=[("E", eb), "M1"], writes=[("P", eb)])

    def pv(u):
        r, qt = units[u]
        sb = u % 4
        ob = u % 2
        q0 = 128 * qt * d + r
        for kl in range(2):
            kt = qt + kl
            ti = r * nkt + kt
            vsel = 1 if kt == 0 else (2 if kt == nkt - 1 else 0)
            S.op("pe", lambda e, kl=kl, ti=ti: e.matmul(
                C.O[:, ob, 0:256], lhsT=C.V[:, ti, :], rhs=C.P[:, sb, kl * 256:(kl + 1) * 256],
                start=(kl == 0), stop=False), reads=["V", ("P", sb)], writes=[("O", ob)])
            S.op("pe", lambda e, kl=kl, vsel=vsel: e.matmul(
                C.O[:, ob, 256:512], lhsT=C.vm[:, vsel, :], rhs=C.P[:, sb, kl * 256:(kl + 1) * 256],
                start=False, stop=(kl == 1)), reads=["vm", ("P", sb)], writes=[("O", ob)])
        blk_lo, blk_hi = q0 // 128, (q0 + 127 * d) // 128
        keys = [("acc", k) for k in range(blk_lo, blk_hi + 1)]
        for hh in range(2):
            hs = slice(64 * hh, 64 * hh + 64)
            dst = acc[hs, :, q0:q0 + 127 * d + 1:d]
            src = C.O[hs, ob, :].rearrange("p (a b c) -> p a b c", a=2, b=2)[:, :, hh, :]
            if g == 0:
                S.op("dve", lambda e, dst=dst, src=src: e.tensor_copy(out=dst, in_=src), reads=[("O", ob)], writes=keys)
            else:
                S.op("dve", lambda e, dst=dst, src=src: e.tensor_tensor(out=dst, in0=src, in1=dst, op=ALU.add),
                     reads=[("O", ob)] + keys, writes=keys)

    for u0 in range(min(3, len(units))):
        qk(u0)
    for u in range(len(units)):
        if u + 3 < len(units):
            qk(u + 3)
        pv(u)


def wblock(w, col0):
    return np.ascontiguousarray(w[:, col0:col0 + 128].reshape(NCH, 128, 128).transpose(1, 0, 2).reshape(128, 1024))


def gam_layout(*gs):
    return np.ascontiguousarray(np.concatenate([g.reshape(NCH, 128).T for g in gs], axis=1)).astype(np.float32)


def make_cst(j):
    cst = np.zeros((128, CSTW), np.float32)
    cst[:, 0:128] = np.eye(128, dtype=np.float32)
    vl = np.ones(128, np.float32)
    vr = np.ones(128, np.float32)
    if j == 0:
        vl[:64] = 0.0
    if j == 3:
        vr[64:] = 0.0
    cst[:, 128:256] = 1.0
    cst[:, 256:384] = vl[:, None]
    cst[:, 384:512] = vr[:, None]
    cst[:, 512:576] = 1.0
    cst[:, 704:768] = 1.0
    cst[:, 768] = vl
    cst[:, 769] = vr
    cst[:, 770] = 0.0 if j == 0 else 1.0
    cst[:, 771] = 0.0 if j == 3 else 1.0
    return cst


def make_tab(rpb, j):
    kp = np.arange(128)
    kr2, kc = kp // 64, kp % 64
    q = np.arange(128)
    qr2, qc = q // 64, q % 64
    cstart = np.clip(qc - 8, 0, 64 - 16)
    colv = (kc[:, None] >= cstart[None, :]) & (kc[:, None] < cstart[None, :] + 16)
    coff = np.clip(kc[:, None] - qc[None, :] + 15, 0, 30)
    out = np.full((16, NTABBLK, 128, 128), NEG, np.float32)
    for key, (toff, offs) in TAB_OFFS.items():
        lt = 5 if key == "G" else key
        m = 16 * j + lt
        r = 2 * m + qr2
        rs = np.clip(r - 4, 0, 128 - 8)
        for k, o in enumerate(offs):
            krow = 2 * (m + o) + kr2
            rowv = (krow[:, None] >= rs[None, :]) & (krow[:, None] < rs[None, :] + 8) & (krow[:, None] >= 0) & (krow[:, None] < 128)
            valid = rowv & colv
            roff = np.clip(krow[:, None] - r[None, :] + 7, 0, 14)
            vals = rpb[:, roff, coff]
            out[:, toff + k] = np.where(valid[None], vals, np.float32(NEG))
    out = out.reshape(NPAIR, 2, NTABBLK, 128, 128).transpose(0, 3, 2, 1, 4).reshape(NPAIR, 128, 2 * NTABBLK * 128)
    return np.ascontiguousarray(out)


def window_T(xb, t0, halo):
    out = np.zeros((xb.shape[1], OWN + 2 * halo), xb.dtype)
    lo, hi = t0 - halo, t0 + OWN + halo
    a, b = max(lo, 0), min(hi, SEQ)
    out[:, a - lo:b - lo] = xb[a:b].T
    return out


_CACHE = {}


def get_prog(name, builder):
    if name not in _CACHE:
        nc = bass.Bass("TRN2", target_bir_lowering=False)
        es = ExitStack()
        builder(nc, es)
        _CACHE[name] = (nc, es)
    return _CACHE[name][0]


def run_l0(x, norm_0, w_in_0, rpb_0, w_out_0, norm_1, norm_f, trace=False):
    nc = get_prog("l0", build_l0)
    w_in_b = np.stack([wblock(w_in_0, k * 1024 + p * 128) for p in range(NPAIR) for k in range(4)])
    w_out_b = np.ascontiguousarray(w_out_0.reshape(NPAIR, 128, 1024))
    gam = gam_layout(norm_0, norm_1, norm_f)
    tabs = [make_tab(rpb_0, j) for j in range(4)]
    csts = [make_cst(j) for j in range(4)]
    in_maps = []
    for c in range(NCORE):
        b, j = divmod(c, 4)
        in_maps.append({"xw": window_T(x[b], j * OWN, HALO0), "gamd": gam, "w_in0": w_in_b, "w_out0": w_out_b,
                        "tab0": tabs[j], "cst": csts[j]})
    res = run_bass_kernel_spmd(nc, in_maps, core_ids=list(range(NCORE)), trace=trace)
    return res


def make_dtab():
    kp = np.arange(128)[:, None]
    j = np.arange(128)[None, :]
    out = np.empty((128, 256), np.float32)
    for kl, sh in enumerate((-64, 64)):
        delta = np.abs(kp + sh - j)
        out[:, kl * 128:(kl + 1) * 128] = np.where(delta <= 64, delta, 1.0e6)
    return out


def run_l1(x1T_list, h1T_list, w_in_1, w_out_1, norm_0, norm_1, norm_f, trace=False):
    nc = get_prog("l1", build_l1)
    cols = []
    for p in range(NPAIR):
        cols.append(9216 + p * 128)
        for g in range(3):
            for k in range(3):
                cols.append(g * 3072 + k * 1024 + p * 128)
    w_in_b = np.stack([wblock(w_in_1, c0) for c0 in cols])
    w_out_b = np.ascontiguousarray(w_out_1.reshape(NPAIR, 128, 1024))
    gam = gam_layout(norm_0, norm_1, norm_f)
    dt = make_dtab()
    csts = [make_cst(j) for j in range(4)]
    in_maps = []
    for c in range(NCORE):
        b, j = divmod(c, 4)
        hw = np.zeros((D_MODEL, W1), h1T_list[c].dtype)
        hw[:, HALO1:HALO1 + OWN] = h1T_list[c]
        if j > 0:
            hw[:, 0:HALO1] = h1T_list[c - 1][:, OWN - HALO1:]
        if j < 3:
            hw[:, HALO1 + OWN:] = h1T_list[c + 1][:, 0:HALO1]
        in_maps.append({"x1w": x1T_list[c], "h1w": hw, "gamd": gam, "w_in1": w_in_b, "w_out1": w_out_b,
                        "cst": csts[j], "dtab": dt})
    return run_bass_kernel_spmd(nc, in_maps, core_ids=list(range(NCORE)), trace=trace)


def kernel(x, norm_0, w_in_0, rpb_0, w_out_0, norm_1, w_in_1, w_out_1, norm_f):
    f = lambda a: np.ascontiguousarray(np.asarray(a, dtype=np.float32))
    x, norm_0, w_in_0, rpb_0, w_out_0, norm_1, w_in_1, w_out_1, norm_f = map(
        f, (x, norm_0, w_in_0, rpb_0, w_out_0, norm_1, w_in_1, w_out_1, norm_f))
    nc, in_maps = run_fused(x, norm_0, w_in_0, rpb_0, w_out_0, norm_1, w_in_1, w_out_1, norm_f)
    res = run_bass_kernel_spmd(nc, in_maps, core_ids=list(range(NCORE)))
    out = np.empty((2, SEQ, D_MODEL), np.float32)
    for c in range(NCORE):
        b, j = divmod(c, 4)
        out[b, j * OWN:(j + 1) * OWN, :] = np.asarray(res.results[c]["yT"]).T
    return out


def make_hidx(j):
    l = j - 1 if j > 0 else j
    r = j + 1 if j < 3 else j
    p = np.arange(128, dtype=np.int64)
    out = np.zeros((128, NHIDX), np.uint32)
    for side, nbr in enumerate((l, r)):
        for c in range(NCH):
            for blk in range(4):
                tb = 4 + blk if side == 0 else blk
                out[:, side * 32 + c * 4 + blk] = (nbr * D_MODEL + c * 128 + p) * 8 + tb
    return out


def l1_cols():
    cols = []
    for p in range(NPAIR):
        cols.append(9216 + p * 128)
        for g in range(3):
            for k in range(3):
                cols.append(g * 3072 + k * 1024 + p * 128)
    return cols


def run_fused(x, norm_0, w_in_0, rpb_0, w_out_0, norm_1, w_in_1, w_out_1, norm_f, trace=False):
    nc = get_prog("fused", build_fused)
    w_in0_b = np.stack([wblock(w_in_0, k * 1024 + p * 128) for p in range(NPAIR) for k in range(4)])
    w_out0_b = np.ascontiguousarray(w_out_0.reshape(NPAIR, 128, 1024))
    w_in1_b = np.stack([wblock(w_in_1, c0) for c0 in l1_cols()])
    w_out1_b = np.ascontiguousarray(w_out_1.reshape(NPAIR, 128, 1024))
    gam = gam_layout(norm_0, norm_1, norm_f)
    dt = make_dtab()
    tabs = [make_tab(rpb_0, j) for j in range(4)]
    csts = [make_cst(j) for j in range(4)]
    hidx = [make_hidx(j) for j in range(4)]
    in_maps = []
    for c in range(NCORE):
        b, j = divmod(c, 4)
        in_maps.append({"xw": window_T(x[b], j * OWN, HALO0 + HALO1), "gamd": gam, "w_in0": w_in0_b, "w_out0": w_out0_b,
                        "tab0": tabs[j], "cst": csts[j], "w_in1": w_in1_b, "w_out1": w_out1_b, "dtab": dt})
    return nc, in_maps
```

```python
import numpy as np
from contextlib import ExitStack
import concourse.bass as bass
import concourse.mybir as mybir
from concourse.bass_utils import run_bass_kernel_spmd

F32 = mybir.dt.float32
BF16 = mybir.dt.bfloat16
AF = mybir.ActivationFunctionType
ALU = mybir.AluOpType


class Op:
    __slots__ = ("eng", "fn", "deps", "is_dma", "slot", "ord", "marked", "name")

    def __init__(self, eng, fn, is_dma=False, slot=None, name=""):
        self.eng = eng
        self.fn = fn
        self.deps = []
        self.is_dma = is_dma
        self.slot = slot
        self.ord = None
        self.marked = False
        self.name = name


class Sched:
    ENGS = ("pe", "act", "dve", "pool", "sp")

    def __init__(self, nc, es):
        self.nc = nc
        self.es = es
        self.q = {e: [] for e in self.ENGS}
        self.last_w = {}
        self.readers = {}
        self.finals = []
        self.slot_count = {}
        self.n_sb = 0

    def sbuf(self, name, shape, dtype):
        return self.es.enter_context(self.nc.sbuf_tensor(name, shape, dtype))

    def psum(self, name, shape, dtype):
        return self.es.enter_context(self.nc.psum_tensor(name, shape, dtype))

    def _add(self, op, reads, writes, deps):
        dl = []
        for r in reads:
            dl.extend(self.last_w.get(r, ()))
            self.readers.setdefault(r, []).append(op)
        for w_ in writes:
            prev = self.last_w.get(w_, [])
            dl.extend(prev)
            for rd in self.readers.get(w_, ()):
                if rd is not op:
                    dl.append(rd)
            if op.is_dma and prev and all(q.is_dma for q in prev):
                self.last_w[w_] = prev + [op]
            else:
                self.last_w[w_] = [op]
            self.readers[w_] = []
        dl.extend(deps)
        seen = set()
        for d in dl:
            if d is op or id(d) in seen:
                continue
            seen.add(id(d))
            if d.eng == "pe" and op.eng == "pe" and not d.is_dma and not op.is_dma:
                continue
            op.deps.append(d)
        self.q[op.eng].append(op)
        return op

    def op(self, eng, fn, reads=(), writes=(), deps=(), name=""):
        return self._add(Op(eng, fn, name=name), reads, writes, deps)

    def dma(self, eng, fn, reads=(), writes=(), deps=(), slot=None, name=""):
        if slot is None:
            slot = ("auto", len(self.slot_count))
        op = Op(eng, fn, is_dma=True, slot=slot, name=name)
        k = self.slot_count.get(slot, 0) + 1
        self.slot_count[slot] = k
        op.ord = k
        return self._add(op, reads, writes, deps)

    def final_wait(self, eng, ops):
        self.finals.append((eng, list(ops)))

    def emit(self):
        nc = self.nc
        for e in self.ENGS:
            for op in self.q[e]:
                for d in op.deps:
                    d.marked = True
        for _, ops in self.finals:
            for d in ops:
                d.marked = True
        for e in self.ENGS:
            for op in self.q[e]:
                if op.is_dma:
                    op.marked = True
        for e in self.ENGS:
            k = 0
            for op in self.q[e]:
                if not op.is_dma and op.marked:
                    k += 1
                    op.ord = k
        sems = {}

        def sem_of(op):
            key = ("dma", op.slot) if op.is_dma else ("eng", op.eng)
            if key not in sems:
                sems[key] = self.es.enter_context(nc.semaphore("s%d" % len(sems)))
            return sems[key]

        def val_of(op):
            return op.ord * 16 if op.is_dma else op.ord

        for e in self.ENGS:
            for op in self.q[e]:
                if op.marked:
                    sem_of(op)
        finals = {}
        for eng, ops in self.finals:
            finals.setdefault(eng, []).extend(ops)
        self.n_waits = 0

        def run(engname, engine):
            known = {}

            def need(d):
                s = sem_of(d)
                v = val_of(d)
                if known.get(id(s), 0) >= v:
                    return
                known[id(s)] = v
                engine.wait_ge(s, v)
                self.n_waits += 1

            for op in self.q[engname]:
                best = {}
                for d in op.deps:
                    s = sem_of(d)
                    if id(s) not in best or val_of(best[id(s)]) < val_of(d):
                        best[id(s)] = d
                for d in best.values():
                    need(d)
                ins = op.fn(engine)
                if op.marked:
                    ins.then_inc(sem_of(op), 16 if op.is_dma else 1)
            for d in finals.get(engname, ()):
                need(d)

        with nc.Block() as block:
            @block.sync
            def _(eng):
                run("sp", eng)

            @block.tensor
            def _(eng):
                run("pe", eng)

            @block.scalar
            def _(eng):
                run("act", eng)

            @block.vector
            def _(eng):
                run("dve", eng)

            @block.gpsimd
            def _(eng):
                run("pool", eng)


D_MODEL = 1024
SEQ = 8192
NCORE = 8
OWN = 2048
NCH = 8
NPAIR = 8
HALO0 = 256
W0 = OWN + 2 * HALO0
HALO1 = 1024
W1 = OWN + 2 * HALO1
DILS = (1, 4, 16)
NEG = -30000.0
EPS = 1e-6
TAB_OFFS = {"G": (0, (-2, -1, 0, 1, 2)), 0: (5, (-2, -1, 0, 1, 2, 3)), 1: (11, (-2, -1, 0, 1, 2)),
            14: (16, (-2, -1, 0, 1, 2)), 15: (21, (-3, -2, -1, 0, 1, 2))}
NTABBLK = 27
U32 = mybir.dt.uint32
NHIDX = 64
MASK_ENG = "pool"
FIN_ENG = "pool"
CSTW = 128 + 5 * 128 + 4


def tab_for(lt):
    return TAB_OFFS[lt] if lt in TAB_OFFS else TAB_OFFS["G"]


class Ctx:
    pass


def common_setup(nc, es, hcols=W0):
    C = Ctx()
    S = Sched(nc, es)
    C.S = S
    C.nc = nc
    C.G = S.psum("G", [128, 2, 512], F32)
    C.Sp = S.psum("Sp", [128, 4, 512], F32)
    C.O = S.psum("O", [128, 2, 512], F32)
    C.gi = 0
    C.xT = S.sbuf("xT", [128, NCH, OWN], F32)
    C.hT = S.sbuf("hT", [128, NCH, hcols], BF16)
    C.QT = S.sbuf("QT", [128, 2, OWN], BF16)
    C.KV = S.sbuf("KV", [128, 2 * W1], BF16)
    C.KT = C.KV[:, 0:W1]
    C.VT = C.KV[:, W1:2 * W1]
    C.V = S.sbuf("V", [128, 32, 128], BF16)
    C.GT = S.sbuf("GT", [128, OWN], BF16)
    C.uT = S.sbuf("uT", [128, OWN], BF16)
    C.Wb = S.sbuf("Wb", [128, 8, 1024], BF16)
    C.Wo = S.sbuf("Wo", [128, 2, 1024], BF16)
    C.E = S.sbuf("E", [128, 4, 512], BF16)
    C.P = S.sbuf("P", [128, 4, 512], BF16)
    C.sq = S.sbuf("sq", [128, 2, 512], BF16)
    C.rstd = S.sbuf("rstd", [128, 512], F32)
    C.tmp = S.sbuf("tmp", [128, 4, 128], F32)
    C.onesf = S.sbuf("onesf", [128, 128], BF16)
    C.vm = S.sbuf("vm", [128, 5, 128], BF16)
    C.vcol = S.sbuf("vcol", [128, 4], F32)
    C.gam = S.sbuf("gam", [128, 3, NCH], F32)
    C.wslot = 0
    C.pend = []
    C.oi = 0
    C.pops = 2
    C.nrm = 0
    C.epsb = S.sbuf("epsb", [128, 1], F32)
    C.ident = S.sbuf("ident", [128, 128], BF16)
    S.op("dve", lambda e: e.memset(C.epsb[:], EPS), writes=["epsb"])
    S.op("dve", lambda e: e.memset(C.onesf[:], 1.0), writes=["onesf"])
    S.op("pool", lambda e: e.memset(C.QT[:], 0.0), writes=["QT"])
    return C


def next_g(C):
    gi = C.gi
    C.gi = (gi + 1) % 2
    return gi


def load_wblock(C, src_ap):
    S = C.S
    s = C.wslot
    C.wslot = (s + 1) % 8
    S.dma("pool", lambda e: e.dma_start(out=C.Wb[:, s, :], in_=src_ap), writes=[("W", s)], slot=("W", s))
    return s


def proj_block(C, ws, h_of, n, evac):
    S = C.S
    gi = next_g(C)
    gk = ("G", gi)
    for c in range(NCH):
        rhs, rk = h_of(c)
        S.op("pe", lambda e, c=c, rhs=rhs: e.matmul(C.G[:, gi, 0:n], lhsT=C.Wb[:, ws, c * 128:(c + 1) * 128],
                                                    rhs=rhs, start=(c == 0), stop=(c == NCH - 1)),
             reads=[("W", ws), rk], writes=[gk])
    evac(C.G[:, gi, 0:n], gk)
    pop_outproj(C, C.pops)


def emit_norm(C, x_of, ntok, gidx, out_of):
    t0 = 0
    while t0 < ntok:
        n = min(512, ntok - t0)
        _norm_block(C, x_of, t0, n, gidx, out_of)
        t0 += n


def rstd_buf(C):
    k = C.nrm % 2
    C.nrm += 1
    if k == 0:
        return C.rstd[:, :], [("rstd", 0)]
    return C.tmp[:, :, :].rearrange("p a c -> p (a c)"), [("rstd", 1)] + [("tmp", i) for i in range(4)]


def _norm_block(C, x_of, t0, n, gidx, out_of):
    S = C.S
    gi = next_g(C)
    gk = ("G", gi)
    rs, rkeys = rstd_buf(C)
    for c in range(NCH):
        xa, xk = x_of(c, t0, n)
        sb = c % 2
        S.op("act", lambda e, xa=xa, sb=sb: e.activation(out=C.sq[:, sb, 0:n], in_=xa, func=AF.Square),
             reads=[xk], writes=[("sq", sb)])
        S.op("pe", lambda e, c=c, sb=sb: e.matmul(C.G[:, gi, 0:n], lhsT=C.onesf[:, :], rhs=C.sq[:, sb, 0:n],
                                                  start=(c == 0), stop=(c == NCH - 1)),
             reads=["onesf", ("sq", sb)], writes=[gk])
    S.op("act", lambda e: e.activation(out=rs[:, 0:n], in_=C.G[:, gi, 0:n], func=AF.Ln,
                                       scale=1.0 / D_MODEL, bias=C.epsb[:, 0:1]),
         reads=[gk, "epsb"], writes=rkeys)
    S.op("act", lambda e: e.activation(out=rs[:, 0:n], in_=rs[:, 0:n], func=AF.Exp, scale=-0.5), reads=rkeys, writes=rkeys)
    for c in range(NCH):
        xa, xk = x_of(c, t0, n)
        oa, ok = out_of(c, t0, n)
        if False:
            S.op("pool", lambda e, xa=xa, oa=oa, c=c: e.scalar_tensor_tensor(
                out=oa, in0=xa, scalar=C.gam[:, gidx, c:c + 1], in1=rs[:, 0:n], op0=ALU.mult, op1=ALU.mult),
                 reads=[xk, "gam"] + rkeys, writes=[ok])
        else:
            S.op("dve", lambda e, xa=xa, oa=oa, c=c: e.scalar_tensor_tensor(
                out=oa, in0=xa, scalar=C.gam[:, gidx, c:c + 1], in1=rs[:, 0:n], op0=ALU.mult, op1=ALU.mult),
                 reads=[xk, "gam"] + rkeys, writes=[ok])


def emit_outproj(C, pair, wo_src, tbs, ms=None):
    S = C.S
    for tb in tbs:
        for m in (range(NCH) if ms is None else ms):
            gi = C.oi
            C.oi = (gi + 1) % 4
            gk = ("S", gi)
            S.op("pe", lambda e, m=m, gi=gi, tb=tb: e.matmul(C.Sp[:, gi, :], lhsT=C.Wo[:, pair % 2, m * 128:(m + 1) * 128],
                                                          rhs=C.uT[:, tb * 512:(tb + 1) * 512], start=True, stop=True),
                 reads=[("Wo", pair % 2), ("uT", tb)], writes=[gk])
            S.op("dve", lambda e, m=m, gi=gi, tb=tb: e.tensor_tensor(out=C.xT[:, m, tb * 512:(tb + 1) * 512],
                                                                 in0=C.Sp[:, gi, :], in1=C.xT[:, m, tb * 512:(tb + 1) * 512],
                                                                 op=ALU.add),
                 reads=[gk, ("x", m, tb)], writes=[("x", m, tb)])


def pend_outproj(C, pair, tb):
    for m in range(NCH):
        C.pend.append((pair, tb, m))


def pop_outproj(C, n=1):
    for _ in range(n):
        if C.pend:
            pair, tb, m = C.pend.pop(0)
            emit_outproj(C, pair, None, [tb], [m])


def flush_outproj(C):
    pop_outproj(C, len(C.pend))


def evac_copy_act(C, dst, dkey):
    def f(ps, gk):
        C.S.op("act", lambda e: e.activation(out=dst, in_=ps, func=AF.Copy), reads=[gk], writes=[dkey])
    return f


def evac_copy_dve(C, dst, dkey):
    def f(ps, gk):
        C.S.op("dve", lambda e: e.tensor_copy(out=dst, in_=ps), reads=[gk], writes=[dkey])
    return f


def evac_q(C, c0, n):
    def f(ps, gk):
        C.S.op("act", lambda e: e.activation(out=C.QT[0:64, 0, c0:c0 + n], in_=ps[0:64, :], func=AF.Copy), reads=[gk], writes=["QT"])
        C.S.op("act", lambda e: e.activation(out=C.QT[64:128, 1, c0:c0 + n], in_=ps[64:128, :], func=AF.Copy), reads=[gk], writes=["QT"])
    return f


def evac_scaled_dve(C, dst, dkey, flag_ap):
    def f(ps, gk):
        C.S.op("dve", lambda e: e.tensor_scalar(out=dst, in0=ps, scalar1=flag_ap, scalar2=None, op0=ALU.mult), reads=[gk, "vcol"], writes=[dkey])
    return f


def evac_silu(C, dst, dkey):
    def f(ps, gk):
        C.S.op("act", lambda e: e.activation(out=dst, in_=ps, func=AF.Silu), reads=[gk], writes=[dkey])
    return f


def emit_vtrans(C, tiles):
    for i in range(0, len(tiles), 4):
        _vtrans_group(C, tiles[i:i + 4])


def _vtrans_group(C, grp):
    S = C.S
    gi = next_g(C)
    gk = ("G", gi)
    gb = C.G[:, gi, :].bitcast(BF16)
    n = len(grp)
    ti0 = grp[0][0]
    assert [t[0] for t in grp] == list(range(ti0, ti0 + n))
    for j, (ti, src, edge) in enumerate(grp):
        S.op("pe", lambda e, j=j, src=src: e.transpose(gb[:, j * 128:(j + 1) * 128], src, C.ident[:, :]),
             reads=["VT", "ident"], writes=[gk])
    S.op("act", lambda e: e.activation(out=C.V[:, ti0:ti0 + n, :].rearrange("p a c -> p (a c)"), in_=gb[:, 0:n * 128], func=AF.Copy),
         reads=[gk], writes=["V"])


XWF = OWN + 2 * (HALO1 + HALO0)


def l0_decl(C, nc, xwin):
    S = C.S
    io = Ctx()
    io.xw = nc.dram_tensor("xw", [D_MODEL, xwin], F32, kind="ExternalInput").ap()
    io.gam = nc.dram_tensor("gamd", [128, 3 * NCH], F32, kind="ExternalInput").ap()
    io.w_in = nc.dram_tensor("w_in0", [NPAIR * 4, 128, 1024], F32, kind="ExternalInput").ap()
    io.w_out = nc.dram_tensor("w_out0", [NPAIR, 128, 1024], F32, kind="ExternalInput").ap()
    io.tab = nc.dram_tensor("tab0", [NPAIR, 128, 2 * NTABBLK * 128], F32, kind="ExternalInput").ap()
    io.cst = nc.dram_tensor("cst", [128, CSTW], F32, kind="ExternalInput").ap()
    io.xv = io.xw.rearrange("(c p) t -> p c t", p=128)
    io.xh = C.KV[:, :].bitcast(F32).rearrange("p (c t) -> p c t", c=NCH)
    cst = io.cst
    S.dma("sp", lambda e: e.dma_start(out=C.gam[:].rearrange("p a c -> p (a c)"), in_=io.gam), writes=["gam"], slot="c0")
    S.dma("pool", lambda e: e.dma_start(out=C.ident[:], in_=cst[:, 0:128]), writes=["ident"], slot="c1")
    S.dma("pool", lambda e: e.dma_start(out=C.vm[:].rearrange("p a c -> p (a c)"), in_=cst[:, 128:768]), writes=["vm"], slot="c2")
    S.dma("sp", lambda e: e.dma_start(out=C.vcol[:], in_=cst[:, 768:772]), writes=["vcol"], slot="c3")
    return io


def l0_pass(C, io, Tab, rs, NT, special, npair=NPAIR):
    S = C.S
    xv, xh = io.xv, io.xh
    NQ = NT * 128
    WIN = NQ + 2 * HALO0
    NTB = NQ // 512
    for c in range(NCH):
        S.dma("sp", lambda e, c=c: e.dma_start(out=C.xT[:, c, 0:NQ], in_=xv[:, c, rs:rs + NQ]),
              writes=[("x", c, tb) for tb in range(NTB)], slot=("xl", c))
    S.dma("sp", lambda e: e.dma_start(out=xh[:, :, 0:HALO0], in_=xv[:, :, rs - HALO0:rs]), writes=["xh", "KT", "VT"], slot="xh0")
    S.dma("sp", lambda e: e.dma_start(out=xh[:, :, HALO0:2 * HALO0], in_=xv[:, :, rs + NQ:rs + NQ + HALO0]),
          writes=["xh", "KT", "VT"], slot="xh1")

    wslots = {}

    def prefetch(p):
        if p >= npair:
            return
        wslots[p] = [load_wblock(C, io.w_in[p * 4 + k]) for k in range(4)]

    def load_wo(p):
        S.dma("pool", lambda e: e.dma_start(out=C.Wo[:, p % 2, :], in_=io.w_out[p]), writes=[("Wo", p % 2)], slot=("Wo", p % 2))

    def prefetch_tab(p):
        S.dma("pool", lambda e: e.dma_start(out=Tab[:, 0, :], in_=io.tab[p]), writes=[("Tab", k_) for k_ in TAB_OFFS], slot=("Tab", 0))

    prefetch(0)

    emit_norm(C, lambda c, t0, n: (C.xT[:, c, t0:t0 + n], ("x", c, t0 // 512)), NQ, 0,
              lambda c, t0, n: (C.hT[:, c, HALO0 + t0:HALO0 + t0 + n], "hT"))
    for half in range(2):
        emit_norm(C, lambda c, t0, n, half=half: (xh[:, c, half * HALO0 + t0:half * HALO0 + t0 + n], "xh"), HALO0, 0,
                  lambda c, t0, n, half=half: (C.hT[:, c, half * (HALO0 + NQ) + t0:half * (HALO0 + NQ) + t0 + n], "hT"))

    hkey = "hT"
    for p in range(npair):
        prefetch(p + 1)
        load_wo(p)
        prefetch_tab(p)
        _l0_pair(C, p, wslots[p], Tab, NT, WIN, NTB, special)
    flush_outproj(C)


def _l0_pair(C, p, ws, Tab, NT, WIN, NTB, special):
    S = C.S
    hkey = "hT"
    wq, wk, wv, wg = ws
    for tb in range(WIN // 512):
        proj_block(C, wk, lambda c, tb=tb: (C.hT[:, c, tb * 512:(tb + 1) * 512], hkey), 512,
                   evac_copy_act(C, C.KT[:, tb * 512:(tb + 1) * 512], "KT"))
    for tb in range(WIN // 512):
        proj_block(C, wv, lambda c, tb=tb: (C.hT[:, c, tb * 512:(tb + 1) * 512], hkey), 512,
                   evac_copy_dve(C, C.VT[:, tb * 512:(tb + 1) * 512], "VT"))
    emit_vtrans(C, [(wt, C.VT[:, wt * 128:(wt + 1) * 128], None) for wt in range(WIN // 128)])
    for tb in range(NTB):
        proj_block(C, wq, lambda c, tb=tb: (C.hT[:, c, HALO0 + tb * 512:HALO0 + (tb + 1) * 512], hkey), 512,
                   evac_q(C, tb * 512, 512))
    for tb in range(NTB):
        proj_block(C, wg, lambda c, tb=tb: (C.hT[:, c, HALO0 + tb * 512:HALO0 + (tb + 1) * 512], hkey), 512,
                   evac_silu(C, C.GT[:, tb * 512:(tb + 1) * 512], "GT"))
    for key in ((0, 1, "G", 14, 15) if special else ("G",)):
        toff_, offs_ = TAB_OFFS[key]
        a_, b_ = toff_ * 256, (toff_ + len(offs_)) * 256
        S.op("act", lambda e, a_=a_, b_=b_: e.activation(out=Tab[:, 0, a_:b_], in_=Tab[:, 0, a_:b_], func=AF.Exp),
             reads=[("Tab", key)], writes=[("Tab", key)])
    tabf = (lambda lt: tab_for(lt)) if special else (lambda lt: TAB_OFFS["G"])
    units = []
    for lt in range(NT):
        toff, offs = tabf(lt)
        for i in range(0, len(offs), 2):
            units.append((lt, toff, offs[i:i + 2], i, i == 0, i + 2 >= len(offs)))
    tkey = (lambda lt: lt if lt in TAB_OFFS else "G") if special else (lambda lt: "G")

    def qk(u):
        lt, toff, offs, kbase, first, last = units[u]
        sb, eb = u % 4, u % 4
        n = len(offs)
        for k, o in enumerate(offs):
            wt = lt + 2 + o
            S.op("pe", lambda e, wt=wt, k=k: e.matmul(
                C.Sp[:, sb, k * 256:(k + 1) * 256].rearrange("p (a c) -> p a c", a=2), lhsT=C.KT[:, wt * 128:(wt + 1) * 128],
                rhs=C.QT[:, :, lt * 128:(lt + 1) * 128], start=True, stop=True),
                reads=["KT", "QT"], writes=[("S", sb)])
        S.op("act", lambda e: e.activation(out=C.E[:, eb, 0:n * 256], in_=C.Sp[:, sb, 0:n * 256], func=AF.Exp, scale=0.125),
             reads=[("S", sb)], writes=[("E", eb)])
        t0 = (toff + kbase) * 256
        S.op(MASK_ENG if u % 3 == 2 else "dve", lambda e: e.tensor_tensor(out=C.P[:, eb, 0:n * 256], in0=C.E[:, eb, 0:n * 256],
                                                 in1=Tab[:, 0, t0:t0 + n * 256], op=ALU.mult),
             reads=[("E", eb), ("Tab", tkey(lt))], writes=[("P", eb)])

    def pv(u):
        lt, toff, offs, kbase, first, last = units[u]
        eb = u % 4
        ob = lt % 2
        for k, o in enumerate(offs):
            wt = lt + 2 + o
            S.op("pe", lambda e, wt=wt, k=k: e.matmul(
                C.O[:, ob, 0:256], lhsT=C.V[:, wt, :], rhs=C.P[:, eb, k * 256:(k + 1) * 256],
                start=(first and k == 0), stop=False), reads=["V", ("P", eb)], writes=[("O", ob)])
            for hh in range(2):
                S.op("pe", lambda e, k=k, hh=hh, fin=(last and k == len(offs) - 1 and hh == 1): e.matmul(
                    C.O[:, ob, 256:384], lhsT=C.vm[:, 3 + hh, :], rhs=C.P[:, eb, k * 256 + hh * 128:k * 256 + (hh + 1) * 128],
                    start=False, stop=fin), reads=["vm", ("P", eb)], writes=[("O", ob)])
        if last:
            td, tn = 2 * (lt % 2), 2 * (lt % 2) + 1
            S.op("act", lambda e: e.activation(out=C.tmp[:, td, :], in_=C.O[:, ob, 256:384], func=AF.Ln), reads=[("O", ob)], writes=[("tmp", td)])
            S.op("act", lambda e: e.activation(out=C.tmp[:, td, :], in_=C.tmp[:, td, :], func=AF.Exp, scale=-1.0), reads=[("tmp", td)], writes=[("tmp", td)])
            for hh in range(2):
                hs = slice(64 * hh, 64 * hh + 64)
                S.op("dve", lambda e, hs=hs, hh=hh: e.tensor_tensor(out=C.tmp[hs, tn, :], in0=C.O[hs, ob, hh * 128:(hh + 1) * 128],
                                                                 in1=C.tmp[hs, td, :], op=ALU.mult),
                     reads=[("O", ob), ("tmp", td)], writes=[("tmp", tn)])
            S.op(FIN_ENG, lambda e: e.tensor_tensor(out=C.uT[:, lt * 128:(lt + 1) * 128], in0=C.tmp[:, tn, :],
                                                    in1=C.GT[:, lt * 128:(lt + 1) * 128], op=ALU.mult),
                 reads=[("tmp", tn), "GT"], writes=[("uT", lt // 4)])
            if lt % 4 == 3:
                pend_outproj(C, p, lt // 4)

    for u0 in range(min(3, len(units))):
        qk(u0)
    for u in range(len(units)):
        if u + 3 < len(units):
            qk(u + 3)
        pv(u)


def build_l0(nc, es, npair=NPAIR):
    C = common_setup(nc, es)
    S = C.S
    io = l0_decl(C, nc, W0)
    x1o = nc.dram_tensor("x1T", [D_MODEL, OWN], F32, kind="ExternalOutput").ap()
    h1o = nc.dram_tensor("h1T", [D_MODEL, OWN], BF16, kind="ExternalOutput").ap()
    Tab = S.sbuf("Tab", [128, 1, 2 * NTABBLK * 128], BF16)
    hout = S.sbuf("hout", [128, 2, 512], BF16)
    l0_pass(C, io, Tab, HALO0, 16, True, npair=npair)
    outs = []
    for c in range(NCH):
        outs.append(S.dma("sp", lambda e, c=c: e.dma_start(out=x1o.rearrange("(c p) t -> p c t", p=128)[:, c, :], in_=C.xT[:, c, :]),
                          reads=[("x", c, tb) for tb in range(4)], slot=("xo", c)))
    hov = h1o.rearrange("(c p) t -> p c t", p=128)
    emit_norm_out(C, 1, hout, hov, outs, 4)
    S.final_wait("sp", outs)
    S.emit()
    return C


def emit_norm_out(C, gidx, stage, dst_view, outs, ntb=4, dcol=0):
    k = [0]
    for tb in range(ntb):
        emit_norm_block_out(C, tb, gidx, stage, dst_view, outs, k, dcol)


def emit_norm_block_out(C, tb, gidx, stage, dst_view, outs, k, dcol=0):
    S = C.S
    n = 512
    t0 = tb * 512
    gi = next_g(C)
    gk = ("G", gi)
    for c in range(NCH):
        sb = c % 2
        S.op("act", lambda e, c=c, sb=sb: e.activation(out=C.sq[:, sb, :], in_=C.xT[:, c, t0:t0 + n], func=AF.Square),
             reads=[("x", c, tb)], writes=[("sq", sb)])
        S.op("pe", lambda e, c=c, sb=sb: e.matmul(C.G[:, gi, :], lhsT=C.onesf[:, :], rhs=C.sq[:, sb, :],
                                                  start=(c == 0), stop=(c == NCH - 1)),
             reads=["onesf", ("sq", sb)], writes=[gk])
    rs, rkeys = rstd_buf(C)
    S.op("act", lambda e: e.activation(out=rs[:, :], in_=C.G[:, gi, :], func=AF.Ln, scale=1.0 / D_MODEL, bias=C.epsb[:, 0:1]),
         reads=[gk, "epsb"], writes=rkeys)
    S.op("act", lambda e: e.activation(out=rs[:, :], in_=rs[:, :], func=AF.Exp, scale=-0.5), reads=rkeys, writes=rkeys)
    for c in range(NCH):
        b = k[0] % 2
        k[0] += 1
        S.op("dve", lambda e, c=c, b=b: e.scalar_tensor_tensor(out=stage[:, b, :], in0=C.xT[:, c, t0:t0 + n],
                                                             scalar=C.gam[:, gidx, c:c + 1], in1=rs[:, :],
                                                             op0=ALU.mult, op1=ALU.mult),
             reads=[("x", c, tb), "gam"] + rkeys, writes=[("stage", b)])
        outs.append(S.dma("sp", lambda e, c=c, b=b: e.dma_start(out=dst_view[:, c, dcol + t0:dcol + t0 + n], in_=stage[:, b, :]),
                          reads=[("stage", b)], writes=["hwin"], slot=("so", b)))


def build_fused(nc, es, npair=NPAIR):
    C = common_setup(nc, es)
    S = C.S
    C.big = S.sbuf("big", [128, 2 * OWN], F32)
    io = l0_decl(C, nc, XWF)
    w_in = nc.dram_tensor("w_in1", [NPAIR * 10, 128, 1024], F32, kind="ExternalInput").ap()
    w_out = nc.dram_tensor("w_out1", [NPAIR, 128, 1024], F32, kind="ExternalInput").ap()
    dtab = nc.dram_tensor("dtab", [128, 256], F32, kind="ExternalInput").ap()
    yo = nc.dram_tensor("yT", [D_MODEL, OWN], F32, kind="ExternalOutput").ap()
    hwin = nc.dram_tensor("hwin", [D_MODEL, W1], BF16).ap()
    hv = hwin.rearrange("(c p) t -> p c t", p=128)

    Tab = C.big[:, :].bitcast(BF16)[:, 0:2 * NTABBLK * 128].rearrange("p (a c) -> p a c", a=1)
    hb = S.sbuf("hb", [128, 2, NCH, 256], BF16)
    acc = C.big[:, :].rearrange("p (a t) -> p a t", a=2)
    M1 = S.sbuf("M1", [128, 3, 512], BF16)
    D = S.sbuf("D", [128, 256], BF16)
    stage = acc[:, 0, 0:1024].rearrange("p (a c) -> p a c", a=2)
    S.dma("pool", lambda e: e.dma_start(out=D[:], in_=dtab), writes=["D"], slot="c4")

    hst = hb[:, :, 0:2, :].rearrange("p a b c -> p a (b c)")
    hw_outs = []
    for rs, dcol in ((HALO0, 0), (HALO0 + HALO1 + OWN + 0, HALO1 + OWN)):
        l0_pass(C, io, Tab, rs, 8, False, npair=npair)
        emit_norm_out(C, 1, hst, hv, hw_outs, 2, dcol)
    for b_ in range(2):
        S.readers.setdefault(("hb", b_), []).extend(S.readers.get(("stage", b_), []))
        S.last_w.setdefault(("hb", b_), []).extend(S.last_w.get(("stage", b_), []))
    l0_pass(C, io, Tab, HALO0 + HALO1, 16, True, npair=npair)
    emit_norm(C, lambda c, t0, n: (C.xT[:, c, t0:t0 + n], ("x", c, t0 // 512)), OWN, 1,
              lambda c, t0, n: (C.hT[:, c, t0:t0 + n], "hT"))

    blocks = [w_in[p * 10 + k_] for p in range(npair) for k_ in range(10)]
    issued = [0]
    slots = {}

    def wget(i):
        while issued[0] < min(len(blocks), i + 5):
            slots[issued[0]] = load_wblock(C, blocks[issued[0]])
            issued[0] += 1
        return slots[i]

    hbc = [0]
    l1_body(C, npair, wget, w_out, hv, hb, hbc, acc, M1, D)
    flush_outproj(C)
    outs = []
    emit_norm_out(C, 2, stage, yo.rearrange("(c p) t -> p c t", p=128), outs)
    S.final_wait("sp", outs)
    S.emit()
    return C


def l1_body(C, npair, wget, w_out, hv, hb, hbc, acc, M1, D):
    S = C.S
    for p in range(npair):
        _l1_pair(C, p, wget, w_out, hv, hb, hbc, acc, M1, D)


def _l1_pair(C, p, wget, w_out, hv, hb, hbc, acc, M1, D):
    S = C.S
    S.dma("pool", lambda e: e.dma_start(out=C.Wo[:, p % 2, :], in_=w_out[p]), writes=[("Wo", p % 2)], slot=("Wo", p % 2))
    for g, d in enumerate(DILS):
        for hh in range(2):
            slope = 2.0 ** (-8.0 * (2 * p + hh + 1) / 16.0)
            S.op("act", lambda e, g=g, hh=hh, sc=-slope * d: e.activation(
                out=M1[:, g, :].rearrange("p (kl hh q) -> p kl hh q", kl=2, hh=2)[:, :, hh, :],
                in_=D[:, :].rearrange("p (kl q) -> p kl q", kl=2), func=AF.Exp, scale=sc),
                 reads=["D"], writes=["M1"])
    wg = wget(p * 10)
    for tb in range(4):
        proj_block(C, wg, lambda c, tb=tb: (C.hT[:, c, tb * 512:(tb + 1) * 512], "hT"), 512,
                   evac_silu(C, C.GT[:, tb * 512:(tb + 1) * 512], "GT"))
    for g, d in enumerate(DILS):
        _l1_group(C, p, g, d, wget, hv, hb, hbc, acc, M1)
    flush_outproj(C)
    uTp = C.uT[:, :].rearrange("p (j r) -> p r j", r=16)
    GTp = C.GT[:, :].rearrange("p (j r) -> p r j", r=16)
    for ch in range(4):
        ts_ = slice(ch * 512, (ch + 1) * 512)
        keys = [("acc", k) for k in range(ch * 4, ch * 4 + 4)]
        S.op("act", lambda e, ts_=ts_: e.activation(out=acc[:, 1, ts_], in_=acc[:, 1, ts_], func=AF.Ln), reads=keys, writes=keys)
        S.op("act", lambda e, ts_=ts_: e.activation(out=acc[:, 1, ts_], in_=acc[:, 1, ts_], func=AF.Exp, scale=-1.0), reads=keys, writes=keys)
        S.op("dve", lambda e, ts_=ts_: e.tensor_tensor(out=acc[:, 0, ts_], in0=acc[:, 0, ts_], in1=acc[:, 1, ts_], op=ALU.mult),
             reads=keys, writes=keys)
        S.op("dve", lambda e, ts_=ts_, ch=ch: e.tensor_tensor(out=uTp[:, 4 * ch:4 * ch + 4, :],
                                                            in0=acc[:, 0, ts_].rearrange("p (r j) -> p r j", r=4),
                                                            in1=GTp[:, 4 * ch:4 * ch + 4, :], op=ALU.mult),
             reads=keys + ["GT"], writes=[("uT", tb) for tb in range(4)])
    for tb in range(4):
        pend_outproj(C, p, tb)


def _l1_group(C, p, g, d, wget, hv, hb, hbc, acc, M1):
    S = C.S
    halo = 64 * d
    base = HALO1 - halo
    wq, wk, wv = wget(p * 10 + 1 + g * 3), wget(p * 10 + 2 + g * 3), wget(p * 10 + 3 + g * 3)
    hn = min(halo, 256)
    hblks = [(True, w0, hn) for w0 in range(base, HALO1, hn)] + \
            [(True, w0, hn) for w0 in range(HALO1 + OWN, HALO1 + OWN + halo, hn)]
    oblks = [(False, HALO1 + tb * 512, 512) for tb in range(4)]
    blks = []
    per = -(-len(hblks) // 4)
    for i in range(4):
        blks.extend(hblks[i * per:(i + 1) * per])
        blks.append(oblks[i])
    for (is_h, w0, n) in blks:
        col = w0 - base
        if is_h:
            hs_ = hbc[0] % 2
            hbc[0] += 1
            S.dma("sp", lambda e, hs_=hs_, w0=w0, n=n: e.dma_start(out=hb[:, hs_, :, 0:n], in_=hv[:, :, w0:w0 + n]),
                  reads=["hwin"], writes=[("hb", hs_)], slot=("hb", hs_))
            h_of = lambda c, hs_=hs_, n=n: (hb[:, hs_, c, 0:n], ("hb", hs_))
        else:
            t0 = w0 - HALO1
            h_of = lambda c, t0=t0, n=n: (C.hT[:, c, t0:t0 + n], "hT")
        proj_block(C, wk, h_of, n, evac_copy_act(C, C.KT[:, col:col + n], "KT"))
        if is_h:
            side = 0 if w0 < HALO1 else 1
            proj_block(C, wv, h_of, n, evac_scaled_dve(C, C.VT[:, col:col + n], "VT", C.vcol[:, 2 + side:3 + side]))
        else:
            proj_block(C, wv, h_of, n, evac_copy_dve(C, C.VT[:, col:col + n], "VT"))
    nq = OWN // (128 * d)
    nkt = nq + 1
    tiles = []
    for r in range(d):
        for kt in range(nkt):
            c0 = 128 * kt * d + r
            edge = None
            tiles.append((r * nkt + kt, C.VT[:, c0:c0 + 127 * d + 1:d], edge))
    emit_vtrans(C, tiles)
    for tb in range(4):
        proj_block(C, wq, lambda c, tb=tb: (C.hT[:, c, tb * 512:(tb + 1) * 512], "hT"), 512,
                   evac_q(C, tb * 512, 512))
    units = [(r, qt) for r in range(d) for qt in range(nq)]

    def qk(u):
        r, qt = units[u]
        sb, eb = u % 4, u % 4
        q0 = 128 * qt * d + r
        for kl in range(2):
            c0 = 128 * (qt + kl) * d + r
            S.op("pe", lambda e, kl=kl, c0=c0: e.matmul(
                C.Sp[:, sb, kl * 256:(kl + 1) * 256].rearrange("p (a c) -> p a c", a=2),
                lhsT=C.KT[:, c0:c0 + 127 * d + 1:d], rhs=C.QT[:, :, q0:q0 + 127 * d + 1:d], start=True, stop=True),
                reads=["KT", "QT"], writes=[("S", sb)])
        S.op("act", lambda e: e.activation(out=C.E[:, eb, 0:512], in_=C.Sp[:, sb, 0:512], func=AF.Exp, scale=0.125),
             reads=[("S", sb)], writes=[("E", eb)])
        S.op(MASK_ENG if u % 2 == 1 else "dve", lambda e: e.tensor_tensor(out=C.P[:, eb, 0:512], in0=C.E[:, eb, 0:512], in1=M1[:, g, :], op=ALU.mult),
             reads=[("E", eb), "M1"], writes=[("P", eb)])

    def pv(u):
        r, qt = units[u]
        sb = u % 4
        ob = u % 2
        q0 = 128 * qt * d + r
        for kl in range(2):
            kt = qt + kl
            ti = r * nkt + kt
            vsel = 1 if kt == 0 else (2 if kt == nkt - 1 else 0)
            S.op("pe", lambda e, kl=kl, ti=ti: e.matmul(
                C.O[:, ob, 0:256], lhsT=C.V[:, ti, :], rhs=C.P[:, sb, kl * 256:(kl + 1) * 256],
                start=(kl == 0), stop=False), reads=["V", ("P", sb)], writes=[("O", ob)])
            S.op("pe", lambda e, kl=kl, vsel=vsel: e.matmul(
                C.O[:, ob, 256:512], lhsT=C.vm[:, vsel, :], rhs=C.P[:, sb, kl * 256:(kl + 1) * 256],
                start=False, stop=(kl == 1)), reads=["vm", ("P", sb)], writes=[("O", ob)])
        A4 = acc.rearrange("p a (r j) -> p a r j", r=16)
        if d == 16:
            rows = [r]
        elif d == 4:
            rows = [r + 4 * b for b in range(4)]
        else:
            rows = list(range(16))
        keys = [("acc", k) for k in rows]
        for hh in range(2):
            hs = slice(64 * hh, 64 * hh + 64)
            srcv = C.O[hs, ob, :].rearrange("p (a b c) -> p a b c", a=2, b=2)[:, :, hh, :]
            if d == 16:
                dst, src = A4[hs, :, r, :], srcv
            elif d == 4:
                dst = A4[hs, :, r:r + 13:4, 32 * qt:32 * qt + 32]
                src = srcv.rearrange("p a (x b) -> p a b x", b=4)
            else:
                dst = A4[hs, :, :, 8 * qt:8 * qt + 8]
                src = srcv.rearrange("p a (x r) -> p a r x", r=16)
            if g == 0:
                S.op("dve", lambda e, dst=dst, src=src: e.tensor_copy(out=dst, in_=src), reads=[("O", ob)], writes=keys)
            else:
                S.op("dve", lambda e, dst=dst, src=src: e.tensor_tensor(out=dst, in0=src, in1=dst, op=ALU.add),
                     reads=[("O", ob)] + keys, writes=keys)

    for u0 in range(min(3, len(units))):
        qk(u0)
    for u in range(len(units)):
        if u + 3 < len(units):
            qk(u + 3)
        pv(u)


def wblock(w, col0):
    return np.ascontiguousarray(w[:, col0:col0 + 128].reshape(NCH, 128, 128).transpose(1, 0, 2).reshape(128, 1024))


def gam_layout(*gs):
    return np.ascontiguousarray(np.concatenate([g.reshape(NCH, 128).T for g in gs], axis=1)).astype(np.float32)


def make_cst(j):
    cst = np.zeros((128, CSTW), np.float32)
    cst[:, 0:128] = np.eye(128, dtype=np.float32)
    vl = np.ones(128, np.float32)
    vr = np.ones(128, np.float32)
    if j == 0:
        vl[:64] = 0.0
    if j == 3:
        vr[64:] = 0.0
    cst[:, 128:256] = 1.0
    cst[:, 256:384] = vl[:, None]
    cst[:, 384:512] = vr[:, None]
    cst[:, 512:576] = 1.0
    cst[:, 704:768] = 1.0
    cst[:, 768] = vl
    cst[:, 769] = vr
    cst[:, 770] = 0.0 if j == 0 else 1.0
    cst[:, 771] = 0.0 if j == 3 else 1.0
    return cst


def make_tab(rpb, j):
    kp = np.arange(128)
    kr2, kc = kp // 64, kp % 64
    q = np.arange(128)
    qr2, qc = q // 64, q % 64
    cstart = np.clip(qc - 8, 0, 64 - 16)
    colv = (kc[:, None] >= cstart[None, :]) & (kc[:, None] < cstart[None, :] + 16)
    coff = np.clip(kc[:, None] - qc[None, :] + 15, 0, 30)
    out = np.full((16, NTABBLK, 128, 128), NEG, np.float32)
    for key, (toff, offs) in TAB_OFFS.items():
        lt = 5 if key == "G" else key
        m = 16 * j + lt
        r = 2 * m + qr2
        rs = np.clip(r - 4, 0, 128 - 8)
        for k, o in enumerate(offs):
            krow = 2 * (m + o) + kr2
            rowv = (krow[:, None] >= rs[None, :]) & (krow[:, None] < rs[None, :] + 8) & (krow[:, None] >= 0) & (krow[:, None] < 128)
            valid = rowv & colv
            roff = np.clip(krow[:, None] - r[None, :] + 7, 0, 14)
            vals = rpb[:, roff, coff]
            out[:, toff + k] = np.where(valid[None], vals, np.float32(NEG))
    out = out.reshape(NPAIR, 2, NTABBLK, 128, 128).transpose(0, 3, 2, 1, 4).reshape(NPAIR, 128, 2 * NTABBLK * 128)
    return np.ascontiguousarray(out)


def window_T(xb, t0, halo):
    out = np.zeros((xb.shape[1], OWN + 2 * halo), xb.dtype)
    lo, hi = t0 - halo, t0 + OWN + halo
    a, b = max(lo, 0), min(hi, SEQ)
    out[:, a - lo:b - lo] = xb[a:b].T
    return out


_CACHE = {}


def get_prog(name, builder):
    if name not in _CACHE:
        nc = bass.Bass("TRN2", target_bir_lowering=False)
        es = ExitStack()
        builder(nc, es)
        _CACHE[name] = (nc, es)
    return _CACHE[name][0]


def run_l0(x, norm_0, w_in_0, rpb_0, w_out_0, norm_1, norm_f, trace=False):
    nc = get_prog("l0", build_l0)
    w_in_b = np.stack([wblock(w_in_0, k * 1024 + p * 128) for p in range(NPAIR) for k in range(4)])
    w_out_b = np.ascontiguousarray(w_out_0.reshape(NPAIR, 128, 1024))
    gam = gam_layout(norm_0, norm_1, norm_f)
    tabs = [make_tab(rpb_0, j) for j in range(4)]
    csts = [make_cst(j) for j in range(4)]
    in_maps = []
    for c in range(NCORE):
        b, j = divmod(c, 4)
        in_maps.append({"xw": window_T(x[b], j * OWN, HALO0), "gamd": gam, "w_in0": w_in_b, "w_out0": w_out_b,
                        "tab0": tabs[j], "cst": csts[j]})
    res = run_bass_kernel_spmd(nc, in_maps, core_ids=list(range(NCORE)), trace=trace)
    return res


def make_dtab():
    kp = np.arange(128)[:, None]
    j = np.arange(128)[None, :]
    out = np.empty((128, 256), np.float32)
    for kl, sh in enumerate((-64, 64)):
        delta = np.abs(kp + sh - j)
        out[:, kl * 128:(kl + 1) * 128] = np.where(delta <= 64, delta, 1.0e6)
    return out


def run_l1(x1T_list, h1T_list, w_in_1, w_out_1, norm_0, norm_1, norm_f, trace=False):
    nc = get_prog("l1", build_l1)
    cols = []
    for p in range(NPAIR):
        cols.append(9216 + p * 128)
        for g in range(3):
            for k in range(3):
                cols.append(g * 3072 + k * 1024 + p * 128)
    w_in_b = np.stack([wblock(w_in_1, c0) for c0 in cols])
    w_out_b = np.ascontiguousarray(w_out_1.reshape(NPAIR, 128, 1024))
    gam = gam_layout(norm_0, norm_1, norm_f)
    dt = make_dtab()
    csts = [make_cst(j) for j in range(4)]
    in_maps = []
    for c in range(NCORE):
        b, j = divmod(c, 4)
        hw = np.zeros((D_MODEL, W1), h1T_list[c].dtype)
        hw[:, HALO1:HALO1 + OWN] = h1T_list[c]
        if j > 0:
            hw[:, 0:HALO1] = h1T_list[c - 1][:, OWN - HALO1:]
        if j < 3:
            hw[:, HALO1 + OWN:] = h1T_list[c + 1][:, 0:HALO1]
        in_maps.append({"x1w": x1T_list[c], "h1w": hw, "gamd": gam, "w_in1": w_in_b, "w_out1": w_out_b,
                        "cst": csts[j], "dtab": dt})
    return run_bass_kernel_spmd(nc, in_maps, core_ids=list(range(NCORE)), trace=trace)


def kernel(x, norm_0, w_in_0, rpb_0, w_out_0, norm_1, w_in_1, w_out_1, norm_f):
    f = lambda a: np.ascontiguousarray(np.asarray(a, dtype=np.float32))
    x, norm_0, w_in_0, rpb_0, w_out_0, norm_1, w_in_1, w_out_1, norm_f = map(
        f, (x, norm_0, w_in_0, rpb_0, w_out_0, norm_1, w_in_1, w_out_1, norm_f))
    nc, in_maps = run_fused(x, norm_0, w_in_0, rpb_0, w_out_0, norm_1, w_in_1, w_out_1, norm_f)
    res = run_bass_kernel_spmd(nc, in_maps, core_ids=list(range(NCORE)))
    out = np.empty((2, SEQ, D_MODEL), np.float32)
    for c in range(NCORE):
        b, j = divmod(c, 4)
        out[b, j * OWN:(j + 1) * OWN, :] = np.asarray(res.results[c]["yT"]).T
    return out


def make_hidx(j):
    l = j - 1 if j > 0 else j
    r = j + 1 if j < 3 else j
    p = np.arange(128, dtype=np.int64)
    out = np.zeros((128, NHIDX), np.uint32)
    for side, nbr in enumerate((l, r)):
        for c in range(NCH):
            for blk in range(4):
                tb = 4 + blk if side == 0 else blk
                out[:, side * 32 + c * 4 + blk] = (nbr * D_MODEL + c * 128 + p) * 8 + tb
    return out


def l1_cols():
    cols = []
    for p in range(NPAIR):
        cols.append(9216 + p * 128)
        for g in range(3):
            for k in range(3):
                cols.append(g * 3072 + k * 1024 + p * 128)
    return cols


def run_fused(x, norm_0, w_in_0, rpb_0, w_out_0, norm_1, w_in_1, w_out_1, norm_f, trace=False):
    nc = get_prog("fused", build_fused)
    w_in0_b = np.stack([wblock(w_in_0, k * 1024 + p * 128) for p in range(NPAIR) for k in range(4)])
    w_out0_b = np.ascontiguousarray(w_out_0.reshape(NPAIR, 128, 1024))
    w_in1_b = np.stack([wblock(w_in_1, c0) for c0 in l1_cols()])
    w_out1_b = np.ascontiguousarray(w_out_1.reshape(NPAIR, 128, 1024))
    gam = gam_layout(norm_0, norm_1, norm_f)
    dt = make_dtab()
    tabs = [make_tab(rpb_0, j) for j in range(4)]
    csts = [make_cst(j) for j in range(4)]
    hidx = [make_hidx(j) for j in range(4)]
    in_maps = []
    for c in range(NCORE):
        b, j = divmod(c, 4)
        in_maps.append({"xw": window_T(x[b], j * OWN, HALO0 + HALO1), "gamd": gam, "w_in0": w_in0_b, "w_out0": w_out0_b,
                        "tab0": tabs[j], "cst": csts[j], "w_in1": w_in1_b, "w_out1": w_out1_b, "dtab": dt})
    return nc, in_maps
```

```python
import numpy as np
from contextlib import ExitStack
import concourse.bass as bass
import concourse.mybir as mybir
from concourse.bass_utils import run_bass_kernel_spmd

F32 = mybir.dt.float32
BF16 = mybir.dt.bfloat16
AF = mybir.ActivationFunctionType
ALU = mybir.AluOpType


class Op:
    __slots__ = ("eng", "fn", "deps", "is_dma", "slot", "ord", "marked", "name")

    def __init__(self, eng, fn, is_dma=False, slot=None, name=""):
        self.eng = eng
        self.fn = fn
        self.deps = []
        self.is_dma = is_dma
        self.slot = slot
        self.ord = None
        self.marked = False
        self.name = name


class Sched:
    ENGS = ("pe", "act", "dve", "pool", "sp")

    def __init__(self, nc, es):
        self.nc = nc
        self.es = es
        self.q = {e: [] for e in self.ENGS}
        self.last_w = {}
        self.readers = {}
        self.finals = []
        self.slot_count = {}
        self.n_sb = 0

    def sbuf(self, name, shape, dtype):
        return self.es.enter_context(self.nc.sbuf_tensor(name, shape, dtype))

    def psum(self, name, shape, dtype):
        return self.es.enter_context(self.nc.psum_tensor(name, shape, dtype))

    def _add(self, op, reads, writes, deps):
        dl = []
        for r in reads:
            dl.extend(self.last_w.get(r, ()))
            self.readers.setdefault(r, []).append(op)
        for w_ in writes:
            prev = self.last_w.get(w_, [])
            dl.extend(prev)
            for rd in self.readers.get(w_, ()):
                if rd is not op:
                    dl.append(rd)
            if op.is_dma and prev and all(q.is_dma for q in prev):
                self.last_w[w_] = prev + [op]
            else:
                self.last_w[w_] = [op]
            self.readers[w_] = []
        dl.extend(deps)
        seen = set()
        for d in dl:
            if d is op or id(d) in seen:
                continue
            seen.add(id(d))
            if d.eng == "pe" and op.eng == "pe" and not d.is_dma and not op.is_dma:
                continue
            op.deps.append(d)
        self.q[op.eng].append(op)
        return op

    def op(self, eng, fn, reads=(), writes=(), deps=(), name=""):
        return self._add(Op(eng, fn, name=name), reads, writes, deps)

    def dma(self, eng, fn, reads=(), writes=(), deps=(), slot=None, name=""):
        if slot is None:
            slot = ("auto", len(self.slot_count))
        op = Op(eng, fn, is_dma=True, slot=slot, name=name)
        k = self.slot_count.get(slot, 0) + 1
        self.slot_count[slot] = k
        op.ord = k
        return self._add(op, reads, writes, deps)

    def final_wait(self, eng, ops):
        self.finals.append((eng, list(ops)))

    def emit(self):
        nc = self.nc
        for e in self.ENGS:
            for op in self.q[e]:
                for d in op.deps:
                    d.marked = True
        for _, ops in self.finals:
            for d in ops:
                d.marked = True
        for e in self.ENGS:
            for op in self.q[e]:
                if op.is_dma:
                    op.marked = True
        for e in self.ENGS:
            k = 0
            for op in self.q[e]:
                if not op.is_dma and op.marked:
                    k += 1
                    op.ord = k
        sems = {}

        def sem_of(op):
            key = ("dma", op.slot) if op.is_dma else ("eng", op.eng)
            if key not in sems:
                sems[key] = self.es.enter_context(nc.semaphore("s%d" % len(sems)))
            return sems[key]

        def val_of(op):
            return op.ord * 16 if op.is_dma else op.ord

        for e in self.ENGS:
            for op in self.q[e]:
                if op.marked:
                    sem_of(op)
        finals = {}
        for eng, ops in self.finals:
            finals.setdefault(eng, []).extend(ops)
        self.n_waits = 0

        def run(engname, engine):
            known = {}

            def need(d):
                s = sem_of(d)
                v = val_of(d)
                if known.get(id(s), 0) >= v:
                    return
                known[id(s)] = v
                engine.wait_ge(s, v)
                self.n_waits += 1

            for op in self.q[engname]:
                best = {}
                for d in op.deps:
                    s = sem_of(d)
                    if id(s) not in best or val_of(best[id(s)]) < val_of(d):
                        best[id(s)] = d
                for d in best.values():
                    need(d)
                ins = op.fn(engine)
                if op.marked:
                    ins.then_inc(sem_of(op), 16 if op.is_dma else 1)
            for d in finals.get(engname, ()):
                need(d)

        with nc.Block() as block:
            @block.sync
            def _(eng):
                run("sp", eng)

            @block.tensor
            def _(eng):
                run("pe", eng)

            @block.scalar
            def _(eng):
                run("act", eng)

            @block.vector
            def _(eng):
                run("dve", eng)

            @block.gpsimd
            def _(eng):
                run("pool", eng)


D_MODEL = 1024
SEQ = 8192
NCORE = 8
OWN = 2048
NCH = 8
NPAIR = 8
HALO0 = 256
W0 = OWN + 2 * HALO0
HALO1 = 1024
W1 = OWN + 2 * HALO1
DILS = (1, 4, 16)
NEG = -30000.0
EPS = 1e-6
TAB_OFFS = {"G": (0, (-2, -1, 0, 1, 2)), 0: (5, (-2, -1, 0, 1, 2, 3)), 1: (11, (-2, -1, 0, 1, 2)),
            14: (16, (-2, -1, 0, 1, 2)), 15: (21, (-3, -2, -1, 0, 1, 2))}
NTABBLK = 27
U32 = mybir.dt.uint32
NHIDX = 64
MASK_ENG = "pool"
FIN_ENG = "pool"
CSTW = 128 + 5 * 128 + 4


def tab_for(lt):
    return TAB_OFFS[lt] if lt in TAB_OFFS else TAB_OFFS["G"]


class Ctx:
    pass


def common_setup(nc, es, hcols=W0):
    C = Ctx()
    S = Sched(nc, es)
    C.S = S
    C.nc = nc
    C.G = S.psum("G", [128, 2, 512], F32)
    C.Sp = S.psum("Sp", [128, 4, 512], F32)
    C.O = S.psum("O", [128, 2, 512], F32)
    C.gi = 0
    C.xT = S.sbuf("xT", [128, NCH, OWN], F32)
    C.hT = S.sbuf("hT", [128, NCH, hcols], BF16)
    C.QT = S.sbuf("QT", [128, 2, OWN], BF16)
    C.KV = S.sbuf("KV", [128, 2 * W1], BF16)
    C.KT = C.KV[:, 0:W1]
    C.VT = C.KV[:, W1:2 * W1]
    C.V = S.sbuf("V", [128, 32, 128], BF16)
    C.GT = S.sbuf("GT", [128, OWN], BF16)
    C.uT = S.sbuf("uT", [128, OWN], BF16)
    C.Wb = S.sbuf("Wb", [128, 8, 1024], BF16)
    C.Wo = S.sbuf("Wo", [128, 2, 1024], BF16)
    C.E = S.sbuf("E", [128, 4, 512], BF16)
    C.P = S.sbuf("P", [128, 4, 512], BF16)
    C.sq = S.sbuf("sq", [128, 2, 512], BF16)
    C.rstd = S.sbuf("rstd", [128, 512], F32)
    C.tmp = S.sbuf("tmp", [128, 4, 128], F32)
    C.onesf = S.sbuf("onesf", [128, 128], BF16)
    C.vm = S.sbuf("vm", [128, 5, 128], BF16)
    C.vcol = S.sbuf("vcol", [128, 4], F32)
    C.gam = S.sbuf("gam", [128, 3, NCH], F32)
    C.wslot = 0
    C.pend = []
    C.oi = 0
    C.pops = 2
    C.nrm = 0
    C.epsb = S.sbuf("epsb", [128, 1], F32)
    C.ident = S.sbuf("ident", [128, 128], BF16)
    S.op("dve", lambda e: e.memset(C.epsb[:], EPS), writes=["epsb"])
    S.op("dve", lambda e: e.memset(C.onesf[:], 1.0), writes=["onesf"])
    S.op("pool", lambda e: e.memset(C.QT[:], 0.0), writes=["QT"])
    return C


def next_g(C):
    gi = C.gi
    C.gi = (gi + 1) % 2
    return gi


def load_wblock(C, src_ap):
    S = C.S
    s = C.wslot
    C.wslot = (s + 1) % 8
    S.dma("pool", lambda e: e.dma_start(out=C.Wb[:, s, :], in_=src_ap), writes=[("W", s)], slot=("W", s))
    return s


def proj_block(C, ws, h_of, n, evac):
    S = C.S
    gi = next_g(C)
    gk = ("G", gi)
    for c in range(NCH):
        rhs, rk = h_of(c)
        S.op("pe", lambda e, c=c, rhs=rhs: e.matmul(C.G[:, gi, 0:n], lhsT=C.Wb[:, ws, c * 128:(c + 1) * 128],
                                                    rhs=rhs, start=(c == 0), stop=(c == NCH - 1)),
             reads=[("W", ws), rk], writes=[gk])
    evac(C.G[:, gi, 0:n], gk)
    pop_outproj(C, C.pops)


def emit_norm(C, x_of, ntok, gidx, out_of):
    t0 = 0
    while t0 < ntok:
        n = min(512, ntok - t0)
        _norm_block(C, x_of, t0, n, gidx, out_of)
        t0 += n


def rstd_buf(C):
    k = C.nrm % 2
    C.nrm += 1
    if k == 0:
        return C.rstd[:, :], [("rstd", 0)]
    return C.tmp[:, :, :].rearrange("p a c -> p (a c)"), [("rstd", 1)] + [("tmp", i) for i in range(4)]


def _norm_block(C, x_of, t0, n, gidx, out_of):
    S = C.S
    gi = next_g(C)
    gk = ("G", gi)
    rs, rkeys = rstd_buf(C)
    for c in range(NCH):
        xa, xk = x_of(c, t0, n)
        sb = c % 2
        S.op("act", lambda e, xa=xa, sb=sb: e.activation(out=C.sq[:, sb, 0:n], in_=xa, func=AF.Square),
             reads=[xk], writes=[("sq", sb)])
        S.op("pe", lambda e, c=c, sb=sb: e.matmul(C.G[:, gi, 0:n], lhsT=C.onesf[:, :], rhs=C.sq[:, sb, 0:n],
                                                  start=(c == 0), stop=(c == NCH - 1)),
             reads=["onesf", ("sq", sb)], writes=[gk])
    S.op("act", lambda e: e.activation(out=rs[:, 0:n], in_=C.G[:, gi, 0:n], func=AF.Ln,
                                       scale=1.0 / D_MODEL, bias=C.epsb[:, 0:1]),
         reads=[gk, "epsb"], writes=rkeys)
    S.op("act", lambda e: e.activation(out=rs[:, 0:n], in_=rs[:, 0:n], func=AF.Exp, scale=-0.5), reads=rkeys, writes=rkeys)
    for c in range(NCH):
        xa, xk = x_of(c, t0, n)
        oa, ok = out_of(c, t0, n)
        if False:
            S.op("pool", lambda e, xa=xa, oa=oa, c=c: e.scalar_tensor_tensor(
                out=oa, in0=xa, scalar=C.gam[:, gidx, c:c + 1], in1=rs[:, 0:n], op0=ALU.mult, op1=ALU.mult),
                 reads=[xk, "gam"] + rkeys, writes=[ok])
        else:
            S.op("dve", lambda e, xa=xa, oa=oa, c=c: e.scalar_tensor_tensor(
                out=oa, in0=xa, scalar=C.gam[:, gidx, c:c + 1], in1=rs[:, 0:n], op0=ALU.mult, op1=ALU.mult),
                 reads=[xk, "gam"] + rkeys, writes=[ok])


def emit_outproj(C, pair, wo_src, tbs, ms=None):
    S = C.S
    for tb in tbs:
        for m in (range(NCH) if ms is None else ms):
            gi = C.oi
            C.oi = (gi + 1) % 4
            gk = ("S", gi)
            S.op("pe", lambda e, m=m, gi=gi, tb=tb: e.matmul(C.Sp[:, gi, :], lhsT=C.Wo[:, pair % 2, m * 128:(m + 1) * 128],
                                                          rhs=C.uT[:, tb * 512:(tb + 1) * 512], start=True, stop=True),
                 reads=[("Wo", pair % 2), ("uT", tb)], writes=[gk])
            S.op("dve", lambda e, m=m, gi=gi, tb=tb: e.tensor_tensor(out=C.xT[:, m, tb * 512:(tb + 1) * 512],
                                                                 in0=C.Sp[:, gi, :], in1=C.xT[:, m, tb * 512:(tb + 1) * 512],
                                                                 op=ALU.add),
                 reads=[gk, ("x", m, tb)], writes=[("x", m, tb)])


def pend_outproj(C, pair, tb):
    for m in range(NCH):
        C.pend.append((pair, tb, m))


def pop_outproj(C, n=1):
    for _ in range(n):
        if C.pend:
            pair, tb, m = C.pend.pop(0)
            emit_outproj(C, pair, None, [tb], [m])


def flush_outproj(C):
    pop_outproj(C, len(C.pend))


def deint(ps, d):
    return ps if d == 1 else ps.rearrange("p (i r) -> p r i", r=d)


def evac_copy_act(C, dst, dkey, d=1):
    def f(ps, gk):
        C.S.op("act", lambda e: e.activation(out=dst, in_=deint(ps, d), func=AF.Copy), reads=[gk], writes=[dkey])
    return f


def evac_copy_dve(C, dst, dkey, d=1):
    def f(ps, gk):
        C.S.op("dve", lambda e: e.tensor_copy(out=dst, in_=deint(ps, d)), reads=[gk], writes=[dkey])
    return f


def evac_q(C, c0, n, d=1):
    def f(ps, gk):
        for hh in range(2):
            hs = slice(64 * hh, 64 * hh + 64)
            if d == 1:
                dst = C.QT[hs, hh, c0:c0 + n]
            else:
                dst = C.QT[hs, hh, :].rearrange("p (r i) -> p r i", r=d)[:, :, c0 // d:(c0 + n) // d]
            C.S.op("act", lambda e, hs=hs, dst=dst: e.activation(out=dst, in_=deint(ps[hs, :], d), func=AF.Copy), reads=[gk], writes=["QT"])
    return f


def evac_scaled_dve(C, dst, dkey, flag_ap, d=1):
    def f(ps, gk):
        C.S.op("dve", lambda e: e.tensor_scalar(out=dst, in0=deint(ps, d), scalar1=flag_ap, scalar2=None, op0=ALU.mult), reads=[gk, "vcol"], writes=[dkey])
    return f


def _old_evac_copy_act(C, dst, dkey):
    def f(ps, gk):
        C.S.op("act", lambda e: e.activation(out=dst, in_=ps, func=AF.Copy), reads=[gk], writes=[dkey])
    return f


def _old_evac_copy_dve(C, dst, dkey):
    def f(ps, gk):
        C.S.op("dve", lambda e: e.tensor_copy(out=dst, in_=ps), reads=[gk], writes=[dkey])
    return f


def _old_evac_q(C, c0, n):
    def f(ps, gk):
        C.S.op("act", lambda e: e.activation(out=C.QT[0:64, 0, c0:c0 + n], in_=ps[0:64, :], func=AF.Copy), reads=[gk], writes=["QT"])
        C.S.op("act", lambda e: e.activation(out=C.QT[64:128, 1, c0:c0 + n], in_=ps[64:128, :], func=AF.Copy), reads=[gk], writes=["QT"])
    return f


def _old_evac_scaled_dve(C, dst, dkey, flag_ap):
    def f(ps, gk):
        C.S.op("dve", lambda e: e.tensor_scalar(out=dst, in0=ps, scalar1=flag_ap, scalar2=None, op0=ALU.mult), reads=[gk, "vcol"], writes=[dkey])
    return f


def evac_silu(C, dst, dkey):
    def f(ps, gk):
        C.S.op("act", lambda e: e.activation(out=dst, in_=ps, func=AF.Silu), reads=[gk], writes=[dkey])
    return f


def emit_vtrans(C, tiles):
    for i in range(0, len(tiles), 4):
        _vtrans_group(C, tiles[i:i + 4])


def _vtrans_group(C, grp):
    S = C.S
    gi = next_g(C)
    gk = ("G", gi)
    gb = C.G[:, gi, :].bitcast(BF16)
    n = len(grp)
    ti0 = grp[0][0]
    assert [t[0] for t in grp] == list(range(ti0, ti0 + n))
    for j, (ti, src, edge) in enumerate(grp):
        S.op("pe", lambda e, j=j, src=src: e.transpose(gb[:, j * 128:(j + 1) * 128], src, C.ident[:, :]),
             reads=["VT", "ident"], writes=[gk])
    S.op("act", lambda e: e.activation(out=C.V[:, ti0:ti0 + n, :].rearrange("p a c -> p (a c)"), in_=gb[:, 0:n * 128], func=AF.Copy),
         reads=[gk], writes=["V"])


XWF = OWN + 2 * (HALO1 + HALO0)


def l0_decl(C, nc, xwin):
    S = C.S
    io = Ctx()
    io.xw = nc.dram_tensor("xw", [D_MODEL, xwin], F32, kind="ExternalInput").ap()
    io.gam = nc.dram_tensor("gamd", [128, 3 * NCH], F32, kind="ExternalInput").ap()
    io.w_in = nc.dram_tensor("w_in0", [NPAIR * 4, 128, 1024], F32, kind="ExternalInput").ap()
    io.w_out = nc.dram_tensor("w_out0", [NPAIR, 128, 1024], F32, kind="ExternalInput").ap()
    io.tab = nc.dram_tensor("tab0", [NPAIR, 128, 2 * NTABBLK * 128], F32, kind="ExternalInput").ap()
    io.cst = nc.dram_tensor("cst", [128, CSTW], F32, kind="ExternalInput").ap()
    io.xv = io.xw.rearrange("(c p) t -> p c t", p=128)
    io.xh = C.KV[:, :].bitcast(F32).rearrange("p (c t) -> p c t", c=NCH)
    cst = io.cst
    S.dma("sp", lambda e: e.dma_start(out=C.gam[:].rearrange("p a c -> p (a c)"), in_=io.gam), writes=["gam"], slot="c0")
    S.dma("pool", lambda e: e.dma_start(out=C.ident[:], in_=cst[:, 0:128]), writes=["ident"], slot="c1")
    S.dma("pool", lambda e: e.dma_start(out=C.vm[:].rearrange("p a c -> p (a c)"), in_=cst[:, 128:768]), writes=["vm"], slot="c2")
    S.dma("sp", lambda e: e.dma_start(out=C.vcol[:], in_=cst[:, 768:772]), writes=["vcol"], slot="c3")
    return io


def l0_pass(C, io, Tab, rs, NT, special, npair=NPAIR):
    S = C.S
    xv, xh = io.xv, io.xh
    NQ = NT * 128
    WIN = NQ + 2 * HALO0
    NTB = NQ // 512
    for c in range(NCH):
        S.dma("sp", lambda e, c=c: e.dma_start(out=C.xT[:, c, 0:NQ], in_=xv[:, c, rs:rs + NQ]),
              writes=[("x", c, tb) for tb in range(NTB)], slot=("xl", c))
    S.dma("sp", lambda e: e.dma_start(out=xh[:, :, 0:HALO0], in_=xv[:, :, rs - HALO0:rs]), writes=["xh", "KT", "VT"], slot="xh0")
    S.dma("sp", lambda e: e.dma_start(out=xh[:, :, HALO0:2 * HALO0], in_=xv[:, :, rs + NQ:rs + NQ + HALO0]),
          writes=["xh", "KT", "VT"], slot="xh1")

    wslots = {}

    def prefetch(p):
        if p >= npair:
            return
        wslots[p] = [load_wblock(C, io.w_in[p * 4 + k]) for k in range(4)]

    def load_wo(p):
        S.dma("pool", lambda e: e.dma_start(out=C.Wo[:, p % 2, :], in_=io.w_out[p]), writes=[("Wo", p % 2)], slot=("Wo", p % 2))

    def prefetch_tab(p):
        S.dma("pool", lambda e: e.dma_start(out=Tab[:, 0, :], in_=io.tab[p]), writes=[("Tab", k_) for k_ in TAB_OFFS], slot=("Tab", 0))

    prefetch(0)

    emit_norm(C, lambda c, t0, n: (C.xT[:, c, t0:t0 + n], ("x", c, t0 // 512)), NQ, 0,
              lambda c, t0, n: (C.hT[:, c, HALO0 + t0:HALO0 + t0 + n], "hT"))
    for half in range(2):
        emit_norm(C, lambda c, t0, n, half=half: (xh[:, c, half * HALO0 + t0:half * HALO0 + t0 + n], "xh"), HALO0, 0,
                  lambda c, t0, n, half=half: (C.hT[:, c, half * (HALO0 + NQ) + t0:half * (HALO0 + NQ) + t0 + n], "hT"))

    hkey = "hT"
    for p in range(npair):
        prefetch(p + 1)
        load_wo(p)
        prefetch_tab(p)
        _l0_pair(C, p, wslots[p], Tab, NT, WIN, NTB, special)
    flush_outproj(C)


def _l0_pair(C, p, ws, Tab, NT, WIN, NTB, special):
    S = C.S
    hkey = "hT"
    wq, wk, wv, wg = ws
    for tb in range(WIN // 512):
        proj_block(C, wk, lambda c, tb=tb: (C.hT[:, c, tb * 512:(tb + 1) * 512], hkey), 512,
                   evac_copy_act(C, C.KT[:, tb * 512:(tb + 1) * 512], "KT"))
    for tb in range(WIN // 512):
        proj_block(C, wv, lambda c, tb=tb: (C.hT[:, c, tb * 512:(tb + 1) * 512], hkey), 512,
                   evac_copy_dve(C, C.VT[:, tb * 512:(tb + 1) * 512], "VT"))
    emit_vtrans(C, [(wt, C.VT[:, wt * 128:(wt + 1) * 128], None) for wt in range(WIN // 128)])
    for tb in range(NTB):
        proj_block(C, wq, lambda c, tb=tb: (C.hT[:, c, HALO0 + tb * 512:HALO0 + (tb + 1) * 512], hkey), 512,
                   evac_q(C, tb * 512, 512))
    for tb in range(NTB):
        proj_block(C, wg, lambda c, tb=tb: (C.hT[:, c, HALO0 + tb * 512:HALO0 + (tb + 1) * 512], hkey), 512,
                   evac_silu(C, C.GT[:, tb * 512:(tb + 1) * 512], "GT"))
    for key in ((0, 1, "G", 14, 15) if special else ("G",)):
        toff_, offs_ = TAB_OFFS[key]
        a_, b_ = toff_ * 256, (toff_ + len(offs_)) * 256
        S.op("act", lambda e, a_=a_, b_=b_: e.activation(out=Tab[:, 0, a_:b_], in_=Tab[:, 0, a_:b_], func=AF.Exp),
             reads=[("Tab", key)], writes=[("Tab", key)])
    tabf = (lambda lt: tab_for(lt)) if special else (lambda lt: TAB_OFFS["G"])
    units = []
    for lt in range(NT):
        toff, offs = tabf(lt)
        for i in range(0, len(offs), 2):
            units.append((lt, toff, offs[i:i + 2], i, i == 0, i + 2 >= len(offs)))
    tkey = (lambda lt: lt if lt in TAB_OFFS else "G") if special else (lambda lt: "G")

    def qk(u):
        lt, toff, offs, kbase, first, last = units[u]
        sb, eb = u % 4, u % 4
        n = len(offs)
        for k, o in enumerate(offs):
            wt = lt + 2 + o
            S.op("pe", lambda e, wt=wt, k=k: e.matmul(
                C.Sp[:, sb, k * 256:(k + 1) * 256].rearrange("p (a c) -> p a c", a=2), lhsT=C.KT[:, wt * 128:(wt + 1) * 128],
                rhs=C.QT[:, :, lt * 128:(lt + 1) * 128], start=True, stop=True),
                reads=["KT", "QT"], writes=[("S", sb)])
        S.op("act", lambda e: e.activation(out=C.E[:, eb, 0:n * 256], in_=C.Sp[:, sb, 0:n * 256], func=AF.Exp, scale=0.125),
             reads=[("S", sb)], writes=[("E", eb)])
        t0 = (toff + kbase) * 256
        S.op(MASK_ENG if u % 3 == 2 else "dve", lambda e: e.tensor_tensor(out=C.P[:, eb, 0:n * 256], in0=C.E[:, eb, 0:n * 256],
                                                 in1=Tab[:, 0, t0:t0 + n * 256], op=ALU.mult),
             reads=[("E", eb), ("Tab", tkey(lt))], writes=[("P", eb)])

    def pv(u):
        lt, toff, offs, kbase, first, last = units[u]
        eb = u % 4
        ob = lt % 2
        for k, o in enumerate(offs):
            wt = lt + 2 + o
            S.op("pe", lambda e, wt=wt, k=k: e.matmul(
                C.O[:, ob, 0:256], lhsT=C.V[:, wt, :], rhs=C.P[:, eb, k * 256:(k + 1) * 256],
                start=(first and k == 0), stop=False), reads=["V", ("P", eb)], writes=[("O", ob)])
            for hh in range(2):
                S.op("pe", lambda e, k=k, hh=hh, fin=(last and k == len(offs) - 1 and hh == 1): e.matmul(
                    C.O[:, ob, 256:384], lhsT=C.vm[:, 3 + hh, :], rhs=C.P[:, eb, k * 256 + hh * 128:k * 256 + (hh + 1) * 128],
                    start=False, stop=fin), reads=["vm", ("P", eb)], writes=[("O", ob)])
        if last:
            td, tn = 2 * (lt % 2), 2 * (lt % 2) + 1
            S.op("act", lambda e: e.activation(out=C.tmp[:, td, :], in_=C.O[:, ob, 256:384], func=AF.Ln), reads=[("O", ob)], writes=[("tmp", td)])
            S.op("act", lambda e: e.activation(out=C.tmp[:, td, :], in_=C.tmp[:, td, :], func=AF.Exp, scale=-1.0), reads=[("tmp", td)], writes=[("tmp", td)])
            for hh in range(2):
                hs = slice(64 * hh, 64 * hh + 64)
                S.op("dve", lambda e, hs=hs, hh=hh: e.tensor_tensor(out=C.tmp[hs, tn, :], in0=C.O[hs, ob, hh * 128:(hh + 1) * 128],
                                                                 in1=C.tmp[hs, td, :], op=ALU.mult),
                     reads=[("O", ob), ("tmp", td)], writes=[("tmp", tn)])
            S.op(FIN_ENG, lambda e: e.tensor_tensor(out=C.uT[:, lt * 128:(lt + 1) * 128], in0=C.tmp[:, tn, :],
                                                    in1=C.GT[:, lt * 128:(lt + 1) * 128], op=ALU.mult),
                 reads=[("tmp", tn), "GT"], writes=[("uT", lt // 4)])
            if lt % 4 == 3:
                pend_outproj(C, p, lt // 4)

    for u0 in range(min(3, len(units))):
        qk(u0)
    for u in range(len(units)):
        if u + 3 < len(units):
            qk(u + 3)
        pv(u)


def build_l0(nc, es, npair=NPAIR):
    C = common_setup(nc, es)
    S = C.S
    io = l0_decl(C, nc, W0)
    x1o = nc.dram_tensor("x1T", [D_MODEL, OWN], F32, kind="ExternalOutput").ap()
    h1o = nc.dram_tensor("h1T", [D_MODEL, OWN], BF16, kind="ExternalOutput").ap()
    Tab = S.sbuf("Tab", [128, 1, 2 * NTABBLK * 128], BF16)
    hout = S.sbuf("hout", [128, 2, 512], BF16)
    l0_pass(C, io, Tab, HALO0, 16, True, npair=npair)
    outs = []
    for c in range(NCH):
        outs.append(S.dma("sp", lambda e, c=c: e.dma_start(out=x1o.rearrange("(c p) t -> p c t", p=128)[:, c, :], in_=C.xT[:, c, :]),
                          reads=[("x", c, tb) for tb in range(4)], slot=("xo", c)))
    hov = h1o.rearrange("(c p) t -> p c t", p=128)
    emit_norm_out(C, 1, hout, hov, outs, 4)
    S.final_wait("sp", outs)
    S.emit()
    return C


def emit_norm_out(C, gidx, stage, dst_view, outs, ntb=4, dcol=0):
    k = [0]
    for tb in range(ntb):
        emit_norm_block_out(C, tb, gidx, stage, dst_view, outs, k, dcol)


def emit_norm_block_out(C, tb, gidx, stage, dst_view, outs, k, dcol=0):
    S = C.S
    n = 512
    t0 = tb * 512
    gi = next_g(C)
    gk = ("G", gi)
    for c in range(NCH):
        sb = c % 2
        S.op("act", lambda e, c=c, sb=sb: e.activation(out=C.sq[:, sb, :], in_=C.xT[:, c, t0:t0 + n], func=AF.Square),
             reads=[("x", c, tb)], writes=[("sq", sb)])
        S.op("pe", lambda e, c=c, sb=sb: e.matmul(C.G[:, gi, :], lhsT=C.onesf[:, :], rhs=C.sq[:, sb, :],
                                                  start=(c == 0), stop=(c == NCH - 1)),
             reads=["onesf", ("sq", sb)], writes=[gk])
    rs, rkeys = rstd_buf(C)
    S.op("act", lambda e: e.activation(out=rs[:, :], in_=C.G[:, gi, :], func=AF.Ln, scale=1.0 / D_MODEL, bias=C.epsb[:, 0:1]),
         reads=[gk, "epsb"], writes=rkeys)
    S.op("act", lambda e: e.activation(out=rs[:, :], in_=rs[:, :], func=AF.Exp, scale=-0.5), reads=rkeys, writes=rkeys)
    for c in range(NCH):
        b = k[0] % 2
        k[0] += 1
        S.op("dve", lambda e, c=c, b=b: e.scalar_tensor_tensor(out=stage[:, b, :], in0=C.xT[:, c, t0:t0 + n],
                                                             scalar=C.gam[:, gidx, c:c + 1], in1=rs[:, :],
                                                             op0=ALU.mult, op1=ALU.mult),
             reads=[("x", c, tb), "gam"] + rkeys, writes=[("stage", b)])
        outs.append(S.dma("sp", lambda e, c=c, b=b: e.dma_start(out=dst_view[:, c, dcol + t0:dcol + t0 + n], in_=stage[:, b, :]),
                          reads=[("stage", b)], writes=["hwin"], slot=("so", b)))


def build_fused(nc, es, npair=NPAIR):
    C = common_setup(nc, es)
    S = C.S
    C.big = S.sbuf("big", [128, 2 * OWN], F32)
    io = l0_decl(C, nc, XWF)
    w_in = nc.dram_tensor("w_in1", [NPAIR * 10, 128, 1024], F32, kind="ExternalInput").ap()
    w_out = nc.dram_tensor("w_out1", [NPAIR, 128, 1024], F32, kind="ExternalInput").ap()
    dtab = nc.dram_tensor("dtab", [128, 256], F32, kind="ExternalInput").ap()
    yo = nc.dram_tensor("yT", [D_MODEL, OWN], F32, kind="ExternalOutput").ap()
    hwin = nc.dram_tensor("hwin", [D_MODEL, W1], BF16).ap()
    hv = hwin.rearrange("(c p) t -> p c t", p=128)

    Tab = C.big[:, :].bitcast(BF16)[:, 0:2 * NTABBLK * 128].rearrange("p (a c) -> p a c", a=1)
    hb = S.sbuf("hb", [128, 2, NCH, 256], BF16)
    acc = C.big[:, :].rearrange("p (a t) -> p a t", a=2)
    M1 = S.sbuf("M1", [128, 3, 512], BF16)
    D = S.sbuf("D", [128, 256], BF16)
    stage = acc[:, 0, 0:1024].rearrange("p (a c) -> p a c", a=2)
    S.dma("pool", lambda e: e.dma_start(out=D[:], in_=dtab), writes=["D"], slot="c4")

    hst = hb[:, :, 0:2, :].rearrange("p a b c -> p a (b c)")
    hw_outs = []
    for rs, dcol in ((HALO0, 0), (HALO0 + HALO1 + OWN + 0, HALO1 + OWN)):
        l0_pass(C, io, Tab, rs, 8, False, npair=npair)
        emit_norm_out(C, 1, hst, hv, hw_outs, 2, dcol)
    for b_ in range(2):
        S.readers.setdefault(("hb", b_), []).extend(S.readers.get(("stage", b_), []))
        S.last_w.setdefault(("hb", b_), []).extend(S.last_w.get(("stage", b_), []))
    l0_pass(C, io, Tab, HALO0 + HALO1, 16, True, npair=npair)
    emit_norm(C, lambda c, t0, n: (C.xT[:, c, t0:t0 + n], ("x", c, t0 // 512)), OWN, 1,
              lambda c, t0, n: (C.hT[:, c, t0:t0 + n], "hT"))

    blocks = [w_in[p * 10 + k_] for p in range(npair) for k_ in range(10)]
    issued = [0]
    slots = {}

    def wget(i):
        while issued[0] < min(len(blocks), i + 5):
            slots[issued[0]] = load_wblock(C, blocks[issued[0]])
            issued[0] += 1
        return slots[i]

    hbc = [0]
    l1_body(C, npair, wget, w_out, hv, hb, hbc, acc, M1, D)
    flush_outproj(C)
    outs = []
    emit_norm_out(C, 2, stage, yo.rearrange("(c p) t -> p c t", p=128), outs)
    S.final_wait("sp", outs)
    S.emit()
    return C


def l1_body(C, npair, wget, w_out, hv, hb, hbc, acc, M1, D):
    S = C.S
    for p in range(npair):
        _l1_pair(C, p, wget, w_out, hv, hb, hbc, acc, M1, D)


def _l1_pair(C, p, wget, w_out, hv, hb, hbc, acc, M1, D):
    S = C.S
    S.dma("pool", lambda e: e.dma_start(out=C.Wo[:, p % 2, :], in_=w_out[p]), writes=[("Wo", p % 2)], slot=("Wo", p % 2))
    for g, d in enumerate(DILS):
        for hh in range(2):
            slope = 2.0 ** (-8.0 * (2 * p + hh + 1) / 16.0)
            S.op("act", lambda e, g=g, hh=hh, sc=-slope * d: e.activation(
                out=M1[:, g, :].rearrange("p (kl hh q) -> p kl hh q", kl=2, hh=2)[:, :, hh, :],
                in_=D[:, :].rearrange("p (kl q) -> p kl q", kl=2), func=AF.Exp, scale=sc),
                 reads=["D"], writes=["M1"])
    wg = wget(p * 10)
    for tb in range(4):
        proj_block(C, wg, lambda c, tb=tb: (C.hT[:, c, tb * 512:(tb + 1) * 512], "hT"), 512,
                   evac_silu(C, C.GT[:, tb * 512:(tb + 1) * 512], "GT"))
    for g, d in enumerate(DILS):
        _l1_group(C, p, g, d, wget, hv, hb, hbc, acc, M1)
    flush_outproj(C)
    uTp = C.uT[:, :].rearrange("p (j r) -> p r j", r=16)
    GTp = C.GT[:, :].rearrange("p (j r) -> p r j", r=16)
    for ch in range(4):
        ts_ = slice(ch * 512, (ch + 1) * 512)
        keys = [("acc", k) for k in range(ch * 4, ch * 4 + 4)]
        S.op("act", lambda e, ts_=ts_: e.activation(out=acc[:, 1, ts_], in_=acc[:, 1, ts_], func=AF.Ln), reads=keys, writes=keys)
        S.op("act", lambda e, ts_=ts_: e.activation(out=acc[:, 1, ts_], in_=acc[:, 1, ts_], func=AF.Exp, scale=-1.0), reads=keys, writes=keys)
        S.op("dve", lambda e, ts_=ts_: e.tensor_tensor(out=acc[:, 0, ts_], in0=acc[:, 0, ts_], in1=acc[:, 1, ts_], op=ALU.mult),
             reads=keys, writes=keys)
        S.op("dve", lambda e, ts_=ts_, ch=ch: e.tensor_tensor(out=uTp[:, 4 * ch:4 * ch + 4, :],
                                                            in0=acc[:, 0, ts_].rearrange("p (r j) -> p r j", r=4),
                                                            in1=GTp[:, 4 * ch:4 * ch + 4, :], op=ALU.mult),
             reads=keys + ["GT"], writes=[("uT", tb) for tb in range(4)])
    for tb in range(4):
        pend_outproj(C, p, tb)


def _l1_group(C, p, g, d, wget, hv, hb, hbc, acc, M1):
    S = C.S
    halo = 64 * d
    base = HALO1 - halo
    ncols = OWN + 2 * halo
    L = ncols // d
    Lq = OWN // d
    wq, wk, wv = wget(p * 10 + 1 + g * 3), wget(p * 10 + 2 + g * 3), wget(p * 10 + 3 + g * 3)
    hn = min(halo, 256)
    hblks = [(True, w0, hn) for w0 in range(base, HALO1, hn)] + \
            [(True, w0, hn) for w0 in range(HALO1 + OWN, HALO1 + OWN + halo, hn)]
    oblks = [(False, HALO1 + tb * 512, 512) for tb in range(4)]
    blks = []
    per = -(-len(hblks) // 4)
    for i in range(4):
        blks.extend(hblks[i * per:(i + 1) * per])
        blks.append(oblks[i])
    for (is_h, w0, n) in blks:
        col = w0 - base
        if is_h:
            hs_ = hbc[0] % 2
            hbc[0] += 1
            S.dma("sp", lambda e, hs_=hs_, w0=w0, n=n: e.dma_start(out=hb[:, hs_, :, 0:n], in_=hv[:, :, w0:w0 + n]),
                  reads=["hwin"], writes=[("hb", hs_)], slot=("hb", hs_))
            h_of = lambda c, hs_=hs_, n=n: (hb[:, hs_, c, 0:n], ("hb", hs_))
        else:
            t0 = w0 - HALO1
            h_of = lambda c, t0=t0, n=n: (C.hT[:, c, t0:t0 + n], "hT")
        if d == 1:
            kdst, vdst = C.KT[:, col:col + n], C.VT[:, col:col + n]
        else:
            kdst = C.KT[:, 0:ncols].rearrange("p (r i) -> p r i", r=d)[:, :, col // d:(col + n) // d]
            vdst = C.VT[:, 0:ncols].rearrange("p (r i) -> p r i", r=d)[:, :, col // d:(col + n) // d]
        proj_block(C, wk, h_of, n, evac_copy_act(C, kdst, "KT", d))
        if is_h:
            side = 0 if w0 < HALO1 else 1
            proj_block(C, wv, h_of, n, evac_scaled_dve(C, vdst, "VT", C.vcol[:, 2 + side:3 + side], d))
        else:
            proj_block(C, wv, h_of, n, evac_copy_dve(C, vdst, "VT", d))
    nq = OWN // (128 * d)
    nkt = nq + 1
    tiles = []
    for r in range(d):
        for kt in range(nkt):
            c0 = r * L + 128 * kt
            tiles.append((r * nkt + kt, C.VT[:, c0:c0 + 128], None))
    emit_vtrans(C, tiles)
    for tb in range(4):
        proj_block(C, wq, lambda c, tb=tb: (C.hT[:, c, tb * 512:(tb + 1) * 512], "hT"), 512,
                   evac_q(C, tb * 512, 512, d))
    units = [(r, qt) for r in range(d) for qt in range(nq)]

    def qk(u):
        r, qt = units[u]
        sb, eb = u % 4, u % 4
        qc = r * Lq + 128 * qt
        for kl in range(2):
            c0 = r * L + 128 * (qt + kl)
            S.op("pe", lambda e, kl=kl, c0=c0: e.matmul(
                C.Sp[:, sb, kl * 256:(kl + 1) * 256].rearrange("p (a c) -> p a c", a=2),
                lhsT=C.KT[:, c0:c0 + 128], rhs=C.QT[:, :, qc:qc + 128], start=True, stop=True),
                reads=["KT", "QT"], writes=[("S", sb)])
        S.op("act", lambda e: e.activation(out=C.E[:, eb, 0:512], in_=C.Sp[:, sb, 0:512], func=AF.Exp, scale=0.125),
             reads=[("S", sb)], writes=[("E", eb)])
        S.op(MASK_ENG if u % 2 == 1 else "dve", lambda e: e.tensor_tensor(out=C.P[:, eb, 0:512], in0=C.E[:, eb, 0:512], in1=M1[:, g, :], op=ALU.mult),
             reads=[("E", eb), "M1"], writes=[("P", eb)])

    def pv(u):
        r, qt = units[u]
        sb = u % 4
        ob = u % 2
        q0 = 128 * qt * d + r
        for kl in range(2):
            kt = qt + kl
            ti = r * nkt + kt
            vsel = 1 if kt == 0 else (2 if kt == nkt - 1 else 0)
            S.op("pe", lambda e, kl=kl, ti=ti: e.matmul(
                C.O[:, ob, 0:256], lhsT=C.V[:, ti, :], rhs=C.P[:, sb, kl * 256:(kl + 1) * 256],
                start=(kl == 0), stop=False), reads=["V", ("P", sb)], writes=[("O", ob)])
            S.op("pe", lambda e, kl=kl, vsel=vsel: e.matmul(
                C.O[:, ob, 256:512], lhsT=C.vm[:, vsel, :], rhs=C.P[:, sb, kl * 256:(kl + 1) * 256],
                start=False, stop=(kl == 1)), reads=["vm", ("P", sb)], writes=[("O", ob)])
        A4 = acc.rearrange("p a (r j) -> p a r j", r=16)
        if d == 16:
            rows = [r]
        elif d == 4:
            rows = [r + 4 * b for b in range(4)]
        else:
            rows = list(range(16))
        keys = [("acc", k) for k in rows]
        for hh in range(2):
            hs = slice(64 * hh, 64 * hh + 64)
            srcv = C.O[hs, ob, :].rearrange("p (a b c) -> p a b c", a=2, b=2)[:, :, hh, :]
            if d == 16:
                dst, src = A4[hs, :, r, :], srcv
            elif d == 4:
                dst = A4[hs, :, r:r + 13:4, 32 * qt:32 * qt + 32]
                src = srcv.rearrange("p a (x b) -> p a b x", b=4)
            else:
                dst = A4[hs, :, :, 8 * qt:8 * qt + 8]
                src = srcv.rearrange("p a (x r) -> p a r x", r=16)
            if g == 0:
                S.op("dve", lambda e, dst=dst, src=src: e.tensor_copy(out=dst, in_=src), reads=[("O", ob)], writes=keys)
            else:
                S.op("dve", lambda e, dst=dst, src=src: e.tensor_tensor(out=dst, in0=src, in1=dst, op=ALU.add),
                     reads=[("O", ob)] + keys, writes=keys)

    for u0 in range(min(3, len(units))):
        qk(u0)
    for u in range(len(units)):
        if u + 3 < len(units):
            qk(u + 3)
        pv(u)


def wblock(w, col0):
    return np.ascontiguousarray(w[:, col0:col0 + 128].reshape(NCH, 128, 128).transpose(1, 0, 2).reshape(128, 1024))


def gam_layout(*gs):
    return np.ascontiguousarray(np.concatenate([g.reshape(NCH, 128).T for g in gs], axis=1)).astype(np.float32)


def make_cst(j):
    cst = np.zeros((128, CSTW), np.float32)
    cst[:, 0:128] = np.eye(128, dtype=np.float32)
    vl = np.ones(128, np.float32)
    vr = np.ones(128, np.float32)
    if j == 0:
        vl[:64] = 0.0
    if j == 3:
        vr[64:] = 0.0
    cst[:, 128:256] = 1.0
    cst[:, 256:384] = vl[:, None]
    cst[:, 384:512] = vr[:, None]
    cst[:, 512:576] = 1.0
    cst[:, 704:768] = 1.0
    cst[:, 768] = vl
    cst[:, 769] = vr
    cst[:, 770] = 0.0 if j == 0 else 1.0
    cst[:, 771] = 0.0 if j == 3 else 1.0
    return cst


def make_tab(rpb, j):
    kp = np.arange(128)
    kr2, kc = kp // 64, kp % 64
    q = np.arange(128)
    qr2, qc = q // 64, q % 64
    cstart = np.clip(qc - 8, 0, 64 - 16)
    colv = (kc[:, None] >= cstart[None, :]) & (kc[:, None] < cstart[None, :] + 16)
    coff = np.clip(kc[:, None] - qc[None, :] + 15, 0, 30)
    out = np.full((16, NTABBLK, 128, 128), NEG, np.float32)
    for key, (toff, offs) in TAB_OFFS.items():
        lt = 5 if key == "G" else key
        m = 16 * j + lt
        r = 2 * m + qr2
        rs = np.clip(r - 4, 0, 128 - 8)
        for k, o in enumerate(offs):
            krow = 2 * (m + o) + kr2
            rowv = (krow[:, None] >= rs[None, :]) & (krow[:, None] < rs[None, :] + 8) & (krow[:, None] >= 0) & (krow[:, None] < 128)
            valid = rowv & colv
            roff = np.clip(krow[:, None] - r[None, :] + 7, 0, 14)
            vals = rpb[:, roff, coff]
            out[:, toff + k] = np.where(valid[None], vals, np.float32(NEG))
    out = out.reshape(NPAIR, 2, NTABBLK, 128, 128).transpose(0, 3, 2, 1, 4).reshape(NPAIR, 128, 2 * NTABBLK * 128)
    return np.ascontiguousarray(out)


def window_T(xb, t0, halo):
    out = np.zeros((xb.shape[1], OWN + 2 * halo), xb.dtype)
    lo, hi = t0 - halo, t0 + OWN + halo
    a, b = max(lo, 0), min(hi, SEQ)
    out[:, a - lo:b - lo] = xb[a:b].T
    return out


_CACHE = {}


def get_prog(name, builder):
    if name not in _CACHE:
        nc = bass.Bass("TRN2", target_bir_lowering=False)
        es = ExitStack()
        builder(nc, es)
        _CACHE[name] = (nc, es)
    return _CACHE[name][0]


def run_l0(x, norm_0, w_in_0, rpb_0, w_out_0, norm_1, norm_f, trace=False):
    nc = get_prog("l0", build_l0)
    w_in_b = np.stack([wblock(w_in_0, k * 1024 + p * 128) for p in range(NPAIR) for k in range(4)])
    w_out_b = np.ascontiguousarray(w_out_0.reshape(NPAIR, 128, 1024))
    gam = gam_layout(norm_0, norm_1, norm_f)
    tabs = [make_tab(rpb_0, j) for j in range(4)]
    csts = [make_cst(j) for j in range(4)]
    in_maps = []
    for c in range(NCORE):
        b, j = divmod(c, 4)
        in_maps.append({"xw": window_T(x[b], j * OWN, HALO0), "gamd": gam, "w_in0": w_in_b, "w_out0": w_out_b,
                        "tab0": tabs[j], "cst": csts[j]})
    res = run_bass_kernel_spmd(nc, in_maps, core_ids=list(range(NCORE)), trace=trace)
    return res


def make_dtab():
    kp = np.arange(128)[:, None]
    j = np.arange(128)[None, :]
    out = np.empty((128, 256), np.float32)
    for kl, sh in enumerate((-64, 64)):
        delta = np.abs(kp + sh - j)
        out[:, kl * 128:(kl + 1) * 128] = np.where(delta <= 64, delta, 1.0e6)
    return out


def run_l1(x1T_list, h1T_list, w_in_1, w_out_1, norm_0, norm_1, norm_f, trace=False):
    nc = get_prog("l1", build_l1)
    cols = []
    for p in range(NPAIR):
        cols.append(9216 + p * 128)
        for g in range(3):
            for k in range(3):
                cols.append(g * 3072 + k * 1024 + p * 128)
    w_in_b = np.stack([wblock(w_in_1, c0) for c0 in cols])
    w_out_b = np.ascontiguousarray(w_out_1.reshape(NPAIR, 128, 1024))
    gam = gam_layout(norm_0, norm_1, norm_f)
    dt = make_dtab()
    csts = [make_cst(j) for j in range(4)]
    in_maps = []
    for c in range(NCORE):
        b, j = divmod(c, 4)
        hw = np.zeros((D_MODEL, W1), h1T_list[c].dtype)
        hw[:, HALO1:HALO1 + OWN] = h1T_list[c]
        if j > 0:
            hw[:, 0:HALO1] = h1T_list[c - 1][:, OWN - HALO1:]
        if j < 3:
            hw[:, HALO1 + OWN:] = h1T_list[c + 1][:, 0:HALO1]
        in_maps.append({"x1w": x1T_list[c], "h1w": hw, "gamd": gam, "w_in1": w_in_b, "w_out1": w_out_b,
                        "cst": csts[j], "dtab": dt})
    return run_bass_kernel_spmd(nc, in_maps, core_ids=list(range(NCORE)), trace=trace)


def kernel(x, norm_0, w_in_0, rpb_0, w_out_0, norm_1, w_in_1, w_out_1, norm_f):
    f = lambda a: np.ascontiguousarray(np.asarray(a, dtype=np.float32))
    x, norm_0, w_in_0, rpb_0, w_out_0, norm_1, w_in_1, w_out_1, norm_f = map(
        f, (x, norm_0, w_in_0, rpb_0, w_out_0, norm_1, w_in_1, w_out_1, norm_f))
    nc, in_maps = run_fused(x, norm_0, w_in_0, rpb_0, w_out_0, norm_1, w_in_1, w_out_1, norm_f)
    res = run_bass_kernel_spmd(nc, in_maps, core_ids=list(range(NCORE)))
    out = np.empty((2, SEQ, D_MODEL), np.float32)
    for c in range(NCORE):
        b, j = divmod(c, 4)
        out[b, j * OWN:(j + 1) * OWN, :] = np.asarray(res.results[c]["yT"]).T
    return out


def make_hidx(j):
    l = j - 1 if j > 0 else j
    r = j + 1 if j < 3 else j
    p = np.arange(128, dtype=np.int64)
    out = np.zeros((128, NHIDX), np.uint32)
    for side, nbr in enumerate((l, r)):
        for c in range(NCH):
            for blk in range(4):
                tb = 4 + blk if side == 0 else blk
                out[:, side * 32 + c * 4 + blk] = (nbr * D_MODEL + c * 128 + p) * 8 + tb
    return out


def l1_cols():
    cols = []
    for p in range(NPAIR):
        cols.append(9216 + p * 128)
        for g in range(3):
            for k in range(3):
                cols.append(g * 3072 + k * 1024 + p * 128)
    return cols


def run_fused(x, norm_0, w_in_0, rpb_0, w_out_0, norm_1, w_in_1, w_out_1, norm_f, trace=False):
    nc = get_prog("fused", build_fused)
    w_in0_b = np.stack([wblock(w_in_0, k * 1024 + p * 128) for p in range(NPAIR) for k in range(4)])
    w_out0_b = np.ascontiguousarray(w_out_0.reshape(NPAIR, 128, 1024))
    w_in1_b = np.stack([wblock(w_in_1, c0) for c0 in l1_cols()])
    w_out1_b = np.ascontiguousarray(w_out_1.reshape(NPAIR, 128, 1024))
    gam = gam_layout(norm_0, norm_1, norm_f)
    dt = make_dtab()
    tabs = [make_tab(rpb_0, j) for j in range(4)]
    csts = [make_cst(j) for j in range(4)]
    hidx = [make_hidx(j) for j in range(4)]
    in_maps = []
    for c in range(NCORE):
        b, j = divmod(c, 4)
        in_maps.append({"xw": window_T(x[b], j * OWN, HALO0 + HALO1), "gamd": gam, "w_in0": w_in0_b, "w_out0": w_out0_b,
                        "tab0": tabs[j], "cst": csts[j], "w_in1": w_in1_b, "w_out1": w_out1_b, "dtab": dt})
    return nc, in_maps
```

```python
import numpy as np
from contextlib import ExitStack
import concourse.bass as bass
import concourse.mybir as mybir
from concourse.bass_utils import run_bass_kernel_spmd

F32 = mybir.dt.float32
BF16 = mybir.dt.bfloat16
AF = mybir.ActivationFunctionType
ALU = mybir.AluOpType


class Op:
    __slots__ = ("eng", "fn", "deps", "is_dma", "slot", "ord", "marked", "name")

    def __init__(self, eng, fn, is_dma=False, slot=None, name=""):
        self.eng = eng
        self.fn = fn
        self.deps = []
        self.is_dma = is_dma
        self.slot = slot
        self.ord = None
        self.marked = False
        self.name = name


class Sched:
    ENGS = ("pe", "act", "dve", "pool", "sp")

    def __init__(self, nc, es):
        self.nc = nc
        self.es = es
        self.q = {e: [] for e in self.ENGS}
        self.last_w = {}
        self.readers = {}
        self.finals = []
        self.slot_count = {}
        self.n_sb = 0

    def sbuf(self, name, shape, dtype):
        return self.es.enter_context(self.nc.sbuf_tensor(name, shape, dtype))

    def psum(self, name, shape, dtype):
        return self.es.enter_context(self.nc.psum_tensor(name, shape, dtype))

    def _add(self, op, reads, writes, deps):
        dl = []
        for r in reads:
            dl.extend(self.last_w.get(r, ()))
            self.readers.setdefault(r, []).append(op)
        for w_ in writes:
            prev = self.last_w.get(w_, [])
            dl.extend(prev)
            for rd in self.readers.get(w_, ()):
                if rd is not op:
                    dl.append(rd)
            if op.is_dma and prev and all(q.is_dma for q in prev):
                self.last_w[w_] = prev + [op]
            else:
                self.last_w[w_] = [op]
            self.readers[w_] = []
        dl.extend(deps)
        seen = set()
        for d in dl:
            if d is op or id(d) in seen:
                continue
            seen.add(id(d))
            if d.eng == "pe" and op.eng == "pe" and not d.is_dma and not op.is_dma:
                continue
            op.deps.append(d)
        self.q[op.eng].append(op)
        return op

    def op(self, eng, fn, reads=(), writes=(), deps=(), name=""):
        return self._add(Op(eng, fn, name=name), reads, writes, deps)

    def dma(self, eng, fn, reads=(), writes=(), deps=(), slot=None, name=""):
        if slot is None:
            slot = ("auto", len(self.slot_count))
        op = Op(eng, fn, is_dma=True, slot=slot, name=name)
        k = self.slot_count.get(slot, 0) + 1
        self.slot_count[slot] = k
        op.ord = k
        return self._add(op, reads, writes, deps)

    def final_wait(self, eng, ops):
        self.finals.append((eng, list(ops)))

    def emit(self):
        nc = self.nc
        for e in self.ENGS:
            for op in self.q[e]:
                for d in op.deps:
                    d.marked = True
        for _, ops in self.finals:
            for d in ops:
                d.marked = True
        for e in self.ENGS:
            for op in self.q[e]:
                if op.is_dma:
                    op.marked = True
        for e in self.ENGS:
            k = 0
            for op in self.q[e]:
                if not op.is_dma and op.marked:
                    k += 1
                    op.ord = k
        sems = {}

        def sem_of(op):
            key = ("dma", op.slot) if op.is_dma else ("eng", op.eng)
            if key not in sems:
                sems[key] = self.es.enter_context(nc.semaphore("s%d" % len(sems)))
            return sems[key]

        def val_of(op):
            return op.ord * 16 if op.is_dma else op.ord

        for e in self.ENGS:
            for op in self.q[e]:
                if op.marked:
                    sem_of(op)
        finals = {}
        for eng, ops in self.finals:
            finals.setdefault(eng, []).extend(ops)
        self.n_waits = 0

        def run(engname, engine):
            known = {}

            def need(d):
                s = sem_of(d)
                v = val_of(d)
                if known.get(id(s), 0) >= v:
                    return
                known[id(s)] = v
                engine.wait_ge(s, v)
                self.n_waits += 1

            for op in self.q[engname]:
                best = {}
                for d in op.deps:
                    s = sem_of(d)
                    if id(s) not in best or val_of(best[id(s)]) < val_of(d):
                        best[id(s)] = d
                for d in best.values():
                    need(d)
                ins = op.fn(engine)
                if op.marked:
                    ins.then_inc(sem_of(op), 16 if op.is_dma else 1)
            for d in finals.get(engname, ()):
                need(d)

        with nc.Block() as block:
            @block.sync
            def _(eng):
                run("sp", eng)

            @block.tensor
            def _(eng):
                run("pe", eng)

            @block.scalar
            def _(eng):
                run("act", eng)

            @block.vector
            def _(eng):
                run("dve", eng)

            @block.gpsimd
            def _(eng):
                run("pool", eng)


D_MODEL = 1024
SEQ = 8192
NCORE = 8
OWN = 2048
NCH = 8
NPAIR = 8
HALO0 = 256
W0 = OWN + 2 * HALO0
HALO1 = 1024
W1 = OWN + 2 * HALO1
DILS = (1, 4, 16)
NEG = -30000.0
EPS = 1e-6
TAB_OFFS = {"G": (0, (-2, -1, 0, 1, 2)), 0: (5, (-2, -1, 0, 1, 2, 3)), 1: (11, (-2, -1, 0, 1, 2)),
            14: (16, (-2, -1, 0, 1, 2)), 15: (21, (-3, -2, -1, 0, 1, 2))}
NTABBLK = 27
U32 = mybir.dt.uint32
NHIDX = 64
MASK_ENG = "pool"
FIN_ENG = "pool"
CSTW = 128 + 5 * 128 + 4


def tab_for(lt):
    return TAB_OFFS[lt] if lt in TAB_OFFS else TAB_OFFS["G"]


class Ctx:
    pass


def common_setup(nc, es, hcols=W0):
    C = Ctx()
    S = Sched(nc, es)
    C.S = S
    C.nc = nc
    C.G = S.psum("G", [128, 2, 512], F32)
    C.Sp = S.psum("Sp", [128, 4, 512], F32)
    C.O = S.psum("O", [128, 2, 512], F32)
    C.gi = 0
    C.xT = S.sbuf("xT", [128, NCH, OWN], F32)
    C.hT = S.sbuf("hT", [128, NCH, hcols], BF16)
    C.QT = S.sbuf("QT", [128, 2, OWN], BF16)
    C.KV = S.sbuf("KV", [128, 2 * W1], BF16)
    C.KT = C.KV[:, 0:W1]
    C.VT = C.KV[:, W1:2 * W1]
    C.V = S.sbuf("V", [128, 32, 128], BF16)
    C.GT = S.sbuf("GT", [128, OWN], BF16)
    C.uT = S.sbuf("uT", [128, OWN], BF16)
    C.Wb = S.sbuf("Wb", [128, 8, 1024], BF16)
    C.Wo = S.sbuf("Wo", [128, 2, 1024], BF16)
    C.E = S.sbuf("E", [128, 4, 512], BF16)
    C.P = S.sbuf("P", [128, 4, 512], BF16)
    C.sq = S.sbuf("sq", [128, 2, 512], BF16)
    C.rstd = S.sbuf("rstd", [128, 512], F32)
    C.tmp = S.sbuf("tmp", [128, 4, 128], F32)
    C.onesf = S.sbuf("onesf", [128, 128], BF16)
    C.vm = S.sbuf("vm", [128, 5, 128], BF16)
    C.vcol = S.sbuf("vcol", [128, 4], F32)
    C.gam = S.sbuf("gam", [128, 3, NCH], F32)
    C.wslot = 0
    C.pend = []
    C.oi = 0
    C.pops = 2
    C.nrm = 0
    C.epsb = S.sbuf("epsb", [128, 1], F32)
    C.ident = S.sbuf("ident", [128, 128], BF16)
    S.op("dve", lambda e: e.memset(C.epsb[:], EPS), writes=["epsb"])
    S.op("dve", lambda e: e.memset(C.onesf[:], 1.0), writes=["onesf"])
    S.op("pool", lambda e: e.memset(C.QT[:], 0.0), writes=["QT"])
    return C


def next_g(C):
    gi = C.gi
    C.gi = (gi + 1) % 2
    return gi


def load_wblock(C, src_ap):
    S = C.S
    s = C.wslot
    C.wslot = (s + 1) % 8
    S.dma("pool", lambda e: e.dma_start(out=C.Wb[:, s, :], in_=src_ap), writes=[("W", s)], slot=("W", s))
    return s


def proj_block(C, ws, h_of, n, evac):
    S = C.S
    gi = next_g(C)
    gk = ("G", gi)
    for c in range(NCH):
        rhs, rk = h_of(c)
        S.op("pe", lambda e, c=c, rhs=rhs: e.matmul(C.G[:, gi, 0:n], lhsT=C.Wb[:, ws, c * 128:(c + 1) * 128],
                                                    rhs=rhs, start=(c == 0), stop=(c == NCH - 1)),
             reads=[("W", ws), rk], writes=[gk])
    evac(C.G[:, gi, 0:n], gk)
    pop_outproj(C, C.pops)


def emit_norm(C, x_of, ntok, gidx, out_of):
    t0 = 0
    while t0 < ntok:
        n = min(512, ntok - t0)
        _norm_block(C, x_of, t0, n, gidx, out_of)
        t0 += n


def rstd_buf(C):
    k = C.nrm % 2
    C.nrm += 1
    if k == 0:
        return C.rstd[:, :], [("rstd", 0)]
    return C.tmp[:, :, :].rearrange("p a c -> p (a c)"), [("rstd", 1)] + [("tmp", i) for i in range(4)]


def _norm_block(C, x_of, t0, n, gidx, out_of):
    S = C.S
    gi = next_g(C)
    gk = ("G", gi)
    rs, rkeys = rstd_buf(C)
    for c in range(NCH):
        xa, xk = x_of(c, t0, n)
        sb = c % 2
        S.op("act", lambda e, xa=xa, sb=sb: e.activation(out=C.sq[:, sb, 0:n], in_=xa, func=AF.Square),
             reads=[xk], writes=[("sq", sb)])
        S.op("pe", lambda e, c=c, sb=sb: e.matmul(C.G[:, gi, 0:n], lhsT=C.onesf[:, :], rhs=C.sq[:, sb, 0:n],
                                                  start=(c == 0), stop=(c == NCH - 1)),
             reads=["onesf", ("sq", sb)], writes=[gk])
    S.op("act", lambda e: e.activation(out=rs[:, 0:n], in_=C.G[:, gi, 0:n], func=AF.Ln,
                                       scale=1.0 / D_MODEL, bias=C.epsb[:, 0:1]),
         reads=[gk, "epsb"], writes=rkeys)
    S.op("act", lambda e: e.activation(out=rs[:, 0:n], in_=rs[:, 0:n], func=AF.Exp, scale=-0.5), reads=rkeys, writes=rkeys)
    for c in range(NCH):
        xa, xk = x_of(c, t0, n)
        oa, ok = out_of(c, t0, n)
        if False:
            S.op("pool", lambda e, xa=xa, oa=oa, c=c: e.scalar_tensor_tensor(
                out=oa, in0=xa, scalar=C.gam[:, gidx, c:c + 1], in1=rs[:, 0:n], op0=ALU.mult, op1=ALU.mult),
                 reads=[xk, "gam"] + rkeys, writes=[ok])
        else:
            S.op("dve", lambda e, xa=xa, oa=oa, c=c: e.scalar_tensor_tensor(
                out=oa, in0=xa, scalar=C.gam[:, gidx, c:c + 1], in1=rs[:, 0:n], op0=ALU.mult, op1=ALU.mult),
                 reads=[xk, "gam"] + rkeys, writes=[ok])


def emit_outproj(C, pair, wo_src, tbs, ms=None):
    S = C.S
    for tb in tbs:
        for m in (range(NCH) if ms is None else ms):
            gi = C.oi
            C.oi = (gi + 1) % 4
            gk = ("S", gi)
            S.op("pe", lambda e, m=m, gi=gi, tb=tb: e.matmul(C.Sp[:, gi, :], lhsT=C.Wo[:, pair % 2, m * 128:(m + 1) * 128],
                                                          rhs=C.uT[:, tb * 512:(tb + 1) * 512], start=True, stop=True),
                 reads=[("Wo", pair % 2), ("uT", tb)], writes=[gk])
            S.op("dve", lambda e, m=m, gi=gi, tb=tb: e.tensor_tensor(out=C.xT[:, m, tb * 512:(tb + 1) * 512],
                                                                 in0=C.Sp[:, gi, :], in1=C.xT[:, m, tb * 512:(tb + 1) * 512],
                                                                 op=ALU.add),
                 reads=[gk, ("x", m, tb)], writes=[("x", m, tb)])


def pend_outproj(C, pair, tb):
    for m in range(NCH):
        C.pend.append((pair, tb, m))


def pop_outproj(C, n=1):
    for _ in range(n):
        if C.pend:
            pair, tb, m = C.pend.pop(0)
            emit_outproj(C, pair, None, [tb], [m])


def flush_outproj(C):
    pop_outproj(C, len(C.pend))


def deint(ps, d):
    return ps if d == 1 else ps.rearrange("p (i r) -> p r i", r=d)


def evac_copy_act(C, dst, dkey, d=1):
    def f(ps, gk):
        C.S.op("act", lambda e: e.activation(out=dst, in_=deint(ps, d), func=AF.Copy), reads=[gk], writes=[dkey])
    return f


def evac_copy_dve(C, dst, dkey, d=1):
    def f(ps, gk):
        C.S.op("dve", lambda e: e.tensor_copy(out=dst, in_=deint(ps, d)), reads=[gk], writes=[dkey])
    return f


def evac_q(C, c0, n, d=1):
    def f(ps, gk):
        for hh in range(2):
            hs = slice(64 * hh, 64 * hh + 64)
            if d == 1:
                dst = C.QT[hs, hh, c0:c0 + n]
            else:
                dst = C.QT[hs, hh, :].rearrange("p (r i) -> p r i", r=d)[:, :, c0 // d:(c0 + n) // d]
            C.S.op("act", lambda e, hs=hs, dst=dst: e.activation(out=dst, in_=deint(ps[hs, :], d), func=AF.Copy), reads=[gk], writes=["QT"])
    return f


def evac_scaled_dve(C, dst, dkey, flag_ap, d=1):
    def f(ps, gk):
        C.S.op("dve", lambda e: e.tensor_scalar(out=dst, in0=deint(ps, d), scalar1=flag_ap, scalar2=None, op0=ALU.mult), reads=[gk, "vcol"], writes=[dkey])
    return f


def _old_evac_copy_act(C, dst, dkey):
    def f(ps, gk):
        C.S.op("act", lambda e: e.activation(out=dst, in_=ps, func=AF.Copy), reads=[gk], writes=[dkey])
    return f


def _old_evac_copy_dve(C, dst, dkey):
    def f(ps, gk):
        C.S.op("dve", lambda e: e.tensor_copy(out=dst, in_=ps), reads=[gk], writes=[dkey])
    return f


def _old_evac_q(C, c0, n):
    def f(ps, gk):
        C.S.op("act", lambda e: e.activation(out=C.QT[0:64, 0, c0:c0 + n], in_=ps[0:64, :], func=AF.Copy), reads=[gk], writes=["QT"])
        C.S.op("act", lambda e: e.activation(out=C.QT[64:128, 1, c0:c0 + n], in_=ps[64:128, :], func=AF.Copy), reads=[gk], writes=["QT"])
    return f


def _old_evac_scaled_dve(C, dst, dkey, flag_ap):
    def f(ps, gk):
        C.S.op("dve", lambda e: e.tensor_scalar(out=dst, in0=ps, scalar1=flag_ap, scalar2=None, op0=ALU.mult), reads=[gk, "vcol"], writes=[dkey])
    return f


def evac_silu(C, dst, dkey):
    def f(ps, gk):
        C.S.op("act", lambda e: e.activation(out=dst, in_=ps, func=AF.Silu), reads=[gk], writes=[dkey])
    return f


def emit_vtrans(C, tiles):
    for i in range(0, len(tiles), 4):
        _vtrans_group(C, tiles[i:i + 4])


def _vtrans_group(C, grp):
    S = C.S
    gi = C.oi
    C.oi = (gi + 1) % 4
    gk = ("S", gi)
    gb = C.Sp[:, gi, :].bitcast(BF16)
    n = len(grp)
    ti0 = grp[0][0]
    assert [t[0] for t in grp] == list(range(ti0, ti0 + n))
    for j, (ti, src, edge) in enumerate(grp):
        S.op("pe", lambda e, j=j, src=src: e.transpose(gb[:, j * 128:(j + 1) * 128], src, C.ident[:, :]),
             reads=["VT", "ident"], writes=[gk])
    S.op("act", lambda e: e.activation(out=C.V[:, ti0:ti0 + n, :].rearrange("p a c -> p (a c)"), in_=gb[:, 0:n * 128], func=AF.Copy),
         reads=[gk], writes=["V"])


XWF = OWN + 2 * (HALO1 + HALO0)


def l0_decl(C, nc, xwin):
    S = C.S
    io = Ctx()
    io.xw = nc.dram_tensor("xw", [D_MODEL, xwin], F32, kind="ExternalInput").ap()
    io.gam = nc.dram_tensor("gamd", [128, 3 * NCH], F32, kind="ExternalInput").ap()
    io.w_in = nc.dram_tensor("w_in0", [NPAIR * 4, 128, 1024], F32, kind="ExternalInput").ap()
    io.w_out = nc.dram_tensor("w_out0", [NPAIR, 128, 1024], F32, kind="ExternalInput").ap()
    io.tab = nc.dram_tensor("tab0", [NPAIR, 128, 2 * NTABBLK * 128], F32, kind="ExternalInput").ap()
    io.cst = nc.dram_tensor("cst", [128, CSTW], F32, kind="ExternalInput").ap()
    io.xv = io.xw.rearrange("(c p) t -> p c t", p=128)
    io.xh = C.KV[:, :].bitcast(F32).rearrange("p (c t) -> p c t", c=NCH)
    cst = io.cst
    S.dma("sp", lambda e: e.dma_start(out=C.gam[:].rearrange("p a c -> p (a c)"), in_=io.gam), writes=["gam"], slot="c0")
    S.dma("pool", lambda e: e.dma_start(out=C.ident[:], in_=cst[:, 0:128]), writes=["ident"], slot="c1")
    S.dma("pool", lambda e: e.dma_start(out=C.vm[:].rearrange("p a c -> p (a c)"), in_=cst[:, 128:768]), writes=["vm"], slot="c2")
    S.dma("sp", lambda e: e.dma_start(out=C.vcol[:], in_=cst[:, 768:772]), writes=["vcol"], slot="c3")
    return io


def l0_pass(C, io, Tab, rs, NT, special, npair=NPAIR):
    S = C.S
    xv, xh = io.xv, io.xh
    NQ = NT * 128
    WIN = NQ + 2 * HALO0
    NTB = NQ // 512
    for c in range(NCH):
        S.dma("sp", lambda e, c=c: e.dma_start(out=C.xT[:, c, 0:NQ], in_=xv[:, c, rs:rs + NQ]),
              writes=[("x", c, tb) for tb in range(NTB)], slot=("xl", c))
    S.dma("sp", lambda e: e.dma_start(out=xh[:, :, 0:HALO0], in_=xv[:, :, rs - HALO0:rs]), writes=["xh", "KT", "VT"], slot="xh0")
    S.dma("sp", lambda e: e.dma_start(out=xh[:, :, HALO0:2 * HALO0], in_=xv[:, :, rs + NQ:rs + NQ + HALO0]),
          writes=["xh", "KT", "VT"], slot="xh1")

    wslots = {}

    def prefetch(p):
        if p >= npair:
            return
        wslots[p] = [load_wblock(C, io.w_in[p * 4 + k]) for k in range(4)]

    def load_wo(p):
        S.dma("pool", lambda e: e.dma_start(out=C.Wo[:, p % 2, :], in_=io.w_out[p]), writes=[("Wo", p % 2)], slot=("Wo", p % 2))

    def prefetch_tab(p):
        S.dma("pool", lambda e: e.dma_start(out=Tab[:, 0, :], in_=io.tab[p]), writes=[("Tab", k_) for k_ in TAB_OFFS], slot=("Tab", 0))

    prefetch(0)

    emit_norm(C, lambda c, t0, n: (C.xT[:, c, t0:t0 + n], ("x", c, t0 // 512)), NQ, 0,
              lambda c, t0, n: (C.hT[:, c, HALO0 + t0:HALO0 + t0 + n], "hT"))
    for half in range(2):
        emit_norm(C, lambda c, t0, n, half=half: (xh[:, c, half * HALO0 + t0:half * HALO0 + t0 + n], "xh"), HALO0, 0,
                  lambda c, t0, n, half=half: (C.hT[:, c, half * (HALO0 + NQ) + t0:half * (HALO0 + NQ) + t0 + n], "hT"))

    hkey = "hT"
    for p in range(npair):
        prefetch(p + 1)
        load_wo(p)
        prefetch_tab(p)
        _l0_pair(C, p, wslots[p], Tab, NT, WIN, NTB, special)
    flush_outproj(C)


def _l0_pair(C, p, ws, Tab, NT, WIN, NTB, special):
    S = C.S
    hkey = "hT"
    wq, wk, wv, wg = ws
    for tb in range(WIN // 512):
        proj_block(C, wk, lambda c, tb=tb: (C.hT[:, c, tb * 512:(tb + 1) * 512], hkey), 512,
                   evac_copy_act(C, C.KT[:, tb * 512:(tb + 1) * 512], "KT"))
    for tb in range(WIN // 512):
        proj_block(C, wv, lambda c, tb=tb: (C.hT[:, c, tb * 512:(tb + 1) * 512], hkey), 512,
                   evac_copy_dve(C, C.VT[:, tb * 512:(tb + 1) * 512], "VT"))
    emit_vtrans(C, [(wt, C.VT[:, wt * 128:(wt + 1) * 128], None) for wt in range(WIN // 128)])
    for tb in range(NTB):
        proj_block(C, wq, lambda c, tb=tb: (C.hT[:, c, HALO0 + tb * 512:HALO0 + (tb + 1) * 512], hkey), 512,
                   evac_q(C, tb * 512, 512))
    for tb in range(NTB):
        proj_block(C, wg, lambda c, tb=tb: (C.hT[:, c, HALO0 + tb * 512:HALO0 + (tb + 1) * 512], hkey), 512,
                   evac_silu(C, C.GT[:, tb * 512:(tb + 1) * 512], "GT"))
    for key in ((0, 1, "G", 14, 15) if special else ("G",)):
        toff_, offs_ = TAB_OFFS[key]
        a_, b_ = toff_ * 256, (toff_ + len(offs_)) * 256
        S.op("act", lambda e, a_=a_, b_=b_: e.activation(out=Tab[:, 0, a_:b_], in_=Tab[:, 0, a_:b_], func=AF.Exp),
             reads=[("Tab", key)], writes=[("Tab", key)])
    tabf = (lambda lt: tab_for(lt)) if special else (lambda lt: TAB_OFFS["G"])
    units = []
    for lt in range(NT):
        toff, offs = tabf(lt)
        for i in range(0, len(offs), 2):
            units.append((lt, toff, offs[i:i + 2], i, i == 0, i + 2 >= len(offs)))
    tkey = (lambda lt: lt if lt in TAB_OFFS else "G") if special else (lambda lt: "G")

    def qk(u):
        lt, toff, offs, kbase, first, last = units[u]
        sb, eb = u % 4, u % 4
        n = len(offs)
        for k, o in enumerate(offs):
            wt = lt + 2 + o
            S.op("pe", lambda e, wt=wt, k=k: e.matmul(
                C.Sp[:, sb, k * 256:(k + 1) * 256].rearrange("p (a c) -> p a c", a=2), lhsT=C.KT[:, wt * 128:(wt + 1) * 128],
                rhs=C.QT[:, :, lt * 128:(lt + 1) * 128], start=True, stop=True),
                reads=["KT", "QT"], writes=[("S", sb)])
        S.op("act", lambda e: e.activation(out=C.E[:, eb, 0:n * 256], in_=C.Sp[:, sb, 0:n * 256], func=AF.Exp, scale=0.125),
             reads=[("S", sb)], writes=[("E", eb)])
        t0 = (toff + kbase) * 256
        S.op("dve", lambda e: e.tensor_tensor(out=C.P[:, eb, 0:n * 256], in0=C.E[:, eb, 0:n * 256],
                                                 in1=Tab[:, 0, t0:t0 + n * 256], op=ALU.mult),
             reads=[("E", eb), ("Tab", tkey(lt))], writes=[("P", eb)])

    def pv(u):
        lt, toff, offs, kbase, first, last = units[u]
        eb = u % 4
        ob = lt % 2
        for k, o in enumerate(offs):
            wt = lt + 2 + o
            S.op("pe", lambda e, wt=wt, k=k: e.matmul(
                C.O[:, ob, 0:256], lhsT=C.V[:, wt, :], rhs=C.P[:, eb, k * 256:(k + 1) * 256],
                start=(first and k == 0), stop=False), reads=["V", ("P", eb)], writes=[("O", ob)])
            for hh in range(2):
                S.op("pe", lambda e, k=k, hh=hh, fin=(last and k == len(offs) - 1 and hh == 1): e.matmul(
                    C.O[:, ob, 256:384], lhsT=C.vm[:, 3 + hh, :], rhs=C.P[:, eb, k * 256 + hh * 128:k * 256 + (hh + 1) * 128],
                    start=False, stop=fin), reads=["vm", ("P", eb)], writes=[("O", ob)])
        if last:
            td, tn = 2 * (lt % 2), 2 * (lt % 2) + 1
            S.op("act", lambda e: e.activation(out=C.tmp[:, td, :], in_=C.O[:, ob, 256:384], func=AF.Ln), reads=[("O", ob)], writes=[("tmp", td)])
            S.op("act", lambda e: e.activation(out=C.tmp[:, td, :], in_=C.tmp[:, td, :], func=AF.Exp, scale=-1.0), reads=[("tmp", td)], writes=[("tmp", td)])
            for hh in range(2):
                hs = slice(64 * hh, 64 * hh + 64)
                S.op("dve", lambda e, hs=hs, hh=hh: e.tensor_tensor(out=C.tmp[hs, tn, :], in0=C.O[hs, ob, hh * 128:(hh + 1) * 128],
                                                                 in1=C.tmp[hs, td, :], op=ALU.mult),
                     reads=[("O", ob), ("tmp", td)], writes=[("tmp", tn)])
            S.op(FIN_ENG, lambda e: e.tensor_tensor(out=C.uT[:, lt * 128:(lt + 1) * 128], in0=C.tmp[:, tn, :],
                                                    in1=C.GT[:, lt * 128:(lt + 1) * 128], op=ALU.mult),
                 reads=[("tmp", tn), "GT"], writes=[("uT", lt // 4)])
            if lt % 4 == 3:
                pend_outproj(C, p, lt // 4)

    for u0 in range(min(3, len(units))):
        qk(u0)
    for u in range(len(units)):
        if u + 3 < len(units):
            qk(u + 3)
        pv(u)


def build_l0(nc, es, npair=NPAIR):
    C = common_setup(nc, es)
    S = C.S
    io = l0_decl(C, nc, W0)
    x1o = nc.dram_tensor("x1T", [D_MODEL, OWN], F32, kind="ExternalOutput").ap()
    h1o = nc.dram_tensor("h1T", [D_MODEL, OWN], BF16, kind="ExternalOutput").ap()
    Tab = S.sbuf("Tab", [128, 1, 2 * NTABBLK * 128], BF16)
    hout = S.sbuf("hout", [128, 2, 512], BF16)
    l0_pass(C, io, Tab, HALO0, 16, True, npair=npair)
    outs = []
    for c in range(NCH):
        outs.append(S.dma("sp", lambda e, c=c: e.dma_start(out=x1o.rearrange("(c p) t -> p c t", p=128)[:, c, :], in_=C.xT[:, c, :]),
                          reads=[("x", c, tb) for tb in range(4)], slot=("xo", c)))
    hov = h1o.rearrange("(c p) t -> p c t", p=128)
    emit_norm_out(C, 1, hout, hov, outs, 4)
    S.final_wait("sp", outs)
    S.emit()
    return C


def emit_norm_out(C, gidx, stage, dst_view, outs, ntb=4, dcol=0):
    k = [0]
    for tb in range(ntb):
        emit_norm_block_out(C, tb, gidx, stage, dst_view, outs, k, dcol)


def emit_norm_block_out(C, tb, gidx, stage, dst_view, outs, k, dcol=0):
    S = C.S
    n = 512
    t0 = tb * 512
    gi = next_g(C)
    gk = ("G", gi)
    for c in range(NCH):
        sb = c % 2
        S.op("act", lambda e, c=c, sb=sb: e.activation(out=C.sq[:, sb, :], in_=C.xT[:, c, t0:t0 + n], func=AF.Square),
             reads=[("x", c, tb)], writes=[("sq", sb)])
        S.op("pe", lambda e, c=c, sb=sb: e.matmul(C.G[:, gi, :], lhsT=C.onesf[:, :], rhs=C.sq[:, sb, :],
                                                  start=(c == 0), stop=(c == NCH - 1)),
             reads=["onesf", ("sq", sb)], writes=[gk])
    rs, rkeys = rstd_buf(C)
    S.op("act", lambda e: e.activation(out=rs[:, :], in_=C.G[:, gi, :], func=AF.Ln, scale=1.0 / D_MODEL, bias=C.epsb[:, 0:1]),
         reads=[gk, "epsb"], writes=rkeys)
    S.op("act", lambda e: e.activation(out=rs[:, :], in_=rs[:, :], func=AF.Exp, scale=-0.5), reads=rkeys, writes=rkeys)
    for c in range(NCH):
        b = k[0] % 2
        k[0] += 1
        S.op("dve", lambda e, c=c, b=b: e.scalar_tensor_tensor(out=stage[:, b, :], in0=C.xT[:, c, t0:t0 + n],
                                                             scalar=C.gam[:, gidx, c:c + 1], in1=rs[:, :],
                                                             op0=ALU.mult, op1=ALU.mult),
             reads=[("x", c, tb), "gam"] + rkeys, writes=[("stage", b)])
        outs.append(S.dma("sp", lambda e, c=c, b=b: e.dma_start(out=dst_view[:, c, dcol + t0:dcol + t0 + n], in_=stage[:, b, :]),
                          reads=[("stage", b)], writes=["hwin"], slot=("so", b)))


def build_fused(nc, es, npair=NPAIR):
    C = common_setup(nc, es)
    S = C.S
    C.big = S.sbuf("big", [128, 2 * OWN], F32)
    io = l0_decl(C, nc, XWF)
    w_in = nc.dram_tensor("w_in1", [NPAIR * 10, 128, 1024], F32, kind="ExternalInput").ap()
    w_out = nc.dram_tensor("w_out1", [NPAIR, 128, 1024], F32, kind="ExternalInput").ap()
    dtab = nc.dram_tensor("dtab", [128, 256], F32, kind="ExternalInput").ap()
    yo = nc.dram_tensor("yT", [D_MODEL, OWN], F32, kind="ExternalOutput").ap()
    hwin = nc.dram_tensor("hwin", [D_MODEL, W1], BF16).ap()
    hv = hwin.rearrange("(c p) t -> p c t", p=128)

    Tab = C.big[:, :].bitcast(BF16)[:, 0:2 * NTABBLK * 128].rearrange("p (a c) -> p a c", a=1)
    hb = S.sbuf("hb", [128, 2, NCH, 256], BF16)
    acc = C.big[:, :].rearrange("p (a t) -> p a t", a=2)
    M1 = S.sbuf("M1", [128, 3, 512], BF16)
    D = S.sbuf("D", [128, 256], BF16)
    stage = acc[:, 0, 0:1024].rearrange("p (a c) -> p a c", a=2)
    S.dma("pool", lambda e: e.dma_start(out=D[:], in_=dtab), writes=["D"], slot="c4")

    hst = hb[:, :, 0:2, :].rearrange("p a b c -> p a (b c)")
    hw_outs = []
    for rs, dcol in ((HALO0, 0), (HALO0 + HALO1 + OWN + 0, HALO1 + OWN)):
        l0_pass(C, io, Tab, rs, 8, False, npair=npair)
        emit_norm_out(C, 1, hst, hv, hw_outs, 2, dcol)
    for b_ in range(2):
        S.readers.setdefault(("hb", b_), []).extend(S.readers.get(("stage", b_), []))
        S.last_w.setdefault(("hb", b_), []).extend(S.last_w.get(("stage", b_), []))
    l0_pass(C, io, Tab, HALO0 + HALO1, 16, True, npair=npair)
    emit_norm(C, lambda c, t0, n: (C.xT[:, c, t0:t0 + n], ("x", c, t0 // 512)), OWN, 1,
              lambda c, t0, n: (C.hT[:, c, t0:t0 + n], "hT"))

    blocks = [w_in[p * 10 + k_] for p in range(npair) for k_ in range(10)]
    issued = [0]
    slots = {}

    def wget(i):
        while issued[0] < min(len(blocks), i + 5):
            slots[issued[0]] = load_wblock(C, blocks[issued[0]])
            issued[0] += 1
        return slots[i]

    hbc = [0]
    l1_body(C, npair, wget, w_out, hv, hb, hbc, acc, M1, D)
    flush_outproj(C)
    outs = []
    emit_norm_out(C, 2, stage, yo.rearrange("(c p) t -> p c t", p=128), outs)
    S.final_wait("sp", outs)
    S.emit()
    return C


def l1_body(C, npair, wget, w_out, hv, hb, hbc, acc, M1, D):
    S = C.S
    for p in range(npair):
        _l1_pair(C, p, wget, w_out, hv, hb, hbc, acc, M1, D)


def _l1_pair(C, p, wget, w_out, hv, hb, hbc, acc, M1, D):
    S = C.S
    S.dma("pool", lambda e: e.dma_start(out=C.Wo[:, p % 2, :], in_=w_out[p]), writes=[("Wo", p % 2)], slot=("Wo", p % 2))
    for g, d in enumerate(DILS):
        for hh in range(2):
            slope = 2.0 ** (-8.0 * (2 * p + hh + 1) / 16.0)
            S.op("act", lambda e, g=g, hh=hh, sc=-slope * d: e.activation(
                out=M1[:, g, :].rearrange("p (kl hh q) -> p kl hh q", kl=2, hh=2)[:, :, hh, :],
                in_=D[:, :].rearrange("p (kl q) -> p kl q", kl=2), func=AF.Exp, scale=sc),
                 reads=["D"], writes=["M1"])
    wg = wget(p * 10)
    for tb in range(4):
        proj_block(C, wg, lambda c, tb=tb: (C.hT[:, c, tb * 512:(tb + 1) * 512], "hT"), 512,
                   evac_silu(C, C.GT[:, tb * 512:(tb + 1) * 512], "GT"))
    for g, d in enumerate(DILS):
        _l1_group(C, p, g, d, wget, hv, hb, hbc, acc, M1)
    flush_outproj(C)
    uTp = C.uT[:, :].rearrange("p (j r) -> p r j", r=16)
    GTp = C.GT[:, :].rearrange("p (j r) -> p r j", r=16)
    for ch in range(4):
        ts_ = slice(ch * 512, (ch + 1) * 512)
        keys = [("acc", k) for k in range(ch * 4, ch * 4 + 4)]
        S.op("act", lambda e, ts_=ts_: e.activation(out=acc[:, 1, ts_], in_=acc[:, 1, ts_], func=AF.Ln), reads=keys, writes=keys)
        S.op("act", lambda e, ts_=ts_: e.activation(out=acc[:, 1, ts_], in_=acc[:, 1, ts_], func=AF.Exp, scale=-1.0), reads=keys, writes=keys)
        S.op("dve", lambda e, ts_=ts_: e.tensor_tensor(out=acc[:, 0, ts_], in0=acc[:, 0, ts_], in1=acc[:, 1, ts_], op=ALU.mult),
             reads=keys, writes=keys)
        S.op("dve", lambda e, ts_=ts_, ch=ch: e.tensor_tensor(out=uTp[:, 4 * ch:4 * ch + 4, :],
                                                            in0=acc[:, 0, ts_].rearrange("p (r j) -> p r j", r=4),
                                                            in1=GTp[:, 4 * ch:4 * ch + 4, :], op=ALU.mult),
             reads=keys + ["GT"], writes=[("uT", tb) for tb in range(4)])
    for tb in range(4):
        pend_outproj(C, p, tb)


def _l1_group(C, p, g, d, wget, hv, hb, hbc, acc, M1):
    S = C.S
    halo = 64 * d
    base = HALO1 - halo
    ncols = OWN + 2 * halo
    L = ncols // d
    Lq = OWN // d
    wq, wk, wv = wget(p * 10 + 1 + g * 3), wget(p * 10 + 2 + g * 3), wget(p * 10 + 3 + g * 3)
    hn = min(halo, 256)
    hblks = [(True, w0, hn) for w0 in range(base, HALO1, hn)] + \
            [(True, w0, hn) for w0 in range(HALO1 + OWN, HALO1 + OWN + halo, hn)]
    oblks = [(False, HALO1 + tb * 512, 512) for tb in range(4)]
    blks = []
    per = -(-len(hblks) // 4)
    for i in range(4):
        blks.extend(hblks[i * per:(i + 1) * per])
        blks.append(oblks[i])
    for (is_h, w0, n) in blks:
        col = w0 - base
        if is_h:
            hs_ = hbc[0] % 2
            hbc[0] += 1
            S.dma("sp", lambda e, hs_=hs_, w0=w0, n=n: e.dma_start(out=hb[:, hs_, :, 0:n], in_=hv[:, :, w0:w0 + n]),
                  reads=["hwin"], writes=[("hb", hs_)], slot=("hb", hs_))
            h_of = lambda c, hs_=hs_, n=n: (hb[:, hs_, c, 0:n], ("hb", hs_))
        else:
            t0 = w0 - HALO1
            h_of = lambda c, t0=t0, n=n: (C.hT[:, c, t0:t0 + n], "hT")
        if d == 1:
            kdst, vdst = C.KT[:, col:col + n], C.VT[:, col:col + n]
        else:
            kdst = C.KT[:, 0:ncols].rearrange("p (r i) -> p r i", r=d)[:, :, col // d:(col + n) // d]
            vdst = C.VT[:, 0:ncols].rearrange("p (r i) -> p r i", r=d)[:, :, col // d:(col + n) // d]
        proj_block(C, wk, h_of, n, evac_copy_act(C, kdst, "KT", d))
        if is_h:
            side = 0 if w0 < HALO1 else 1
            proj_block(C, wv, h_of, n, evac_scaled_dve(C, vdst, "VT", C.vcol[:, 2 + side:3 + side], d))
        else:
            proj_block(C, wv, h_of, n, evac_copy_dve(C, vdst, "VT", d))
    nq = OWN // (128 * d)
    nkt = nq + 1
    tiles = []
    for r in range(d):
        for kt in range(nkt):
            c0 = r * L + 128 * kt
            tiles.append((r * nkt + kt, C.VT[:, c0:c0 + 128], None))
    emit_vtrans(C, tiles)
    for tb in range(4):
        proj_block(C, wq, lambda c, tb=tb: (C.hT[:, c, tb * 512:(tb + 1) * 512], "hT"), 512,
                   evac_q(C, tb * 512, 512, d))
    units = [(r, qt) for r in range(d) for qt in range(nq)]

    def qk(u):
        r, qt = units[u]
        sb, eb = u % 4, u % 4
        qc = r * Lq + 128 * qt
        for kl in range(2):
            c0 = r * L + 128 * (qt + kl)
            S.op("pe", lambda e, kl=kl, c0=c0: e.matmul(
                C.Sp[:, sb, kl * 256:(kl + 1) * 256].rearrange("p (a c) -> p a c", a=2),
                lhsT=C.KT[:, c0:c0 + 128], rhs=C.QT[:, :, qc:qc + 128], start=True, stop=True),
                reads=["KT", "QT"], writes=[("S", sb)])
        S.op("act", lambda e: e.activation(out=C.E[:, eb, 0:512], in_=C.Sp[:, sb, 0:512], func=AF.Exp, scale=0.125),
             reads=[("S", sb)], writes=[("E", eb)])
        S.op(MASK_ENG if u % 2 == 1 else "dve", lambda e: e.tensor_tensor(out=C.P[:, eb, 0:512], in0=C.E[:, eb, 0:512], in1=M1[:, g, :], op=ALU.mult),
             reads=[("E", eb), "M1"], writes=[("P", eb)])

    def pv(u):
        r, qt = units[u]
        sb = u % 4
        ob = u % 2
        q0 = 128 * qt * d + r
        for kl in range(2):
            kt = qt + kl
            ti = r * nkt + kt
            vsel = 1 if kt == 0 else (2 if kt == nkt - 1 else 0)
            S.op("pe", lambda e, kl=kl, ti=ti: e.matmul(
                C.O[:, ob, 0:256], lhsT=C.V[:, ti, :], rhs=C.P[:, sb, kl * 256:(kl + 1) * 256],
                start=(kl == 0), stop=False), reads=["V", ("P", sb)], writes=[("O", ob)])
            S.op("pe", lambda e, kl=kl, vsel=vsel: e.matmul(
                C.O[:, ob, 256:512], lhsT=C.vm[:, vsel, :], rhs=C.P[:, sb, kl * 256:(kl + 1) * 256],
                start=False, stop=(kl == 1)), reads=["vm", ("P", sb)], writes=[("O", ob)])
        A4 = acc.rearrange("p a (r j) -> p a r j", r=16)
        if d == 16:
            rows = [r]
        elif d == 4:
            rows = [r + 4 * b for b in range(4)]
        else:
            rows = list(range(16))
        keys = [("acc", k) for k in rows]
        for hh in range(2):
            hs = slice(64 * hh, 64 * hh + 64)
            srcv = C.O[hs, ob, :].rearrange("p (a b c) -> p a b c", a=2, b=2)[:, :, hh, :]
            if d == 16:
                dst, src = A4[hs, :, r, :], srcv
            elif d == 4:
                dst = A4[hs, :, r:r + 13:4, 32 * qt:32 * qt + 32]
                src = srcv.rearrange("p a (x b) -> p a b x", b=4)
            else:
                dst = A4[hs, :, :, 8 * qt:8 * qt + 8]
                src = srcv.rearrange("p a (x r) -> p a r x", r=16)
            if g == 0:
                S.op("act", lambda e, dst=dst, src=src: e.activation(out=dst, in_=src, func=AF.Copy), reads=[("O", ob)], writes=keys)
            else:
                S.op("dve", lambda e, dst=dst, src=src: e.tensor_tensor(out=dst, in0=src, in1=dst, op=ALU.add),
                     reads=[("O", ob)] + keys, writes=keys)

    for u0 in range(min(3, len(units))):
        qk(u0)
    for u in range(len(units)):
        if u + 3 < len(units):
            qk(u + 3)
        pv(u)


def wblock(w, col0):
    return np.ascontiguousarray(w[:, col0:col0 + 128].reshape(NCH, 128, 128).transpose(1, 0, 2).reshape(128, 1024))


def gam_layout(*gs):
    return np.ascontiguousarray(np.concatenate([g.reshape(NCH, 128).T for g in gs], axis=1)).astype(np.float32)


def make_cst(j):
    cst = np.zeros((128, CSTW), np.float32)
    cst[:, 0:128] = np.eye(128, dtype=np.float32)
    vl = np.ones(128, np.float32)
    vr = np.ones(128, np.float32)
    if j == 0:
        vl[:64] = 0.0
    if j == 3:
        vr[64:] = 0.0
    cst[:, 128:256] = 1.0
    cst[:, 256:384] = vl[:, None]
    cst[:, 384:512] = vr[:, None]
    cst[:, 512:576] = 1.0
    cst[:, 704:768] = 1.0
    cst[:, 768] = vl
    cst[:, 769] = vr
    cst[:, 770] = 0.0 if j == 0 else 1.0
    cst[:, 771] = 0.0 if j == 3 else 1.0
    return cst


def make_tab(rpb, j):
    kp = np.arange(128)
    kr2, kc = kp // 64, kp % 64
    q = np.arange(128)
    qr2, qc = q // 64, q % 64
    cstart = np.clip(qc - 8, 0, 64 - 16)
    colv = (kc[:, None] >= cstart[None, :]) & (kc[:, None] < cstart[None, :] + 16)
    coff = np.clip(kc[:, None] - qc[None, :] + 15, 0, 30)
    out = np.full((16, NTABBLK, 128, 128), NEG, np.float32)
    for key, (toff, offs) in TAB_OFFS.items():
        lt = 5 if key == "G" else key
        m = 16 * j + lt
        r = 2 * m + qr2
        rs = np.clip(r - 4, 0, 128 - 8)
        for k, o in enumerate(offs):
            krow = 2 * (m + o) + kr2
            rowv = (krow[:, None] >= rs[None, :]) & (krow[:, None] < rs[None, :] + 8) & (krow[:, None] >= 0) & (krow[:, None] < 128)
            valid = rowv & colv
            roff = np.clip(krow[:, None] - r[None, :] + 7, 0, 14)
            vals = rpb[:, roff, coff]
            out[:, toff + k] = np.where(valid[None], vals, np.float32(NEG))
    out = out.reshape(NPAIR, 2, NTABBLK, 128, 128).transpose(0, 3, 2, 1, 4).reshape(NPAIR, 128, 2 * NTABBLK * 128)
    return np.ascontiguousarray(out)


def window_T(xb, t0, halo):
    out = np.zeros((xb.shape[1], OWN + 2 * halo), xb.dtype)
    lo, hi = t0 - halo, t0 + OWN + halo
    a, b = max(lo, 0), min(hi, SEQ)
    out[:, a - lo:b - lo] = xb[a:b].T
    return out


_CACHE = {}


def get_prog(name, builder):
    if name not in _CACHE:
        nc = bass.Bass("TRN2", target_bir_lowering=False)
        es = ExitStack()
        builder(nc, es)
        _CACHE[name] = (nc, es)
    return _CACHE[name][0]


def run_l0(x, norm_0, w_in_0, rpb_0, w_out_0, norm_1, norm_f, trace=False):
    nc = get_prog("l0", build_l0)
    w_in_b = np.stack([wblock(w_in_0, k * 1024 + p * 128) for p in range(NPAIR) for k in range(4)])
    w_out_b = np.ascontiguousarray(w_out_0.reshape(NPAIR, 128, 1024))
    gam = gam_layout(norm_0, norm_1, norm_f)
    tabs = [make_tab(rpb_0, j) for j in range(4)]
    csts = [make_cst(j) for j in range(4)]
    in_maps = []
    for c in range(NCORE):
        b, j = divmod(c, 4)
        in_maps.append({"xw": window_T(x[b], j * OWN, HALO0), "gamd": gam, "w_in0": w_in_b, "w_out0": w_out_b,
                        "tab0": tabs[j], "cst": csts[j]})
    res = run_bass_kernel_spmd(nc, in_maps, core_ids=list(range(NCORE)), trace=trace)
    return res


def make_dtab():
    kp = np.arange(128)[:, None]
    j = np.arange(128)[None, :]
    out = np.empty((128, 256), np.float32)
    for kl, sh in enumerate((-64, 64)):
        delta = np.abs(kp + sh - j)
        out[:, kl * 128:(kl + 1) * 128] = np.where(delta <= 64, delta, 1.0e6)
    return out


def run_l1(x1T_list, h1T_list, w_in_1, w_out_1, norm_0, norm_1, norm_f, trace=False):
    nc = get_prog("l1", build_l1)
    cols = []
    for p in range(NPAIR):
        cols.append(9216 + p * 128)
        for g in range(3):
            for k in range(3):
                cols.append(g * 3072 + k * 1024 + p * 128)
    w_in_b = np.stack([wblock(w_in_1, c0) for c0 in cols])
    w_out_b = np.ascontiguousarray(w_out_1.reshape(NPAIR, 128, 1024))
    gam = gam_layout(norm_0, norm_1, norm_f)
    dt = make_dtab()
    csts = [make_cst(j) for j in range(4)]
    in_maps = []
    for c in range(NCORE):
        b, j = divmod(c, 4)
        hw = np.zeros((D_MODEL, W1), h1T_list[c].dtype)
        hw[:, HALO1:HALO1 + OWN] = h1T_list[c]
        if j > 0:
            hw[:, 0:HALO1] = h1T_list[c - 1][:, OWN - HALO1:]
        if j < 3:
            hw[:, HALO1 + OWN:] = h1T_list[c + 1][:, 0:HALO1]
        in_maps.append({"x1w": x1T_list[c], "h1w": hw, "gamd": gam, "w_in1": w_in_b, "w_out1": w_out_b,
                        "cst": csts[j], "dtab": dt})
    return run_bass_kernel_spmd(nc, in_maps, core_ids=list(range(NCORE)), trace=trace)


def kernel(x, norm_0, w_in_0, rpb_0, w_out_0, norm_1, w_in_1, w_out_1, norm_f):
    f = lambda a: np.ascontiguousarray(np.asarray(a, dtype=np.float32))
    x, norm_0, w_in_0, rpb_0, w_out_0, norm_1, w_in_1, w_out_1, norm_f = map(
        f, (x, norm_0, w_in_0, rpb_0, w_out_0, norm_1, w_in_1, w_out_1, norm_f))
    nc, in_maps = run_fused(x, norm_0, w_in_0, rpb_0, w_out_0, norm_1, w_in_1, w_out_1, norm_f)
    res = run_bass_kernel_spmd(nc, in_maps, core_ids=list(range(NCORE)))
    out = np.empty((2, SEQ, D_MODEL), np.float32)
    for c in range(NCORE):
        b, j = divmod(c, 4)
        out[b, j * OWN:(j + 1) * OWN, :] = np.asarray(res.results[c]["yT"]).T
    return out


def make_hidx(j):
    l = j - 1 if j > 0 else j
    r = j + 1 if j < 3 else j
    p = np.arange(128, dtype=np.int64)
    out = np.zeros((128, NHIDX), np.uint32)
    for side, nbr in enumerate((l, r)):
        for c in range(NCH):
            for blk in range(4):
                tb = 4 + blk if side == 0 else blk
                out[:, side * 32 + c * 4 + blk] = (nbr * D_MODEL + c * 128 + p) * 8 + tb
    return out


def l1_cols():
    cols = []
    for p in range(NPAIR):
        cols.append(9216 + p * 128)
        for g in range(3):
            for k in range(3):
                cols.append(g * 3072 + k * 1024 + p * 128)
    return cols


def run_fused(x, norm_0, w_in_0, rpb_0, w_out_0, norm_1, w_in_1, w_out_1, norm_f, trace=False):
    nc = get_prog("fused", build_fused)
    w_in0_b = np.stack([wblock(w_in_0, k * 1024 + p * 128) for p in range(NPAIR) for k in range(4)])
    w_out0_b = np.ascontiguousarray(w_out_0.reshape(NPAIR, 128, 1024))
    w_in1_b = np.stack([wblock(w_in_1, c0) for c0 in l1_cols()])
    w_out1_b = np.ascontiguousarray(w_out_1.reshape(NPAIR, 128, 1024))
    gam = gam_layout(norm_0, norm_1, norm_f)
    dt = make_dtab()
    tabs = [make_tab(rpb_0, j) for j in range(4)]
    csts = [make_cst(j) for j in range(4)]
    hidx = [make_hidx(j) for j in range(4)]
    in_maps = []
    for c in range(NCORE):
        b, j = divmod(c, 4)
        in_maps.append({"xw": window_T(x[b], j * OWN, HALO0 + HALO1), "gamd": gam, "w_in0": w_in0_b, "w_out0": w_out0_b,
                        "tab0": tabs[j], "cst": csts[j], "w_in1": w_in1_b, "w_out1": w_out1_b, "dtab": dt})
    return nc, in_maps
```
